# Optimizing a Trainium2 kernel written in Bass

```python
import functools
import jax, jax.numpy as jnp
from jax import lax
import numpy as np

D_MODEL = 1024
BATCH = 8
SEQ = 4096
DEPTH = 1
DEC_BATCH = 128
DEC_SEQ = 4
PAST_LEN = 16384
PAGE_SIZE = 128

N_HEADS = 8
Q_LORA = 256
KV_LORA = 256
NOPE_DIM = 64
ROPE_DIM = 32
V_DIM = 64
ATTN_WIDTH = N_HEADS * V_DIM
ROPE_THETA = 10000.0
SM_SCALE = (NOPE_DIM + ROPE_DIM) ** -0.5
Q_BLOCK = 128
CHUNK = 128
GV = 512
N_GROUPS = 8
GROUP_DIM = GV // N_GROUPS
D_FF = 2816
N_SUB = 3
EPS = 1e-6
IN_SPLITS = (Q_LORA, KV_LORA, ROPE_DIM, GV, GV, D_MODEL, D_MODEL)
N_IN = Q_LORA + KV_LORA + ROPE_DIM + 2 * GV + 2 * D_MODEL

kernel_name = "mla_gmlp_macaron_adaln_step"


def rms_norm(x, g):
    xf = x.astype(jnp.float32)
    y = xf * lax.rsqrt(jnp.mean(xf * xf, axis=-1, keepdims=True) + EPS)
    return (y * g.astype(jnp.float32)).astype(x.dtype)


def layer_norm(x, g, b):
    xf = x.astype(jnp.float32)
    mu = jnp.mean(xf, axis=-1, keepdims=True)
    xc = xf - mu
    var = jnp.mean(xc * xc, axis=-1, keepdims=True)
    return (xc * lax.rsqrt(var + EPS) * g.astype(jnp.float32) + b.astype(jnp.float32)).astype(x.dtype)


def swiglu(h, w_gate, w_up, w_down):
    return (jax.nn.silu(h @ w_gate) * (h @ w_up)) @ w_down


def apply_rope(x, pos):
    half = ROPE_DIM // 2
    freqs = ROPE_THETA ** (-2.0 * jnp.arange(half, dtype=jnp.float32) / ROPE_DIM)
    ang = pos.astype(jnp.float32)[:, None] * freqs[None, :]
    shape = (ang.shape[0],) + (1,) * (x.ndim - 3) + (half,)
    cos = jnp.cos(ang).reshape(shape)
    sin = jnp.sin(ang).reshape(shape)
    xf = x.astype(jnp.float32)
    x1, x2 = xf[..., :half], xf[..., half:]
    return jnp.concatenate([x1 * cos - x2 * sin, x1 * sin + x2 * cos], axis=-1).astype(x.dtype)


def split_in(proj):
    idx, off = [], 0
    for s in IN_SPLITS[:-1]:
        off += s
        idx.append(off)
    return jnp.split(proj, idx, axis=-1)


def mla_scores(q_lat, q_pe, kv, k_pe):
    s = jnp.einsum('bqhc,bkc->bhqk', q_lat, kv) + jnp.einsum('bqhr,bkr->bhqk', q_pe, k_pe)
    return s.astype(jnp.float32) * SM_SCALE


def mla_prompt_attend(q_lat, q_pe, kv, k_pe):
    B, S = q_lat.shape[:2]
    nb = S // Q_BLOCK
    ql = q_lat.reshape(B, nb, Q_BLOCK, N_HEADS, KV_LORA).transpose(1, 0, 2, 3, 4)
    qp = q_pe.reshape(B, nb, Q_BLOCK, N_HEADS, ROPE_DIM).transpose(1, 0, 2, 3, 4)
    k_pos = jnp.arange(S)

    def block(args):
        i, ql_b, qp_b = args
        s = mla_scores(ql_b, qp_b, kv, k_pe)
        q_pos = i * Q_BLOCK + jnp.arange(Q_BLOCK)
        s = jnp.where(k_pos[None, :] <= q_pos[:, None], s, -jnp.inf)
        p = jax.nn.softmax(s, axis=-1).astype(kv.dtype)
        return jnp.einsum('bhqk,bkc->bqhc', p, kv)

    out = lax.map(block, (jnp.arange(nb), ql, qp))
    return out.transpose(1, 0, 2, 3, 4).reshape(B, S, N_HEADS, KV_LORA)


def mla_sample_attend(q_lat, q_pe, kv_new, pe_new, cache_kv, cache_pe, page_table, layer):
    DB, T = q_lat.shape[:2]
    past_kv = cache_kv[layer, page_table].reshape(DB, -1, KV_LORA)
    past_pe = cache_pe[layer, page_table].reshape(DB, -1, ROPE_DIM)
    n_past = past_kv.shape[1]
    s_past = mla_scores(q_lat, q_pe, past_kv, past_pe)
    s_new = mla_scores(q_lat, q_pe, kv_new, pe_new)
    s_new = jnp.where(jnp.tril(jnp.ones((T, T), dtype=bool)), s_new, -jnp.inf)
    p = jax.nn.softmax(jnp.concatenate([s_past, s_new], axis=-1), axis=-1).astype(kv_new.dtype)
    return (jnp.einsum('bhqk,bkc->bqhc', p[..., :n_past], past_kv)
            + jnp.einsum('bhqk,bkc->bqhc', p[..., n_past:], kv_new))


def spatial_gating(u, v_n, w_s, b_s, chunk_len):
    B, S, _ = u.shape
    tri = jnp.tril(jnp.ones((chunk_len, chunk_len), dtype=bool))
    w = jnp.where(tri, w_s[:, :chunk_len, :chunk_len], 0)
    vc = v_n.reshape(B, S // chunk_len, chunk_len, N_GROUPS, GROUP_DIM)
    mixed = jnp.einsum('gts,bnsgd->bntgd', w, vc) + b_s[:, :chunk_len].T[:, :, None]
    return u * mixed.reshape(B, S, GV)


def layer_forward(x, c, pos, attend, chunk_len, p):
    B, S, _ = x.shape
    mod = (jax.nn.silu(c) @ p['w_ada'] + p['b_ada']).reshape(B, N_SUB, 3, D_MODEL)
    shift, scale, gate = mod[:, :, 0, None, :], mod[:, :, 1, None, :], mod[:, :, 2, None, :]

    def modulated(h, g, i):
        return rms_norm(h, g) * (1 + scale[:, i]) + shift[:, i]

    x = x + 0.5 * gate[:, 0] * swiglu(modulated(x, p['g_ffn1'], 0), p['w1_gate'], p['w1_up'], p['w1_down'])

    n = modulated(x, p['g_mix'], 1)
    c_q, c_kv, k_pe, u, v, g_a, g_b = split_in(n @ p['w_in'])
    q = (rms_norm(c_q, p['g_q']) @ p['w_uq']).reshape(B, S, N_HEADS, NOPE_DIM + ROPE_DIM)
    q_pe = apply_rope(q[..., NOPE_DIM:], pos)
    q_lat = jnp.einsum('bshn,chn->bshc', q[..., :NOPE_DIM], p['w_uk'])
    kv = rms_norm(c_kv, p['g_kv'])
    k_pe = apply_rope(k_pe, pos)
    o_lat = attend(q_lat, q_pe, kv, k_pe)
    o_a = jnp.einsum('bshc,chv->bshv', o_lat, p['w_uv']).reshape(B, S, ATTN_WIDTH)
    u = jax.nn.gelu(u)
    v_n = layer_norm(jax.nn.gelu(v), p['ln_v_g'], p['ln_v_b'])
    o_b = spatial_gating(u, v_n, p['w_s'], p['b_s'], chunk_len)
    merged = jax.nn.sigmoid(g_a) * (o_a @ p['w_pa']) + jax.nn.sigmoid(g_b) * (o_b @ p['w_pb'])
    x = x + gate[:, 1] * (merged @ p['w_o'])

    x = x + 0.5 * gate[:, 2] * swiglu(modulated(x, p['g_ffn2'], 2), p['w2_gate'], p['w2_up'], p['w2_down'])
    return x, kv, k_pe, v_n


def setup_inputs(seed: int = 0) -> dict:
    key = jax.random.key(seed)
    ks = iter(jax.random.split(key, 48))

    def nrm(shape, scale):
        return jax.random.normal(next(ks), shape, jnp.float32) * scale

    def gain(shape):
        return 1.0 + 0.02 * jax.random.normal(next(ks), shape, jnp.float32)

    L = DEPTH
    n_pages = PAST_LEN // PAGE_SIZE
    n_used = DEC_BATCH * n_pages
    n_pool = n_used + max(1, n_used // 4)
    x_prompt = nrm((BATCH, SEQ, D_MODEL), 1.0)
    x_sample = nrm((DEC_BATCH, DEC_SEQ, D_MODEL), 1.0)
    cache_kv = nrm((L, n_pool, PAGE_SIZE, KV_LORA), 1.0)
    cache_pe = nrm((L, n_pool, PAGE_SIZE, ROPE_DIM), 1.0)
    page_table = jax.random.permutation(next(ks), n_pool)[:n_used].reshape(DEC_BATCH, n_pages).astype(jnp.int32)
    return {
        'x_prompt': x_prompt,
        'x_sample': x_sample,
        'cache_kv': cache_kv,
        'cache_pe': cache_pe,
        'page_table': page_table,
        'c_prompt': nrm((BATCH, D_MODEL), 1.0),
        'c_sample': nrm((DEC_BATCH, D_MODEL), 1.0),
        'w_ada': nrm((L, D_MODEL, N_SUB * 3 * D_MODEL), 0.5 * D_MODEL ** -0.5),
        'b_ada': nrm((L, N_SUB * 3 * D_MODEL), 0.02),
        'g_ffn1': gain((L, D_MODEL)),
        'w1_gate': nrm((L, D_MODEL, D_FF), D_MODEL ** -0.5),
        'w1_up': nrm((L, D_MODEL, D_FF), D_MODEL ** -0.5),
        'w1_down': nrm((L, D_FF, D_MODEL), D_FF ** -0.5),
        'g_mix': gain((L, D_MODEL)),
        'w_in': nrm((L, D_MODEL, N_IN), D_MODEL ** -0.5),
        'g_q': gain((L, Q_LORA)),
        'w_uq': nrm((L, Q_LORA, N_HEADS * (NOPE_DIM + ROPE_DIM)), Q_LORA ** -0.5),
        'g_kv': gain((L, KV_LORA)),
        'w_uk': nrm((L, KV_LORA, N_HEADS, NOPE_DIM), KV_LORA ** -0.5),
        'w_uv': nrm((L, KV_LORA, N_HEADS, V_DIM), KV_LORA ** -0.5),
        'ln_v_g': gain((L, GV)),
        'ln_v_b': nrm((L, GV), 0.02),
        'w_s': nrm((L, N_GROUPS, CHUNK, CHUNK), CHUNK ** -0.5),
        'b_s': gain((L, N_GROUPS, CHUNK)),
        'w_pa': nrm((L, ATTN_WIDTH, D_MODEL), ATTN_WIDTH ** -0.5),
        'w_pb': nrm((L, GV, D_MODEL), GV ** -0.5),
        'w_o': nrm((L, D_MODEL, D_MODEL), D_MODEL ** -0.5),
        'g_ffn2': gain((L, D_MODEL)),
        'w2_gate': nrm((L, D_MODEL, D_FF), D_MODEL ** -0.5),
        'w2_up': nrm((L, D_MODEL, D_FF), D_MODEL ** -0.5),
        'w2_down': nrm((L, D_FF, D_MODEL), D_FF ** -0.5),
        'g_final': gain((D_MODEL,)),
    }


def reference(x_prompt, x_sample, cache_kv, cache_pe, page_table, c_prompt, c_sample,
              w_ada, b_ada, g_ffn1, w1_gate, w1_up, w1_down, g_mix, w_in, g_q, w_uq,
              g_kv, w_uk, w_uv, ln_v_g, ln_v_b, w_s, b_s, w_pa, w_pb, w_o,
              g_ffn2, w2_gate, w2_up, w2_down, g_final):
    pos_p = jnp.arange(x_prompt.shape[1])
    pos_s = PAST_LEN + jnp.arange(x_sample.shape[1])
    h_p, h_s = x_prompt, x_sample
    kv_p, pe_p, kv_s, pe_s, gv_s = [], [], [], [], []
    for l in range(DEPTH):
        p = {
            'w_ada': w_ada[l], 'b_ada': b_ada[l],
            'g_ffn1': g_ffn1[l], 'w1_gate': w1_gate[l], 'w1_up': w1_up[l], 'w1_down': w1_down[l],
            'g_mix': g_mix[l], 'w_in': w_in[l], 'g_q': g_q[l], 'w_uq': w_uq[l],
            'g_kv': g_kv[l], 'w_uk': w_uk[l], 'w_uv': w_uv[l],
            'ln_v_g': ln_v_g[l], 'ln_v_b': ln_v_b[l], 'w_s': w_s[l], 'b_s': b_s[l],
            'w_pa': w_pa[l], 'w_pb': w_pb[l], 'w_o': w_o[l],
            'g_ffn2': g_ffn2[l], 'w2_gate': w2_gate[l], 'w2_up': w2_up[l], 'w2_down': w2_down[l],
        }
        h_p, kv, pe, _ = layer_forward(h_p, c_prompt, pos_p, mla_prompt_attend, CHUNK, p)
        kv_p.append(kv)
        pe_p.append(pe)
        attend_s = functools.partial(mla_sample_attend, cache_kv=cache_kv, cache_pe=cache_pe,
                                     page_table=page_table, layer=l)
        h_s, kv, pe, gv = layer_forward(h_s, c_sample, pos_s, attend_s, x_sample.shape[1], p)
        kv_s.append(kv)
        pe_s.append(pe)
        gv_s.append(gv)
    y_prompt = rms_norm(h_p, g_final)
    y_sample = rms_norm(h_s, g_final)
    return (y_prompt, y_sample, jnp.stack(kv_p), jnp.stack(pe_p), jnp.stack(kv_s), jnp.stack(pe_s), jnp.stack(gv_s))
```

```python
import numpy as np
from contextlib import ExitStack
import concourse.bass as bass
import concourse.mybir as mybir
from concourse.bass_utils import run_bass_kernel_spmd

F32, BF16, I32 = mybir.dt.float32, mybir.dt.bfloat16, mybir.dt.int32
AF = mybir.ActivationFunctionType
ALU = mybir.AluOpType

D = 1024
DFF = 2816
NFC = 22
NIN = 3616
SEQ = 4096
NH = 8
EPS = 1e-6
SM_SCALE = 96 ** -0.5
PAST = 16384
ENGS = ['pe', 'act', 'dve', 'pool', 'sp']


class Tl:
    def __init__(s, name, h):
        s.name, s.h = name, h
        s.w = None
        s.r = {}
        s.pre = {}
        s.dsem = None
        s.dcnt = 0

    def __getitem__(s, k):
        return V(s, s.h[k])


class V:
    def __init__(s, t, ap):
        s.t, s.ap = t, ap

    def __getitem__(s, k):
        return V(s.t, s.ap[k])

    def f(s, fn):
        return V(s.t, fn(s.ap))

    def re(self, pat, **kw):
        return V(self.t, self.ap.rearrange(pat, **kw))

    def bc(s, shape):
        return V(s.t, s.ap.broadcast_to(list(shape)))

    def us(s, ax):
        return V(s.t, s.ap.unsqueeze(ax))


class Prog:
    def __init__(s, nc, es):
        s.nc, s.es = nc, es
        s.sem = {e: es.enter_context(nc.semaphore('s_' + e)) for e in ENGS}
        s.cnt = {e: 0 for e in ENGS}
        s.seen = {e: {} for e in ENGS}
        s.ops = {e: [] for e in ENGS}
        s.dtiles = []

    def _waits(s, eng, reads, writes, nowait_w=False):
        need = {}

        def add(ev):
            sm, val = ev
            if sm is s.sem[eng] and eng in ('pe', 'sp'):
                return
            k = id(sm)
            if k not in need or need[k][1] < val:
                need[k] = (sm, val)
        for t in reads:
            if t.w is not None:
                add(t.w)
            for ev in t.pre.values():
                add(ev)
        for t in writes:
            if t.w is not None and not nowait_w:
                add(t.w)
            for ev in t.r.values():
                add(ev)
            for ev in t.pre.values():
                add(ev)
        out = []
        for k, (sm, val) in need.items():
            if s.seen[eng].get(k, 0) < val:
                s.seen[eng][k] = val
                out.append((sm, val))
        return out

    @staticmethod
    def _addr(t, ev):
        k = id(ev[0])
        if k not in t.r or t.r[k][1] < ev[1]:
            t.r[k] = ev

    def op(s, eng, fn, reads, writes, sig=True):
        reads = list({id(t): t for t in reads}.values())
        writes = list({id(t): t for t in writes}.values())
        waits = s._waits(eng, reads, writes)
        if sig:
            s.cnt[eng] += 1
            ev = (s.sem[eng], s.cnt[eng])
        else:
            ev = (s.sem[eng], s.cnt[eng] + 1)
        s.ops[eng].append((waits, fn, s.sem[eng] if sig else None, 1))
        for t in writes:
            t.w = ev
            t.r = {}
            t.pre = {}
        for t in reads:
            s._addr(t, ev)

    def dma(s, q, out, in_, nowait_w=False, owner=None, fn=None, extra=(), slow=False):
        t = owner if owner is not None else out.t
        if t.dsem is None:
            t.dsem = s.es.enter_context(s.nc.semaphore('d%d' % len(s.dtiles)))
            s.dtiles.append(t)
        waits = s._waits(q, [in_.t] + list(extra), [out.t], nowait_w)
        t.dcnt += 16
        ev = (t.dsem, t.dcnt)
        if fn is None:
            oa, ia = out.ap, in_.ap
            if slow:
                fn = lambda e: e.dma_start(out=oa, in_=ia, allow_slow_non_contiguous=True)
            else:
                fn = lambda e: e.dma_start(out=oa, in_=ia)
        s.ops[q].append((waits, fn, t.dsem, 16))
        out.t.w = ev
        if not nowait_w:
            out.t.r = {}
            out.t.pre = {}
        s._addr(in_.t, ev)
        for x in extra:
            s._addr(x, ev)

    @staticmethod
    def inherit(dsts, srcs):
        evs = {}
        for t in srcs:
            for ev in ([t.w] if t.w is not None else []) + list(t.r.values()) + list(t.pre.values()):
                k = id(ev[0])
                if k not in evs or evs[k][1] < ev[1]:
                    evs[k] = ev
        for d in dsts:
            for k, ev in evs.items():
                if k not in d.pre or d.pre[k][1] < ev[1]:
                    d.pre[k] = ev

    def finish(s):
        waits = []
        for t in s.dtiles:
            waits.append((t.dsem, t.dcnt))
        for e in ENGS:
            if e != 'sp' and s.cnt[e] > 0:
                waits.append((s.sem[e], s.cnt[e]))
        s.ops['sp'].append((waits, None, None, 0))

    def emit(s):
        nc = s.nc

        def replay(name, e):
            for waits, fn, sem, inc in s.ops[name]:
                for sm, val in waits:
                    e.wait_ge(sm, val)
                if fn is None:
                    continue
                ins = fn(e)
                if sem is not None:
                    ins.then_inc(sem, inc)
        with nc.Block() as block:
            @block.tensor
            def _(e):
                replay('pe', e)

            @block.scalar
            def _(e):
                replay('act', e)

            @block.vector
            def _(e):
                replay('dve', e)

            @block.gpsimd
            def _(e):
                replay('pool', e)

            @block.sync
            def _(e):
                replay('sp', e)


def build(n_pool=20480, n_tiles=8, do_sample=True):
    nc = bass.Bass("TRN2", target_bir_lowering=False)
    es = ExitStack()
    P = Prog(nc, es)

    def din(name, shape, dt=F32):
        return Tl(name, nc.dram_tensor(name, list(shape), dt, kind="ExternalInput").ap())

    def dout(name, shape, dt=F32):
        return Tl(name, nc.dram_tensor(name, list(shape), dt, kind="ExternalOutput").ap())

    def dscr(name, shape, dt=BF16):
        return Tl(name, nc.dram_tensor(name, list(shape), dt).ap())

    def sb(name, shape, dt=F32):
        return Tl(name, es.enter_context(nc.sbuf_tensor(name, list(shape), dt)))

    import os
    DBG = os.environ.get('KDBG', '0') == '1'
    dseen = set()

    def dump(name, v):
        if not DBG or name in dseen:
            return
        dseen.add(name)
        o = dout('dbg_' + name, list(v.ap.shape), v.ap.dtype)
        P.dma('sp', o[:], v)

    def mm(out, lhsT, rhs, start=True, stop=True, sig=None, skip=False):
        oa, la, ra = out.ap, lhsT.ap, rhs.ap
        if skip:
            P.op('pe', lambda e: e.matmul(oa, lhsT=la, rhs=ra, start=start, stop=stop, skip_group_check=True),
                 [lhsT.t, rhs.t], [out.t], sig=(stop if sig is None else sig))
        else:
            P.op('pe', lambda e: e.matmul(oa, lhsT=la, rhs=ra, start=start, stop=stop),
                 [lhsT.t, rhs.t], [out.t], sig=(stop if sig is None else sig))

    def act(out, in_, func, scale=None, bias=None, accum=None, eng='act'):
        oa, ia = out.ap, in_.ap
        kw = {}
        rd = [in_.t]
        wr = [out.t]
        if scale is not None:
            if isinstance(scale, V):
                kw['scale'] = scale.ap
                rd.append(scale.t)
            else:
                kw['scale'] = scale
        if bias is not None:
            if isinstance(bias, V):
                kw['bias'] = bias.ap
                rd.append(bias.t)
            else:
                kw['bias'] = bias
        if accum is not None:
            kw['accum_out'] = accum.ap
            wr.append(accum.t)
        P.op('act', lambda e: e.activation(out=oa, in_=ia, func=func, **kw), rd, wr)

    def tt(eng, out, in0, in1, op):
        oa, a0, a1 = out.ap, in0.ap, in1.ap
        P.op(eng, lambda e: e.tensor_tensor(out=oa, in0=a0, in1=a1, op=op), [in0.t, in1.t], [out.t])

    def ts(eng, out, in0, s1, s2, op0, op1=None):
        oa, a0 = out.ap, in0.ap
        rd = [in0.t]
        if isinstance(s1, V):
            rd.append(s1.t)
            s1 = s1.ap
        if isinstance(s2, V):
            rd.append(s2.t)
            s2 = s2.ap
        if op1 is None:
            P.op(eng, lambda e: e.tensor_scalar(out=oa, in0=a0, scalar1=s1, scalar2=None, op0=op0), rd, [out.t])
        else:
            P.op(eng, lambda e: e.tensor_scalar(out=oa, in0=a0, scalar1=s1, scalar2=s2, op0=op0, op1=op1), rd, [out.t])

    def stt(eng, out, in0, sc, in1, op0, op1):
        oa, a0, a1 = out.ap, in0.ap, in1.ap
        rd = [in0.t, in1.t]
        if isinstance(sc, V):
            rd.append(sc.t)
            sc = sc.ap
        P.op(eng, lambda e: e.scalar_tensor_tensor(out=oa, in0=a0, scalar=sc, in1=a1, op0=op0, op1=op1), rd, [out.t])

    def cp(eng, out, in_):
        oa, ia = out.ap, in_.ap
        if eng == 'act':
            P.op('act', lambda e: e.activation(out=oa, in_=ia, func=AF.Identity), [in_.t], [out.t])
        else:
            P.op(eng, lambda e: e.tensor_copy(out=oa, in_=ia), [in_.t], [out.t])

    def recip(out, in_):
        oa, ia = out.ap, in_.ap
        P.op('dve', lambda e: e.reciprocal(out=oa, in_=ia), [in_.t], [out.t])

    def rsqrt(out, in_, scale):
        ts('dve', out, in_, scale, EPS, ALU.mult, ALU.add)
        act(out, out, AF.Sqrt)
        recip(out, out)

    def rsum(out, in_):
        oa, ia = out.ap, in_.ap
        P.op('dve', lambda e: e.reduce_sum(out=oa, in_=ia, axis=mybir.AxisListType.X), [in_.t], [out.t])

    def mset(eng, out, val):
        oa = out.ap
        P.op(eng, lambda e: e.memset(oa, val), [], [out.t])

    x_p = din('x_p', [SEQ, D])
    x_s = din('x_s', [64, D])
    cache_kv = din('cache_kv', [n_pool, 128, 256])
    cache_pe = din('cache_pe', [n_pool, 128, 32])
    ptab = din('ptab', [128, 16], I32)
    c_all = din('c_all', [17, D])
    w_ada = din('w_ada', [D, 9 * D])
    b_ada = din('b_ada', [72, 128])
    gall = din('gall', [28, 128])
    w1g = din('w1g', [D, DFF]); w1u = din('w1u', [D, DFF]); w1d = din('w1d', [DFF, D])
    w2g = din('w2g', [D, DFF]); w2u = din('w2u', [D, DFF]); w2d = din('w2d', [DFF, D])
    w_in = din('w_in', [D, NIN])
    w_uq = din('w_uq', [256, 768])
    w_uk = din('w_uk', [256, 512])
    w_uv = din('w_uv', [256, 512])
    lnv = din('lnv', [2, 512])
    w_s = din('w_s', [8, 128, 128])
    b_s = din('b_s', [8, 128])
    w_pa = din('w_pa', [512, D]); w_pb = din('w_pb', [512, D]); w_o = din('w_o', [D, D])
    g_kv1 = din('g_kv1', [256])
    g_final = din('g_final', [D])
    c_identf = din('c_identf', [128, 128])
    c_maskT = din('c_maskT', [128, 128])
    c_masknew = din('c_masknew', [4, 32])
    c_maskblk = din('c_maskblk', [64, 64])
    c_csp = din('c_csp', [SEQ, 32])
    c_css = din('c_css', [64, 32])

    y_p = dout('y_p', [SEQ, D]); y_s = dout('y_s', [64, D])
    kv_p = dout('kv_p', [SEQ, 256]); pe_p = dout('pe_p', [SEQ, 32])
    kv_s = dout('kv_s', [64, 256]); pe_s = dout('pe_s', [64, 32]); gv_s = dout('gv_s', [64, 512])

    s1g = dscr('s1g', [D, DFF]); s1u = dscr('s1u', [D, DFF]); s1d = dscr('s1d', [DFF, D])
    s2g = dscr('s2g', [D, DFF]); s2u = dscr('s2u', [D, DFF]); s2d = dscr('s2d', [DFF, D])
    s_in = dscr('s_in', [D, NIN])
    s_pa = dscr('s_pa', [512, D]); s_pb = dscr('s_pb', [512, D]); s_o = dscr('s_o', [D, D])

    PS = [Tl('ps%d' % i, es.enter_context(nc.psum_tensor('ps%d' % i, [128, 512], F32))) for i in range(8)]
    NRING = 4
    RSZ = 4096
    ring = [sb('ring%d' % i, [128, RSZ], BF16) for i in range(NRING)]
    ring_i = [0]

    def ring_next():
        t = ring[ring_i[0] % NRING]
        ring_i[0] += 1
        return t

    identf = sb('identf', [128, 128]); identb = sb('identb', [128, 128], BF16)
    maskTb = sb('maskTb', [128, 128], BF16); maskTf = sb('maskTf', [128, 128])
    masknewf = sb('masknewf', [4, 32]); masknewb = sb('masknewb', [4, 32], BF16)
    maskblk = sb('maskblk', [64, 64])
    onesb = sb('onesb', [128, 1], BF16)
    cst = sb('cst', [128, 32]); css = sb('css', [64, 32])
    xt = sb('xt', [128, 4, D])
    xs = sb('xs', [128, D], BF16)
    junk = xs
    nT = sb('nT', [128, 8, 512], BF16)
    actT = sb('actT', [128, NFC, 512], BF16)
    tmpf = sb('tmpf', [128, 512]); tmpg = sb('tmpg', [128, 512]); tmph = sb('tmph', [128, 512])
    st = sb('st', [128, 16])
    KT = sb('KT', [128, 2, SEQ], BF16); KTpe = sb('KTpe', [32, SEQ], BF16)
    Vc = sb('Vc', [128, 32, 256], BF16)
    modT = sb('modT', [128, 72, 17])
    badaT = sb('badaT', [128, 72]); bada72 = sb('bada72', [72, 128])
    gall_sb = sb('gall_sb', [28, 128]); gT = sb('gT', [128, 28])
    callsb = Tl('callsb', xt.h[0:17, 0, :]); scb = Tl('scb', xs.h[0:17, :]); scT = sb('scT', [128, 8, 17], BF16)
    GT = sb('GT', [128, 3, 8, 16]); GATE = sb('GATE', [128, 3, D])
    GKV = sb('GKV', [128, 256]); GF = sb('GF', [128, D]); LNG = sb('LNG', [128, 512]); LNB = sb('LNB', [128, 512])
    wuq = sb('wuq', [128, 2, 768], BF16); wuqf = Tl('wuqf', actT.h[:, 0:6, :].bitcast(F32).rearrange('p a c -> p (a c)').rearrange('p (k c) -> p k c', k=2))
    wukf = Tl('wukf', actT.h[:, 6:10, :].bitcast(F32).rearrange('p a c -> p (a c)').rearrange('p (k c) -> p k c', k=2)); wukb = Tl('wukb', actT.h[:, 20:22, :])
    wukT = sb('wukT', [128, 8, 256], BF16)
    wuvf = Tl('wuvf', actT.h[:, 10:14, :].bitcast(F32).rearrange('p a c -> p (a c)').rearrange('p (k c) -> p k c', k=2)); wuvpad = sb('wuvpad', [128, 8, 2, 128], BF16)
    wsf = Tl('wsf', actT.h[:, 14:18, :].bitcast(F32).rearrange('p a c -> p (a c)').rearrange('p (g c) -> p g c', g=8)); WS = sb('WS', [128, 8, 128], BF16)
    WSs = sb('WSs', [64, 8, 64], BF16); wssf = Tl('wssf', actT.h[0:64, 18:20, :].bitcast(F32).rearrange('p a c -> p (a c)').rearrange('p (g c) -> p g c', g=8))
    bsf = sb('bsf', [8, 128]); bsT = sb('bsT', [128, 8])
    cq_b = sb('cq_b', [128, 256], BF16); cqnT = sb('cqnT', [128, 2, 128], BF16)
    kvf = sb('kvf', [128, 256]); pef = sb('pef', [128, 32]); peb = sb('peb', [128, 32], BF16)
    kvb_s = sb('kvb_s', [64, 256], BF16); kvT_s = sb('kvT_s', [128, 2, 64], BF16); peT_s = sb('peT_s', [32, 64], BF16)
    qf = sb('qf', [128, 768]); qnb = sb('qnb', [128, 512], BF16); qnT = sb('qnT', [128, 4, 128], BF16)
    qpf = sb('qpf', [128, 8, 32]); qpb = sb('qpb', [128, 8, 32], BF16)
    QT = sb('QT', [128, 2, 1024], BF16); QTpe = sb('QTpe', [32, 1024], BF16)
    uf = sb('uf', [128, 512]); gvf = sb('gvf', [128, 512]); vnf = gvf; yoA = Tl('yoA', uf.h[:, :]); yoB = Tl('yoB', gvf.h[:, :]); vnb = sb('vnb', [128, 512], BF16)
    obb = sb('obb', [128, 512], BF16)
    obT = Tl('obT', actT.h[:, 0:4, :]); oaT = Tl('oaT', actT.h[:, 4:8, :])
    mergedT = Tl('mergedT', actT.h[:, 8:16, :])
    mixviews = [obT, oaT, mergedT]
    PTt = [sb('PT%d' % i, [128, 512], BF16) for i in range(2)]
    olat = sb('olat', [128, 8, 256], BF16)
    olatT = Tl('olatT', actT.h[:, 16:20, :].rearrange('p a c -> p (a c)').rearrange('p (k c) -> p k c', k=2))
    mixviews.append(olatT)
    rden = sb('rden', [128, 8])
    if do_sample:
        LK = [Tl('LK%d' % i, Vc.h[:, 8 * i:8 * i + 8, :]) for i in range(2)]
        LP = [Tl('LP%d' % i, Vc.h[:, 16 + i, :].rearrange('p (r c) -> p r c', c=32)) for i in range(2)]
        KTs = Tl('KTs', Vc.h[:, 18:22, :].rearrange('p a c -> p (a c)').rearrange('p (k t) -> p k t', k=2))
        KTpes = Tl('KTpes', Vc.h[0:32, 22:24, :].rearrange('p a c -> p (a c)'))
        salias = LK + LP + [KTs, KTpes]
        pt_sb = sb('pt_sb', [128, 16], I32); idx_sb = sb('idx_sb', [128, 16, 16], I32)
        Vn = sb('Vn', [4, 256], BF16); PTn = sb('PTn', [4, 32], BF16)
        olat_s = sb('olat_s', [32, 256], BF16)
        gvout = gvf

    P.dma('sp', identf[:], c_identf[:])
    cp('dve', identb[:], identf[:])
    P.dma('sp', maskTf[:], c_maskT[:])
    cp('dve', maskTb[:], maskTf[:])
    P.dma('sp', masknewf[:], c_masknew[:])
    cp('dve', masknewb[:], masknewf[:])
    P.dma('sp', maskblk[:], c_maskblk[:])
    mset('dve', onesb[:], 1.0)
    P.dma('sp', css[:], c_css[:])
    P.dma('sp', callsb[:], c_all[:])
    P.dma('sp', bada72[:], b_ada[:])
    P.dma('sp', gall_sb[:], gall[:])
    P.dma('sp', GKV[:], g_kv1[:].f(lambda a: a.partition_broadcast(128)))
    P.dma('sp', GF[:], g_final[:].f(lambda a: a.partition_broadcast(128)))
    P.dma('sp', LNG[:], lnv[0, :].f(lambda a: a.partition_broadcast(128)))
    P.dma('sp', LNB[:], lnv[1, :].f(lambda a: a.partition_broadcast(128)))
    P.dma('sp', wuqf[:], w_uq[:].re("(k p) c -> p k c", p=128))
    cp('dve', wuq[:], wuqf[:])
    P.dma('sp', wukf[:], w_uk[:].re("(k p) c -> p k c", p=128))
    cp('dve', wukb[:], wukf[:])
    P.dma('sp', wuvf[:], w_uv[:].re("(k p) c -> p k c", p=128))
    P.dma('sp', wsf[:], w_s[:].re("g t s -> t g s"))
    P.dma('sp', bsf[:], b_s[:])

    mm(PS[0][:, 0:72], bada72[0:72, :], identf[0:72, 0:72])
    cp('dve', badaT[:], PS[0][:, 0:72])
    mm(PS[1][:, 0:28], gall_sb[0:28, :], identf[0:28, 0:28])
    cp('dve', gT[:], PS[1][:, 0:28])
    mm(PS[0][:, 0:8], bsf[0:8, :], identf[0:8, 0:8])
    cp('dve', bsT[:], PS[0][:, 0:8])
    mset('dve', wukT[:], 0.0)
    for pr in range(4):
        for cc in range(2):
            mm(PS[1][:, cc * 128:(cc + 1) * 128], wukb[:, cc, pr * 128:(pr + 1) * 128], identb[:])
        for h2 in range(2):
            cp('dve', wukT[h2 * 64:(h2 + 1) * 64, 2 * pr + h2, :], PS[1][h2 * 64:(h2 + 1) * 64, 0:256])
    mset('dve', wuvpad[:], 0.0)
    for h in range(NH):
        cp('dve', wuvpad[:, h, :, (h % 2) * 64:(h % 2) * 64 + 64], wuvf[:, :, h * 64:(h + 1) * 64])
    for g in range(8):
        mm(PS[0][:, 0:128], wsf[:, g, :], identf[:])
        tt('dve', WS[:, g, :], PS[0][:, 0:128], maskTf[:], ALU.mult)
    if do_sample:
        mset('dve', wssf[:], 0.0)
        for j in range(16):
            for g in range(8):
                P.dma('sp', wssf[4 * j:4 * j + 4, g, 4 * j:4 * j + 4], w_s[g, 0:4, 0:4].re("t s -> s t"),
                      nowait_w=(j > 0 or g > 0), slow=True)
        tt('dve', WSs[:], wssf[:], maskblk[:].us(1).bc([64, 8, 64]), ALU.mult)
        P.dma('sp', pt_sb[:], ptab[:])
        for a in range(16):
            ts('dve', idx_sb[:, :, a], pt_sb[:], 16, a, ALU.mult, ALU.add)

    act(scb[:], callsb[:], AF.Silu)
    for kc in range(8):
        mm(PS[0][:, kc * 17:(kc + 1) * 17], scb[0:17, kc * 128:(kc + 1) * 128], identb[0:17, 0:17])
    cp('dve', scT[:], PS[0][:, 0:136].re("p (k s) -> p k s", s=17))
    wadaT = w_ada[:].re("(k p) c -> p k c", p=128)
    for ch in range(18):
        slot = ring_next()
        sv = slot[:, 0:4096].re("p (k c) -> p k c", c=512)
        P.dma('pool', sv, wadaT[:, :, ch * 512:(ch + 1) * 512])
        if ch == 0 and os.environ.get('KD2', '0') != '0':
            if os.environ.get('KD2') in ('1', '3'):
                dump('scT', scT[:])
            if os.environ.get('KD2') in ('2', '3'):
                dump('wada0', sv)
        for j in range(4):
            J = ch * 4 + j
            jj = J % 24
            for kc in range(8):
                mm(PS[2][:, jj * 17:(jj + 1) * 17], sv[:, kc, j * 128:(j + 1) * 128], scT[:, kc, :],
                   start=(kc == 0), stop=(kc == 7))
            if jj == 23:
                J0 = J - 23
                tt('dve', modT[:, J0:J0 + 24, :], PS[2][:, 0:408].re("p (j s) -> p j s", s=17),
                   badaT[:, J0:J0 + 24].us(2).bc([128, 24, 17]), ALU.add)

    dump('modT', modT[:])
    def cast_w(dst, src, rows):
        for r0 in range(0, rows, 128):
            P.dma('pool', dst[r0:r0 + 128, :], src[r0:r0 + 128, :], nowait_w=True)
    cast_w(s1g, w1g, D); cast_w(s1u, w1u, D); cast_w(s1d, w1d, DFF)
    cast_w(s_in, w_in, D); cast_w(s_pa, w_pa, 512); cast_w(s_pb, w_pb, 512); cast_w(s_o, w_o, D)
    cast_w(s2g, w2g, D); cast_w(s2u, w2u, D); cast_w(s2d, w2d, DFF)

    def wload(src_view, kcn, ncols):
        slot = ring_next()
        sv = slot[:, 0:kcn * ncols].re("p (k c) -> p k c", c=ncols)
        P.dma('sp', sv, src_view)
        return sv

    def wcols(scr, kcn, c0, ncols):
        return wload(scr[:].re("(k p) c -> p k c", p=128)[:, 0:kcn, c0:c0 + ncols], kcn, ncols)

    def setup_pass(kind):
        if kind == 's':
            nseq, c0, nt = 16, 1, 64
        else:
            nseq, c0, nt = 1, 0, 128
        for i in range(3):
            sc = modT[:, (i * 3 + 1) * 8:(i * 3 + 1) * 8 + 8, c0:c0 + nseq]
            ts('dve', GT[:, i, :, 0:nseq], sc, 1.0, None, ALU.add)
            tt('dve', GT[:, i, :, 0:nseq], GT[:, i, :, 0:nseq],
               gT[:, i * 8:(i + 1) * 8].us(2).bc([128, 8, nseq]), ALU.mult)
            for kc in range(8):
                J = (i * 3 + 2) * 8 + kc
                gsrc = modT[:, J, c0:c0 + nseq]
                if kind == 's':
                    cp('dve', tmpf[:, 0:64].re("p (s g) -> p s g", g=4), gsrc.us(2).bc([128, 16, 4]))
                else:
                    cp('dve', tmpf[:, 0:128], gsrc.bc([128, 128]))
                mm(PS[kc // 4][0:nt, (kc % 4) * 128:(kc % 4 + 1) * 128], tmpf[:, 0:nt], identf[:])
            fac = 1.0 if i == 1 else 0.5
            for hf in range(2):
                ts('dve', GATE[0:nt, i, hf * 512:(hf + 1) * 512], PS[hf][0:nt, :], fac, None, ALU.mult)

    def bview(v, kind, nk, nt):
        if kind == 's':
            return v.us(3).bc([128, nk, 16, 4])
        return v.bc([128, nk, nt])

    def dview(v, kind, nk, nt):
        if kind == 's':
            return v.re("p (k s g) -> p k s g", k=nk, g=4)
        return v.re("p (k t) -> p k t", k=nk)

    def norm_to_nT(kind, i, nsub, nt):
        c0, nseq = (1, 16) if kind == 's' else (0, 1)
        for s in range(nsub):
            act(junk[0:nt, :], xt[0:nt, s, :], AF.Square)
            rsum(st[0:nt, 0:1], junk[0:nt, :])
            rsqrt(st[0:nt, 1:2], st[0:nt, 0:1], 1.0 / D)
            ts('dve', xs[0:nt, :], xt[0:nt, s, :], st[0:nt, 1:2], None, ALU.mult)
            dump('st_' + kind, st[0:nt, 0:2])
            dump('xs_' + kind, xs[0:nt, :])
            dump('gT', gT[:])
            dump('GT_' + kind, GT[:])
            for hf in range(2):
                for k4 in range(4):
                    kc = hf * 4 + k4
                    mm(PS[hf][:, k4 * nt:(k4 + 1) * nt], xs[0:nt, kc * 128:(kc + 1) * 128], identb[0:nt, 0:nt])
                g = GT[:, i, hf * 4:hf * 4 + 4, 0:nseq]
                sh = modT[:, (i * 3) * 8 + hf * 4:(i * 3) * 8 + hf * 4 + 4, c0:c0 + nseq]
                tt('dve', dview(tmpf[:, 0:4 * nt], kind, 4, nt), dview(PS[hf][:, 0:4 * nt], kind, 4, nt),
                   bview(g, kind, 4, nt), ALU.mult)
                ov = nT[:, hf * 4:hf * 4 + 4, s * nt:(s + 1) * nt]
                if kind == 's':
                    ov = ov.re("p k (s g) -> p k s g", g=4)
                tt('pool', ov, dview(tmpf[:, 0:4 * nt], kind, 4, nt), bview(sh, kind, 4, nt), ALU.add)

    def ffn(kind, i, sg_, su_, sd_, nsub, nt):
        NT = nsub * nt
        gi = 0 if i == 0 else 2
        P.inherit([actT], mixviews)
        norm_to_nT(kind, gi, nsub, nt)
        dump('GATE_' + kind, GATE[0:nt, :, :])
        dump('nT%d_' % i + kind, nT[:, :, 0:NT])
        fc = 0
        flip = 0
        for c0 in range(0, DFF, 512):
            ncols = min(512, DFF - c0)
            wg = wcols(sg_, 8, c0, ncols)
            wu = wcols(su_, 8, c0, ncols)
            for j in range(ncols // 128):
                pg, pu = (PS[2], PS[3]) if flip == 0 else (PS[4], PS[5])
                flip ^= 1
                for kc in range(8):
                    mm(pg[:, 0:NT], wg[:, kc, j * 128:(j + 1) * 128], nT[:, kc, 0:NT], start=(kc == 0), stop=(kc == 7))
                for kc in range(8):
                    mm(pu[:, 0:NT], wu[:, kc, j * 128:(j + 1) * 128], nT[:, kc, 0:NT], start=(kc == 0), stop=(kc == 7))
                tg = tmpg if flip else tmph
                act(tg[:, 0:NT], pg[:, 0:NT], AF.Silu)
                tt('dve', actT[:, fc, 0:NT], tg[:, 0:NT], pu[:, 0:NT], ALU.mult)
                fc += 1
        dump('actT%d_' % i + kind, actT[:, :, 0:NT])
        halves = [[0, 1], [2, 3]] if nsub == 4 else [[0]]
        for hv in halves:
            for f0 in range(0, NFC, 4):
                nf = min(4, NFC - f0)
                wd = wload(sd_[f0 * 128:(f0 + nf) * 128, :].re("(f p) d -> p f d", p=128), nf, D)
                for fl in range(nf):
                    f = f0 + fl
                    for si, s in enumerate(hv):
                        for dh in range(2):
                            mm(PS[4 + si * 2 + dh][0:nt, :], actT[:, f, s * nt:(s + 1) * nt], wd[:, fl, dh * 512:(dh + 1) * 512],
                               start=(f == 0), stop=(f == NFC - 1),
                               sig=(f == NFC - 1) or (fl == nf - 1 and si == len(hv) - 1 and dh == 1))
            for si, s in enumerate(hv):
                for dh in range(2):
                    tt('dve', tmpf[0:nt, :], PS[4 + si * 2 + dh][0:nt, :], GATE[0:nt, gi, dh * 512:(dh + 1) * 512], ALU.mult)
                    tt('pool', xt[0:nt, s, dh * 512:(dh + 1) * 512], xt[0:nt, s, dh * 512:(dh + 1) * 512], tmpf[0:nt, :], ALU.add)

    def dumpx(name, kind, nsub, nt):
        dump(name + '_' + kind, xt[0:nt, 0:nsub, :])

    def transpose_cols(dst, src, nt, nblk, psb):
        for b in range(nblk):
            mm(psb[:, b * nt:(b + 1) * nt], src[0:nt, b * 128:(b + 1) * 128], identb[0:nt, 0:nt])

    def mixer(kind, tile_i, nsub, nt):
        import os
        STOP = float(os.environ.get('KSTOP', '9'))
        NT = nsub * nt
        c0, nseq = (1, 16) if kind == 's' else (0, 1)
        norm_to_nT(kind, 1, nsub, nt)
        dump('nTm_' + kind, nT[:, :, 0:NT])
        P.inherit(mixviews, [actT])
        P.inherit([uf, gvf], [yoA, yoB])
        wA = wcols(s_in, 8, 0, 512)
        wA2 = wcols(s_in, 8, 512, 32)
        wB = wcols(s_in, 8, 544, 512)
        wC = wcols(s_in, 8, 1056, 512)
        for s in range(nsub):
            gq = tile_i * 4 + s
            ncol = slice(s * nt, (s + 1) * nt)
            for kc in range(8):
                mm(PS[0][0:nt, :], nT[:, kc, ncol], wA[:, kc, 0:512], start=(kc == 0), stop=(kc == 7))
            for kc in range(8):
                mm(PS[1][0:nt, 0:32], nT[:, kc, ncol], wA2[:, kc, 0:32], start=(kc == 0), stop=(kc == 7))
            act(junk[0:nt, 0:256], PS[0][0:nt, 0:256], AF.Square)
            rsum(st[0:nt, 2:3], junk[0:nt, 0:256])
            rsqrt(st[0:nt, 3:4], st[0:nt, 2:3], 1.0 / 256)
            ts('dve', cq_b[0:nt, :], PS[0][0:nt, 0:256], st[0:nt, 3:4], None, ALU.mult)
            act(junk[0:nt, 256:512], PS[0][0:nt, 256:512], AF.Square)
            rsum(st[0:nt, 4:5], junk[0:nt, 256:512])
            rsqrt(st[0:nt, 5:6], st[0:nt, 4:5], 1.0 / 256)
            stt('dve', kvf[0:nt, :], PS[0][0:nt, 256:512], st[0:nt, 5:6], GKV[0:nt, :], ALU.mult, ALU.mult)
            if STOP <= 1.01:
                return
            if kind == 's':
                cs = css[0:64, :]
            else:
                P.dma('sp', cst[:], c_csp[gq * 128:(gq + 1) * 128, :])
                cs = cst[:, :]
            x1 = PS[1][0:nt, 0:16]; x2 = PS[1][0:nt, 16:32]
            tt('dve', tmph[0:nt, 0:16], x1, cs[:, 0:16], ALU.mult)
            tt('dve', tmph[0:nt, 16:32], x2, cs[:, 16:32], ALU.mult)
            tt('dve', pef[0:nt, 0:16], tmph[0:nt, 0:16], tmph[0:nt, 16:32], ALU.subtract)
            tt('dve', tmph[0:nt, 32:48], x1, cs[:, 16:32], ALU.mult)
            tt('dve', tmph[0:nt, 48:64], x2, cs[:, 0:16], ALU.mult)
            tt('dve', pef[0:nt, 16:32], tmph[0:nt, 32:48], tmph[0:nt, 48:64], ALU.add)
            cp('dve', peb[0:nt, :], pef[0:nt, :])
            if kind == 's':
                P.dma('sp', kv_s[:], kvf[0:64, :], owner=kvf, nowait_w=True)
                P.dma('sp', pe_s[:], pef[0:64, :], owner=pef, nowait_w=True)
                cp('dve', kvb_s[:], kvf[0:64, :])
                kvsrc = kvb_s
            else:
                P.dma('sp', kv_p[gq * 128:(gq + 1) * 128, :], kvf[:], owner=kvf, nowait_w=True)
                P.dma('sp', pe_p[gq * 128:(gq + 1) * 128, :], pef[:], owner=pef, nowait_w=True)
                if STOP <= 1.015:
                    return
                cp('dve', Vc[:, gq, :], kvf[:])
                kvsrc = None
            if STOP <= 1.02:
                return
            for cc in range(2):
                mm(PS[2][:, cc * nt:(cc + 1) * nt], cq_b[0:nt, cc * 128:(cc + 1) * 128], identb[0:nt, 0:nt])
            for cc in range(2):
                ts('dve', cqnT[:, cc, 0:nt], PS[2][:, cc * nt:(cc + 1) * nt], gT[:, 24 + cc:25 + cc], None, ALU.mult)
            for cc in range(2):
                if kind == 's':
                    mm(PS[3][:, cc * nt:(cc + 1) * nt], kvb_s[0:nt, cc * 128:(cc + 1) * 128], identb[0:nt, 0:nt])
                else:
                    mm(PS[3][:, cc * nt:(cc + 1) * nt], Vc[:, gq, cc * 128:(cc + 1) * 128], identb[:])
            mm(PS[3][0:32, 2 * nt:3 * nt], peb[0:nt, :], identb[0:nt, 0:nt])
            if kind == 's':
                cp('dve', kvT_s[:, :, :], PS[3][:, 0:2 * nt].re("p (c t) -> p c t", c=2))
                cp('dve', peT_s[:, :], PS[3][0:32, 2 * nt:3 * nt])
            else:
                cp('dve', KT[:, :, gq * 128:(gq + 1) * 128], PS[3][:, 0:256].re("p (c t) -> p c t", c=2))
                cp('dve', KTpe[:, gq * 128:(gq + 1) * 128], PS[3][0:32, 256:384])
            if STOP <= 1.1:
                return
            for cc in range(2):
                mm(PS[0][0:nt, :], cqnT[:, cc, 0:nt], wuq[:, cc, 0:512], start=(cc == 0), stop=(cc == 1))
            for cc in range(2):
                mm(PS[1][0:nt, 0:256], cqnT[:, cc, 0:nt], wuq[:, cc, 512:768], start=(cc == 0), stop=(cc == 1))
            cp('act', qf[0:nt, 0:512], PS[0][0:nt, :])
            cp('dve', qf[0:nt, 512:768], PS[1][0:nt, 0:256])
            q3 = qf[0:nt, :].re("p (h e) -> p h e", e=96)
            cp('dve', qnb[0:nt, :].re("p (h e) -> p h e", e=64), q3[:, :, 0:64])
            q1 = q3[:, :, 64:80]; q2 = q3[:, :, 80:96]
            cosb = cs[:, 0:16].us(1).bc([nt, 8, 16]); sinb = cs[:, 16:32].us(1).bc([nt, 8, 16])
            ta = tmpg[0:nt, 0:128].re("p (h e) -> p h e", e=16); tb = tmpg[0:nt, 128:256].re("p (h e) -> p h e", e=16)
            tt('dve', ta, q1, cosb, ALU.mult)
            tt('dve', tb, q2, sinb, ALU.mult)
            tt('dve', qpf[0:nt, :, 0:16], ta, tb, ALU.subtract)
            tt('dve', ta, q1, sinb, ALU.mult)
            tt('dve', tb, q2, cosb, ALU.mult)
            tt('dve', qpf[0:nt, :, 16:32], ta, tb, ALU.add)
            cp('dve', qpb[0:nt, :, :], qpf[0:nt, :, :])
            if STOP <= 1.2:
                return
            transpose_cols(None, qnb, nt, 4, PS[2])
            cp('dve', qnT[:, :, 0:nt], PS[2][:, 0:4 * nt].re("p (b t) -> p b t", b=4))
            if STOP <= 1.25:
                return
            for h in range(NH):
                mm(PS[3][0:32, h * nt:(h + 1) * nt] if nt * 8 <= 512 else PS[3 + h // 4][0:32, (h % 4) * nt:(h % 4 + 1) * nt],
                   qpb[0:nt, h, :], identb[0:nt, 0:nt])
            if nt * 8 <= 512:
                cp('act', QTpe[:, 0:8 * nt], PS[3][0:32, 0:8 * nt])
            else:
                cp('act', QTpe[:, 0:512], PS[3][0:32, :])
                cp('act', QTpe[:, 512:1024], PS[4][0:32, :])
            if STOP <= 1.27:
                return
            for cc in range(2):
                for h in range(NH):
                    pr, h2 = h // 2, h % 2
                    if nt * 8 <= 512:
                        o = PS[5 + cc][:, h * nt:(h + 1) * nt]
                    else:
                        o = PS[5 + cc][:, (h % 4) * nt:(h % 4 + 1) * nt] if h < 4 else None
                    if o is None:
                        continue
                    mm(o, wukT[:, h, cc * 128:(cc + 1) * 128], qnT[:, pr, 0:nt])
                n1 = min(8 * nt, 512)
                cp('act' if cc == 0 else 'dve', QT[:, cc, 0:n1], PS[5 + cc][:, 0:n1])
            if nt * 8 > 512:
                for cc in range(2):
                    for h in range(4, NH):
                        pr, h2 = h // 2, h % 2
                        mm(PS[5 + cc][:, (h % 4) * nt:(h % 4 + 1) * nt], wukT[:, h, cc * 128:(cc + 1) * 128],
                           qnT[:, pr, 0:nt])
                    cp('act' if cc == 0 else 'dve', QT[:, cc, 512:1024], PS[5 + cc][:, :])
            if STOP <= 1.3:
                return
            if os.environ.get('KD2', '0') == '4':
                dump('QT_' + kind, QT[:, :, 0:8 * nt])
            if os.environ.get('KD2', '0') == '5':
                dump('QTpe_' + kind, QTpe[:, 0:8 * nt])
            for kc in range(8):
                mm(PS[0][0:nt, :], nT[:, kc, ncol], wB[:, kc, :], start=(kc == 0), stop=(kc == 7))
            for kc in range(8):
                mm(PS[1][0:nt, :], nT[:, kc, ncol], wC[:, kc, :], start=(kc == 0), stop=(kc == 7))
            act(uf[0:nt, :], PS[0][0:nt, :], AF.Gelu_apprx_tanh)
            act(gvf[0:nt, :], PS[1][0:nt, :], AF.Gelu_apprx_tanh)
            rsum(st[0:nt, 6:7], gvf[0:nt, :])
            act(junk[0:nt, 0:512], gvf[0:nt, :], AF.Square)
            rsum(st[0:nt, 7:8], junk[0:nt, 0:512])
            ts('dve', st[0:nt, 8:9], st[0:nt, 6:7], 1.0 / 512, None, ALU.mult)
            ts('dve', st[0:nt, 9:10], st[0:nt, 7:8], 1.0 / 512, EPS, ALU.mult, ALU.add)
            tt('dve', st[0:nt, 10:11], st[0:nt, 8:9], st[0:nt, 8:9], ALU.mult)
            tt('dve', st[0:nt, 9:10], st[0:nt, 9:10], st[0:nt, 10:11], ALU.subtract)
            act(st[0:nt, 9:10], st[0:nt, 9:10], AF.Sqrt)
            recip(st[0:nt, 9:10], st[0:nt, 9:10])
            ts('dve', vnf[0:nt, :], gvf[0:nt, :], st[0:nt, 8:9], st[0:nt, 9:10], ALU.subtract, ALU.mult)
            tt('dve', vnf[0:nt, :], vnf[0:nt, :], LNG[0:nt, :], ALU.mult)
            if kind == 's':
                tt('dve', gvout[0:64, :], vnf[0:64, :], LNB[0:64, :], ALU.add)
                P.dma('sp', gv_s[:], gvout[0:64, :], owner=gvout, nowait_w=True)
                cp('dve', vnb[0:nt, :], gvout[0:64, :])
            else:
                tt('pool', vnb[0:nt, :], vnf[0:nt, :], LNB[0:nt, :], ALU.add)
            for g in range(8):
                wsm = WSs[0:64, g, :] if kind == 's' else WS[:, g, :]
                mm(PS[2][0:nt, g * 64:(g + 1) * 64], wsm, vnb[0:nt, g * 64:(g + 1) * 64])
            bsv = bsrep[0:64, :] if kind == 's' else bsT[0:nt, :]
            tt('dve', tmph[0:nt, :].re("p (g e) -> p g e", e=64), PS[2][0:nt, :].re("p (g e) -> p g e", e=64),
               bsv.us(2).bc([nt, 8, 64]), ALU.add)
            tt('dve', obb[0:nt, :], tmph[0:nt, :], uf[0:nt, :], ALU.mult)
            transpose_cols(None, obb, nt, 4, PS[3])
            cp('act', obT[:, :, ncol], PS[3][:, 0:4 * nt].re("p (b t) -> p b t", b=4))
            if STOP <= 1.4:
                return
            if kind == 'p':
                attend_prompt(gq)
                nq = 128
            if STOP <= 1.5:
                return
            if kind == 'p':
                dump('olat_p', olat[:])
            if kind == 'p':
                for cc in range(2):
                    for hb in range(2):
                        for h4 in range(4):
                            h = hb * 4 + h4
                            mm(PS[hb][:, h4 * 128:(h4 + 1) * 128], olat[:, h, cc * 128:(cc + 1) * 128], identb[:])
                        cp('act' if hb == 0 else 'dve', olatT[:, cc, hb * 512:(hb + 1) * 512], PS[hb][:, :])
                for pr in range(4):
                    k = 0
                    for h in (2 * pr, 2 * pr + 1):
                        for cc in range(2):
                            mm(PS[2][:, pr * 128:(pr + 1) * 128], wuvpad[:, h, cc, :], olatT[:, cc, h * 128:(h + 1) * 128],
                               start=(k == 0), stop=(k == 3))
                            k += 1
                cp('dve', oaT[:, :, ncol], PS[2][:, :].re("p (b t) -> p b t", b=4))
        if kind == 's':
            attend_sample()
            for pr in range(4):
                k = 0
                for h in (2 * pr, 2 * pr + 1):
                    for cc in range(2):
                        rv = olatT[:, cc, 0:512].re("p (s h q) -> p s h q", h=8, q=4)[:, :, h, :]
                        mm(PS[2][:, pr * 64:(pr + 1) * 64], wuvpad[:, h, cc, :], rv, start=(k == 0), stop=(k == 3))
                        k += 1
            cp('dve', oaT[:, :, 0:64], PS[2][:, 0:256].re("p (b t) -> p b t", b=4))
        if STOP <= 1.6:
            return
        dump('oaT_' + kind, oaT[:, :, 0:NT])
        dump('obT_' + kind, obT[:, :, 0:NT])
        for dq in range(2):
            wpa = wcols(s_pa, 4, dq * 512, 512)
            wpb = wcols(s_pb, 4, dq * 512, 512)
            wga = wcols(s_in, 8, 1568 + dq * 512, 512)
            wgb = wcols(s_in, 8, 2592 + dq * 512, 512)
            for d4 in range(4):
                dc = dq * 4 + d4
                for kc in range(4):
                    mm(PS[0][:, 0:NT], wpa[:, kc, d4 * 128:(d4 + 1) * 128], oaT[:, kc, 0:NT], start=(kc == 0), stop=(kc == 3))
                for kc in range(8):
                    mm(PS[1][:, 0:NT], wga[:, kc, d4 * 128:(d4 + 1) * 128], nT[:, kc, 0:NT], start=(kc == 0), stop=(kc == 7))
                for kc in range(4):
                    mm(PS[2][:, 0:NT], wpb[:, kc, d4 * 128:(d4 + 1) * 128], obT[:, kc, 0:NT], start=(kc == 0), stop=(kc == 3))
                for kc in range(8):
                    mm(PS[3][:, 0:NT], wgb[:, kc, d4 * 128:(d4 + 1) * 128], nT[:, kc, 0:NT], start=(kc == 0), stop=(kc == 7))
                act(tmpg[:, 0:NT], PS[1][:, 0:NT], AF.Sigmoid)
                tt('dve', tmpg[:, 0:NT], tmpg[:, 0:NT], PS[0][:, 0:NT], ALU.mult)
                act(tmph[:, 0:NT], PS[3][:, 0:NT], AF.Sigmoid)
                tt('dve', tmph[:, 0:NT], tmph[:, 0:NT], PS[2][:, 0:NT], ALU.mult)
                tt('pool', mergedT[:, dc, 0:NT], tmpg[:, 0:NT], tmph[:, 0:NT], ALU.add)
        if STOP <= 1.7:
            return
        dump('mergedT_' + kind, mergedT[:, :, 0:NT])
        for dh in range(2):
            wo = wcols(s_o, 8, dh * 512, 512)
            for s in range(nsub):
                pb = PS[4 + (s % 2)]
                for kc in range(8):
                    mm(pb[0:nt, :], mergedT[:, kc, s * nt:(s + 1) * nt], wo[:, kc, :], start=(kc == 0), stop=(kc == 7))
                tt('dve', tmpf[0:nt, :], pb[0:nt, :], GATE[0:nt, 1, dh * 512:(dh + 1) * 512], ALU.mult)
                tt('pool', xt[0:nt, s, dh * 512:(dh + 1) * 512], xt[0:nt, s, dh * 512:(dh + 1) * 512], tmpf[0:nt, :], ALU.add)

    def attend_prompt(gq):
        for ch in range(2):
            cols = slice(ch * 512, (ch + 1) * 512)
            for kj in range(gq + 1):
                pS = PS[2 + (kj % 2)]
                kc_ = slice(kj * 128, (kj + 1) * 128)
                mm(pS[:, :], KT[:, 0, kc_], QT[:, 0, cols], start=True, stop=False)
                mm(pS[:, :], KT[:, 1, kc_], QT[:, 1, cols], start=False, stop=False)
                mm(pS[:, :], KTpe[0:32, kc_], QTpe[0:32, cols], start=False, stop=True)
                pt_ = PTt[kj % 2]
                act(pt_[:, :], pS[:, :], AF.Exp, scale=SM_SCALE)
                if kj == gq:
                    tt('pool', pt_[:, :].re("p (h q) -> p h q", q=128), pt_[:, :].re("p (h q) -> p h q", q=128),
                       maskTb[:, :].us(1).bc([128, 4, 128]), ALU.mult)
                for hh in range(4):
                    mm(PS[4 + hh // 2][:, (hh % 2) * 256:(hh % 2 + 1) * 256], pt_[:, hh * 128:(hh + 1) * 128], Vc[:, kj, :],
                       start=(kj == 0 and hh % 2 == 0), stop=(kj == gq), skip=True)
                    mm(PS[6][:, hh:hh + 1], pt_[:, hh * 128:(hh + 1) * 128], onesb[:, 0:1],
                       start=(kj == 0 and hh == 0), stop=(kj == gq), skip=True)
            P.op('dve', (lambda a, b: (lambda e: e.reciprocal(out=a, in_=b)))(rden[:, ch * 4:ch * 4 + 4].ap, PS[6][:, 0:4].ap),
                 [PS[6]], [rden])
            for hh in range(4):
                h = ch * 4 + hh
                ts('dve', olat[:, h, :], PS[4 + hh // 2][:, (hh % 2) * 256:(hh % 2 + 1) * 256], rden[:, h:h + 1], None, ALU.mult)

    def attend_sample():
        NB = 16
        for sq in range(16):
            qv = [QT[:, cc, 0:512].re("p (h s q) -> p s h q", h=8, q=4)[:, sq, :, :] for cc in range(2)]
            qpv = QTpe[0:32, 0:512].re("p (h s q) -> p s h q", h=8, q=4)[:, sq, :, :]
            for cc in range(2):
                cp('dve', qs_sb[:, cc, :].re("p (h q) -> p h q", q=4), qv[cc])
            cp('dve', qs_sb[0:32, 2, :].re("p (h q) -> p h q", q=4), qpv)
            first = True
            for rb in range(NB):
                lk = LK[rb % 2]; lp = LP[rb % 2]
                ia = idx_sb[:, sq, rb:rb + 1].ap
                lka = lk[:].re("p r c -> p (r c)").ap; lpa = lp[:].re("p r c -> p (r c)").ap
                ckv = cache_kv[:].re("n (a r) c -> (n a) (r c)", r=8).ap
                cpe = cache_pe[:].re("n (a r) c -> (n a) (r c)", r=8).ap
                P.dma('pool', lk[:], cache_kv[:], extra=[idx_sb], fn=(lambda o, i_, x: (lambda e: e.indirect_dma_start(
                    out=o, out_offset=None, in_=i_, in_offset=bass.IndirectOffsetOnAxis(ap=x, axis=0))))(lka, ckv, ia))
                P.dma('pool', lp[:], cache_pe[:], extra=[idx_sb], fn=(lambda o, i_, x: (lambda e: e.indirect_dma_start(
                    out=o, out_offset=None, in_=i_, in_offset=bass.IndirectOffsetOnAxis(ap=x, axis=0))))(lpa, cpe, ia))
                for r4 in range(2):
                    for rr in range(4):
                        r = r4 * 4 + rr
                        mm(PS[0][:, rr * 128:(rr + 1) * 128], lk[:, r, 0:128], identb[:])
                        mm(PS[1][:, rr * 128:(rr + 1) * 128], lk[:, r, 128:256], identb[:])
                        mm(PS[7][0:32, rr * 128:(rr + 1) * 128], lp[:, r, :], identb[:])
                    cp('act', KTs[:, 0, :], PS[0][:, :])
                    cp('dve', KTs[:, 1, :], PS[1][:, :])
                    cp('act', KTpes[:, :], PS[7][0:32, :])
                    for rr in range(4):
                        r = r4 * 4 + rr
                        so = PS[2][:, r * 32:(r + 1) * 32]
                        mm(so, KTs[:, 0, rr * 128:(rr + 1) * 128], qs_sb[:, 0, :], start=True, stop=False)
                        mm(so, KTs[:, 1, rr * 128:(rr + 1) * 128], qs_sb[:, 1, :], start=False, stop=False)
                        mm(so, KTpes[0:32, rr * 128:(rr + 1) * 128], qs_sb[0:32, 2, :], start=False, stop=True)
                pt_ = PTt[rb % 2]
                act(pt_[:, 0:256], PS[2][:, 0:256], AF.Exp, scale=SM_SCALE)
                for r in range(8):
                    mm(PS[4][0:32, 0:256], pt_[:, r * 32:(r + 1) * 32], lk[:, r, :], start=first, stop=False)
                    mm(PS[5][0:32, 0:1], pt_[:, r * 32:(r + 1) * 32], onesb[:, 0:1], start=first, stop=False)
                    first = False
            tcol = slice(sq * 4, sq * 4 + 4)
            mm(PS[3][0:4, 0:32], kvT_s[:, 0, tcol], qs_sb[:, 0, :], start=True, stop=False)
            mm(PS[3][0:4, 0:32], kvT_s[:, 1, tcol], qs_sb[:, 1, :], start=False, stop=False)
            mm(PS[3][0:4, 0:32], peT_s[0:32, tcol], qs_sb[0:32, 2, :], start=False, stop=True)
            act(PTn[:, :], PS[3][0:4, 0:32], AF.Exp, scale=SM_SCALE)
            tt('dve', PTn[:, :], PTn[:, :], masknewb[:, :], ALU.mult)
            mm(PS[3][0:4, 256:512], identb[0:64, tcol], kvb_s[0:64, :])
            cp('dve', Vn[:, :], PS[3][0:4, 256:512])
            mm(PS[4][0:32, 0:256], PTn[0:4, :], Vn[0:4, :], start=False, stop=True)
            mm(PS[5][0:32, 0:1], PTn[0:4, :], onesb[0:4, 0:1], start=False, stop=True)
            P.op('dve', (lambda a, b: (lambda e: e.reciprocal(out=a, in_=b)))(rden[0:32, 0:1].ap, PS[5][0:32, 0:1].ap),
                 [PS[5]], [rden])
            ts('dve', olat_s[:, :], PS[4][0:32, 0:256], rden[0:32, 0:1], None, ALU.mult)
            for cc in range(2):
                mm(PS[6][:, cc * 32:(cc + 1) * 32], olat_s[0:32, cc * 128:(cc + 1) * 128], identb[0:32, 0:32])
            cp('dve', olatT[:, :, sq * 32:(sq + 1) * 32], PS[6][:, 0:64].re("p (c t) -> p c t", c=2))

    if do_sample:
        qs_sb = sb('qs_sb', [128, 3, 32], BF16)
        bsrep = sb('bsrep', [64, 8])
        for j in range(16):
            P.dma('sp', bsrep[4 * j:4 * j + 4, :], b_s[:, 0:4].re("g t -> t g"), nowait_w=True, slow=True)

    def final_out(kind, tile_i, nsub, nt):
        P.inherit([yoA, yoB], [uf, gvf])
        for s in range(nsub):
            act(junk[0:nt, :], xt[0:nt, s, :], AF.Square)
            rsum(st[0:nt, 12:13], junk[0:nt, :])
            rsqrt(st[0:nt, 13:14], st[0:nt, 12:13], 1.0 / D)
            for dh, yo in enumerate((yoA, yoB)):
                dsl = slice(dh * 512, (dh + 1) * 512)
                stt('dve', yo[0:nt, :], xt[0:nt, s, dsl], st[0:nt, 13:14], GF[0:nt, dsl], ALU.mult, ALU.mult)
                if kind == 's':
                    P.dma('sp', y_s[:, dsl], yo[0:64, :], owner=yo, nowait_w=True)
                else:
                    r0 = tile_i * 512 + s * 128
                    P.dma('sp', y_p[r0:r0 + 128, dsl], yo[:], owner=yo, nowait_w=True)

    def run_pass(kind, tile_i):
        if kind == 's':
            nsub, nt = 1, 64
            P.dma('sp', xt[0:64, 0, :], x_s[:])
        else:
            nsub, nt = 4, 128
            P.dma('sp', xt[:], x_p[tile_i * 512:(tile_i + 1) * 512, :].re("(s p) d -> p s d", p=128))
        import os
        STOP = float(os.environ.get('KSTOP', '9'))
        ffn(kind, 0, s1g, s1u, s1d, nsub, nt)
        dumpx('x1', kind, nsub, nt)
        if STOP <= 1:
            return
        mixer(kind, tile_i, nsub, nt)
        dumpx('x2', kind, nsub, nt)
        if STOP <= 2:
            return
        ffn(kind, 2, s2g, s2u, s2d, nsub, nt)
        dumpx('x3', kind, nsub, nt)
        final_out(kind, tile_i, nsub, nt)

    P.inherit([actT], [wuqf, wukf, wuvf, wsf, wukb] + ([wssf] if do_sample else []))
    P.inherit([xt], [callsb])
    P.inherit([xs], [scb])
    if do_sample:
        setup_pass('s')
        run_pass('s', 0)
        P.inherit([Vc], salias)
    if n_tiles > 0:
        setup_pass('p')
        for ti in range(n_tiles):
            run_pass('p', ti)

    P.finish()
    print('ops', {e: len(v) for e, v in P.ops.items()}, 'cnt', P.cnt, 'maxd', max([t.dcnt for t in P.dtiles] + [0]), 'nd', len(P.dtiles))
    P.emit()
    es.close()
    return nc


def _consts():
    identf = np.eye(128, dtype=np.float32)
    maskT = (np.arange(128)[None, :] >= np.arange(128)[:, None]).astype(np.float32)
    masknew = np.zeros((4, 32), np.float32)
    for j in range(4):
        for h in range(8):
            for q in range(4):
                masknew[j, h * 4 + q] = 1.0 if j <= q else 0.0
    s_ = np.arange(64)
    maskblk = ((s_[:, None] // 4 == s_[None, :] // 4) & (s_[None, :] % 4 >= s_[:, None] % 4)).astype(np.float32)
    half = 16
    freqs = (np.float32(10000.0) ** (np.float32(-2.0) * np.arange(half, dtype=np.float32) / np.float32(32))).astype(np.float32)

    def tab(pos):
        ang = pos.astype(np.float32)[:, None] * freqs[None, :]
        return np.concatenate([np.cos(ang), np.sin(ang)], axis=1).astype(np.float32)
    csp = tab(np.arange(SEQ))
    css = tab(PAST + (np.arange(64) % 4))
    return dict(c_identf=identf, c_maskT=maskT, c_masknew=masknew, c_maskblk=maskblk, c_csp=csp, c_css=css)


def make_in_maps(inputs, cores, n_pool=None, cache_override=None):
    f = lambda a: np.ascontiguousarray(np.asarray(a))
    I = {k: np.asarray(v) for k, v in inputs.items()}
    consts = _consts()
    shared = dict(
        w_ada=f(I['w_ada'][0]), b_ada=f(I['b_ada'][0].reshape(72, 128)),
        gall=f(np.concatenate([I['g_ffn1'][0].reshape(8, 128), I['g_mix'][0].reshape(8, 128), I['g_ffn2'][0].reshape(8, 128),
                               I['g_q'][0].reshape(2, 128), I['g_kv'][0].reshape(2, 128)], axis=0)),
        w1g=f(I['w1_gate'][0]), w1u=f(I['w1_up'][0]), w1d=f(I['w1_down'][0]),
        w2g=f(I['w2_gate'][0]), w2u=f(I['w2_up'][0]), w2d=f(I['w2_down'][0]),
        w_in=f(I['w_in'][0]), w_uq=f(I['w_uq'][0]),
        w_uk=f(I['w_uk'][0].reshape(256, 512)), w_uv=f(I['w_uv'][0].reshape(256, 512)),
        lnv=f(np.stack([I['ln_v_g'][0], I['ln_v_b'][0]])), w_s=f(I['w_s'][0]), b_s=f(I['b_s'][0]),
        w_pa=f(I['w_pa'][0]), w_pb=f(I['w_pb'][0]), w_o=f(I['w_o'][0]),
        g_kv1=f(I['g_kv'][0]), g_final=f(I['g_final']), **consts)
    maps = []
    for c in cores:
        m = dict(shared)
        m['x_p'] = f(I['x_prompt'][c])
        m['x_s'] = f(I['x_sample'][16 * c:16 * c + 16].reshape(64, D))
        m['c_all'] = f(np.concatenate([I['c_prompt'][c:c + 1], I['c_sample'][16 * c:16 * c + 16]], axis=0))
        if cache_override is not None:
            m['cache_kv'], m['cache_pe'], m['ptab'] = cache_override[c]
        else:
            m['cache_kv'] = I['cache_kv'][0]
            m['cache_pe'] = I['cache_pe'][0]
            m['ptab'] = f(I['page_table'][16 * c:16 * c + 16].astype(np.int32).T)
        maps.append(m)
    return maps


_NC = {}


def kernel(**inputs):
    n_pool = int(np.asarray(inputs['cache_kv']).shape[1])
    if n_pool not in _NC:
        _NC[n_pool] = build(n_pool=n_pool)
    nc = _NC[n_pool]
    maps = make_in_maps(inputs, list(range(8)))
    res = run_bass_kernel_spmd(nc, maps, core_ids=list(range(8)))
    R = res.results
    y_prompt = np.stack([R[c]['y_p'] for c in range(8)]).astype(np.float32)
    y_sample = np.concatenate([R[c]['y_s'].reshape(16, 4, D) for c in range(8)]).astype(np.float32)
    kv_prompt = np.stack([R[c]['kv_p'] for c in range(8)])[None].astype(np.float32)
    pe_prompt = np.stack([R[c]['pe_p'] for c in range(8)])[None].astype(np.float32)
    kv_sample = np.concatenate([R[c]['kv_s'].reshape(16, 4, 256) for c in range(8)])[None].astype(np.float32)
    pe_sample = np.concatenate([R[c]['pe_s'].reshape(16, 4, 32) for c in range(8)])[None].astype(np.float32)
    gv_sample = np.concatenate([R[c]['gv_s'].reshape(16, 4, 512) for c in range(8)])[None].astype(np.float32)
    return (y_prompt, y_sample, kv_prompt, pe_prompt, kv_sample, pe_sample, gv_sample)
```

```python
import numpy as np
from contextlib import ExitStack
import concourse.bass as bass
import concourse.mybir as mybir
from concourse.bass_utils import run_bass_kernel_spmd

F32, BF16, I32 = mybir.dt.float32, mybir.dt.bfloat16, mybir.dt.int32
AF = mybir.ActivationFunctionType
ALU = mybir.AluOpType

D = 1024
DFF = 2816
NFC = 22
NIN = 3616
SEQ = 4096
NH = 8
EPS = 1e-6
SM_SCALE = 96 ** -0.5
PAST = 16384
ENGS = ['pe', 'act', 'dve', 'pool', 'sp']


class Tl:
    def __init__(s, name, h):
        s.name, s.h = name, h
        s.w = None
        s.r = {}
        s.pre = {}
        s.dsem = None
        s.dcnt = 0

    def __getitem__(s, k):
        return V(s, s.h[k])


class V:
    def __init__(s, t, ap):
        s.t, s.ap = t, ap

    def __getitem__(s, k):
        return V(s.t, s.ap[k])

    def f(s, fn):
        return V(s.t, fn(s.ap))

    def re(self, pat, **kw):
        return V(self.t, self.ap.rearrange(pat, **kw))

    def bc(s, shape):
        return V(s.t, s.ap.broadcast_to(list(shape)))

    def us(s, ax):
        return V(s.t, s.ap.unsqueeze(ax))


class Prog:
    def __init__(s, nc, es):
        s.nc, s.es = nc, es
        s.sem = {e: es.enter_context(nc.semaphore('s_' + e)) for e in ENGS}
        s.cnt = {e: 0 for e in ENGS}
        s.seen = {e: {} for e in ENGS}
        s.ops = {e: [] for e in ENGS}
        s.dtiles = []

    def _waits(s, eng, reads, writes, nowait_w=False):
        need = {}

        def add(ev):
            sm, val = ev
            if sm is s.sem[eng] and eng in ('pe', 'sp'):
                return
            k = id(sm)
            if k not in need or need[k][1] < val:
                need[k] = (sm, val)
        for t in reads:
            if t.w is not None:
                add(t.w)
            for ev in t.pre.values():
                add(ev)
        for t in writes:
            if t.w is not None and not nowait_w:
                add(t.w)
            for ev in t.r.values():
                add(ev)
            for ev in t.pre.values():
                add(ev)
        out = []
        for k, (sm, val) in need.items():
            if s.seen[eng].get(k, 0) < val:
                s.seen[eng][k] = val
                out.append((sm, val))
        return out

    @staticmethod
    def _addr(t, ev):
        k = id(ev[0])
        if k not in t.r or t.r[k][1] < ev[1]:
            t.r[k] = ev

    def op(s, eng, fn, reads, writes, sig=True):
        reads = list({id(t): t for t in reads}.values())
        writes = list({id(t): t for t in writes}.values())
        waits = s._waits(eng, reads, writes)
        if sig:
            s.cnt[eng] += 1
            ev = (s.sem[eng], s.cnt[eng])
        else:
            ev = (s.sem[eng], s.cnt[eng] + 1)
        s.ops[eng].append((waits, fn, s.sem[eng] if sig else None, 1))
        for t in writes:
            t.w = ev
            t.r = {}
            t.pre = {}
        for t in reads:
            s._addr(t, ev)

    def dma(s, q, out, in_, nowait_w=False, owner=None, fn=None, extra=(), slow=False):
        t = owner if owner is not None else out.t
        if t.dsem is None:
            t.dsem = s.es.enter_context(s.nc.semaphore('d%d' % len(s.dtiles)))
            s.dtiles.append(t)
        waits = s._waits(q, [in_.t] + list(extra), [out.t], nowait_w)
        t.dcnt += 16
        ev = (t.dsem, t.dcnt)
        if fn is None:
            oa, ia = out.ap, in_.ap
            if slow:
                fn = lambda e: e.dma_start(out=oa, in_=ia, allow_slow_non_contiguous=True)
            else:
                fn = lambda e: e.dma_start(out=oa, in_=ia)
        s.ops[q].append((waits, fn, t.dsem, 16))
        out.t.w = ev
        if not nowait_w:
            out.t.r = {}
            out.t.pre = {}
        s._addr(in_.t, ev)
        for x in extra:
            s._addr(x, ev)

    @staticmethod
    def inherit(dsts, srcs):
        evs = {}
        for t in srcs:
            for ev in ([t.w] if t.w is not None else []) + list(t.r.values()) + list(t.pre.values()):
                k = id(ev[0])
                if k not in evs or evs[k][1] < ev[1]:
                    evs[k] = ev
        for d in dsts:
            for k, ev in evs.items():
                if k not in d.pre or d.pre[k][1] < ev[1]:
                    d.pre[k] = ev

    def finish(s):
        waits = []
        for t in s.dtiles:
            waits.append((t.dsem, t.dcnt))
        for e in ENGS:
            if e != 'sp' and s.cnt[e] > 0:
                waits.append((s.sem[e], s.cnt[e]))
        s.ops['sp'].append((waits, None, None, 0))

    def emit(s):
        nc = s.nc

        def replay(name, e):
            for waits, fn, sem, inc in s.ops[name]:
                for sm, val in waits:
                    e.wait_ge(sm, val)
                if fn is None:
                    continue
                ins = fn(e)
                if sem is not None:
                    ins.then_inc(sem, inc)
        with nc.Block() as block:
            @block.tensor
            def _(e):
                replay('pe', e)

            @block.scalar
            def _(e):
                replay('act', e)

            @block.vector
            def _(e):
                replay('dve', e)

            @block.gpsimd
            def _(e):
                replay('pool', e)

            @block.sync
            def _(e):
                replay('sp', e)


def build(n_pool=20480, n_tiles=8, do_sample=True):
    nc = bass.Bass("TRN2", target_bir_lowering=False)
    es = ExitStack()
    P = Prog(nc, es)

    def din(name, shape, dt=F32):
        return Tl(name, nc.dram_tensor(name, list(shape), dt, kind="ExternalInput").ap())

    def dout(name, shape, dt=F32):
        return Tl(name, nc.dram_tensor(name, list(shape), dt, kind="ExternalOutput").ap())

    def dscr(name, shape, dt=BF16):
        return Tl(name, nc.dram_tensor(name, list(shape), dt).ap())

    def sb(name, shape, dt=F32):
        return Tl(name, es.enter_context(nc.sbuf_tensor(name, list(shape), dt)))

    import os
    DBG = os.environ.get('KDBG', '0') == '1'
    dseen = set()

    def dump(name, v):
        if not DBG or name in dseen:
            return
        dseen.add(name)
        o = dout('dbg_' + name, list(v.ap.shape), v.ap.dtype)
        P.dma('sp', o[:], v)

    def mm(out, lhsT, rhs, start=True, stop=True, sig=None, skip=False):
        oa, la, ra = out.ap, lhsT.ap, rhs.ap
        if skip:
            P.op('pe', lambda e: e.matmul(oa, lhsT=la, rhs=ra, start=start, stop=stop, skip_group_check=True),
                 [lhsT.t, rhs.t], [out.t], sig=(stop if sig is None else sig))
        else:
            P.op('pe', lambda e: e.matmul(oa, lhsT=la, rhs=ra, start=start, stop=stop),
                 [lhsT.t, rhs.t], [out.t], sig=(stop if sig is None else sig))

    def act(out, in_, func, scale=None, bias=None, accum=None, eng='act'):
        oa, ia = out.ap, in_.ap
        kw = {}
        rd = [in_.t]
        wr = [out.t]
        if scale is not None:
            if isinstance(scale, V):
                kw['scale'] = scale.ap
                rd.append(scale.t)
            else:
                kw['scale'] = scale
        if bias is not None:
            if isinstance(bias, V):
                kw['bias'] = bias.ap
                rd.append(bias.t)
            else:
                kw['bias'] = bias
        if accum is not None:
            kw['accum_out'] = accum.ap
            wr.append(accum.t)
        P.op('act', lambda e: e.activation(out=oa, in_=ia, func=func, **kw), rd, wr)

    def tt(eng, out, in0, in1, op):
        oa, a0, a1 = out.ap, in0.ap, in1.ap
        P.op(eng, lambda e: e.tensor_tensor(out=oa, in0=a0, in1=a1, op=op), [in0.t, in1.t], [out.t])

    def ts(eng, out, in0, s1, s2, op0, op1=None):
        oa, a0 = out.ap, in0.ap
        rd = [in0.t]
        if isinstance(s1, V):
            rd.append(s1.t)
            s1 = s1.ap
        if isinstance(s2, V):
            rd.append(s2.t)
            s2 = s2.ap
        if op1 is None:
            P.op(eng, lambda e: e.tensor_scalar(out=oa, in0=a0, scalar1=s1, scalar2=None, op0=op0), rd, [out.t])
        else:
            P.op(eng, lambda e: e.tensor_scalar(out=oa, in0=a0, scalar1=s1, scalar2=s2, op0=op0, op1=op1), rd, [out.t])

    def stt(eng, out, in0, sc, in1, op0, op1):
        oa, a0, a1 = out.ap, in0.ap, in1.ap
        rd = [in0.t, in1.t]
        if isinstance(sc, V):
            rd.append(sc.t)
            sc = sc.ap
        P.op(eng, lambda e: e.scalar_tensor_tensor(out=oa, in0=a0, scalar=sc, in1=a1, op0=op0, op1=op1), rd, [out.t])

    def cp(eng, out, in_):
        oa, ia = out.ap, in_.ap
        if eng == 'act':
            P.op('act', lambda e: e.activation(out=oa, in_=ia, func=AF.Identity), [in_.t], [out.t])
        else:
            P.op(eng, lambda e: e.tensor_copy(out=oa, in_=ia), [in_.t], [out.t])

    def recip(out, in_):
        oa, ia = out.ap, in_.ap
        P.op('dve', lambda e: e.reciprocal(out=oa, in_=ia), [in_.t], [out.t])

    def rsqrt(out, in_, scale):
        ts('dve', out, in_, scale, EPS, ALU.mult, ALU.add)
        act(out, out, AF.Sqrt)
        recip(out, out)

    def rsum(out, in_):
        oa, ia = out.ap, in_.ap
        P.op('dve', lambda e: e.reduce_sum(out=oa, in_=ia, axis=mybir.AxisListType.X), [in_.t], [out.t])

    def mset(eng, out, val):
        oa = out.ap
        P.op(eng, lambda e: e.memset(oa, val), [], [out.t])

    x_p = din('x_p', [SEQ, D])
    x_s = din('x_s', [64, D])
    cache_kv = din('cache_kv', [n_pool, 128, 256])
    cache_pe = din('cache_pe', [n_pool, 128, 32])
    ptab = din('ptab', [128, 16], I32)
    c_all = din('c_all', [17, D])
    w_ada = din('w_ada', [D, 9 * D])
    b_ada = din('b_ada', [72, 128])
    gall = din('gall', [28, 128])
    w1g = din('w1g', [D, DFF]); w1u = din('w1u', [D, DFF]); w1d = din('w1d', [DFF, D])
    w2g = din('w2g', [D, DFF]); w2u = din('w2u', [D, DFF]); w2d = din('w2d', [DFF, D])
    w_in = din('w_in', [D, NIN])
    w_uq = din('w_uq', [256, 768])
    w_uk = din('w_uk', [256, 512])
    w_uv = din('w_uv', [256, 512])
    lnv = din('lnv', [2, 512])
    w_s = din('w_s', [8, 128, 128])
    b_s = din('b_s', [8, 128])
    w_pa = din('w_pa', [512, D]); w_pb = din('w_pb', [512, D]); w_o = din('w_o', [D, D])
    g_kv1 = din('g_kv1', [256])
    g_final = din('g_final', [D])
    c_identf = din('c_identf', [128, 128])
    c_maskT = din('c_maskT', [128, 128])
    c_masknew = din('c_masknew', [4, 32])
    c_maskblk = din('c_maskblk', [64, 64])
    c_csp = din('c_csp', [SEQ, 32])
    c_css = din('c_css', [64, 32])

    y_p = dout('y_p', [SEQ, D]); y_s = dout('y_s', [64, D])
    kv_p = dout('kv_p', [SEQ, 256]); pe_p = dout('pe_p', [SEQ, 32])
    kv_s = dout('kv_s', [64, 256]); pe_s = dout('pe_s', [64, 32]); gv_s = dout('gv_s', [64, 512])

    s1g = dscr('s1g', [D, DFF]); s1u = dscr('s1u', [D, DFF]); s1d = dscr('s1d', [DFF, D])
    s2g = dscr('s2g', [D, DFF]); s2u = dscr('s2u', [D, DFF]); s2d = dscr('s2d', [DFF, D])
    s_in = dscr('s_in', [D, NIN])
    s_pa = dscr('s_pa', [512, D]); s_pb = dscr('s_pb', [512, D]); s_o = dscr('s_o', [D, D])

    PS = [Tl('ps%d' % i, es.enter_context(nc.psum_tensor('ps%d' % i, [128, 512], F32))) for i in range(8)]
    NRING = 4
    RSZ = 4096
    ring = [sb('ring%d' % i, [128, RSZ], BF16) for i in range(NRING)]
    ring_i = [0]

    def ring_next():
        t = ring[ring_i[0] % NRING]
        ring_i[0] += 1
        return t

    identf = sb('identf', [128, 128]); identb = sb('identb', [128, 128], BF16)
    maskTb = sb('maskTb', [128, 128], BF16); maskTf = sb('maskTf', [128, 128])
    masknewf = sb('masknewf', [4, 32]); masknewb = sb('masknewb', [4, 32], BF16)
    maskblk = sb('maskblk', [64, 64])
    onesb = sb('onesb', [128, 1], BF16)
    cst = sb('cst', [128, 32]); css = sb('css', [64, 32])
    xt = sb('xt', [128, 4, D])
    xs = sb('xs', [128, D], BF16)
    junk = xs
    nT = sb('nT', [128, 8, 512], BF16)
    actT = sb('actT', [128, NFC, 512], BF16)
    tmpf = sb('tmpf', [128, 512]); tmpg = sb('tmpg', [128, 512]); tmph = sb('tmph', [128, 512])
    st = sb('st', [128, 16])
    KT = sb('KT', [128, 2, SEQ], BF16); KTpe = sb('KTpe', [32, SEQ], BF16)
    Vc = sb('Vc', [128, 32, 256], BF16)
    modT = sb('modT', [128, 72, 17])
    badaT = sb('badaT', [128, 72]); bada72 = sb('bada72', [72, 128])
    gall_sb = sb('gall_sb', [28, 128]); gT = sb('gT', [128, 28])
    callsb = Tl('callsb', xt.h[0:17, 0, :]); scb = Tl('scb', xs.h[0:17, :]); scT = sb('scT', [128, 8, 17], BF16)
    GT = sb('GT', [128, 3, 8, 16]); GATE = sb('GATE', [128, 3, D])
    GKV = sb('GKV', [128, 256]); GF = sb('GF', [128, D]); LNG = sb('LNG', [128, 512]); LNB = sb('LNB', [128, 512])
    wuq = sb('wuq', [128, 2, 768], BF16); wuqf = Tl('wuqf', actT.h[:, 0:6, :].bitcast(F32).rearrange('p a c -> p (a c)').rearrange('p (k c) -> p k c', k=2))
    wukf = Tl('wukf', actT.h[:, 6:10, :].bitcast(F32).rearrange('p a c -> p (a c)').rearrange('p (k c) -> p k c', k=2)); wukb = Tl('wukb', actT.h[:, 20:22, :])
    wukT = sb('wukT', [128, 8, 256], BF16)
    wuvf = Tl('wuvf', actT.h[:, 10:14, :].bitcast(F32).rearrange('p a c -> p (a c)').rearrange('p (k c) -> p k c', k=2)); wuvpad = sb('wuvpad', [128, 8, 2, 128], BF16)
    wsf = Tl('wsf', actT.h[:, 14:18, :].bitcast(F32).rearrange('p a c -> p (a c)').rearrange('p (g c) -> p g c', g=8)); WS = sb('WS', [128, 8, 128], BF16)
    WSs = sb('WSs', [64, 8, 64], BF16); wssf = Tl('wssf', actT.h[0:64, 18:20, :].bitcast(F32).rearrange('p a c -> p (a c)').rearrange('p (g c) -> p g c', g=8))
    bsf = sb('bsf', [8, 128]); bsT = sb('bsT', [128, 8])
    cq_b = sb('cq_b', [128, 256], BF16); cqnT = sb('cqnT', [128, 2, 128], BF16)
    kvf = sb('kvf', [128, 256]); pef = sb('pef', [128, 32]); peb = sb('peb', [128, 32], BF16)
    kvb_s = sb('kvb_s', [64, 256], BF16); kvT_s = sb('kvT_s', [128, 2, 64], BF16); peT_s = sb('peT_s', [32, 64], BF16)
    qf = sb('qf', [128, 768]); qnb = sb('qnb', [128, 512], BF16); qnT = sb('qnT', [128, 4, 128], BF16)
    qpf = sb('qpf', [128, 8, 32]); qpb = sb('qpb', [128, 8, 32], BF16)
    QT = sb('QT', [128, 2, 1024], BF16); QTpe = sb('QTpe', [32, 1024], BF16)
    uf = sb('uf', [128, 512]); gvf = sb('gvf', [128, 512]); vnf = gvf; yoA = Tl('yoA', uf.h[:, :]); yoB = Tl('yoB', gvf.h[:, :]); vnb = sb('vnb', [128, 512], BF16)
    obb = sb('obb', [128, 512], BF16)
    obT = Tl('obT', actT.h[:, 0:4, :]); oaT = Tl('oaT', actT.h[:, 4:8, :])
    mergedT = Tl('mergedT', actT.h[:, 8:16, :])
    mixviews = [obT, oaT, mergedT]
    PTt = [sb('PT%d' % i, [128, 512], BF16) for i in range(2)]
    olat = sb('olat', [128, 8, 256], BF16)
    olatT = Tl('olatT', actT.h[:, 16:20, :].rearrange('p a c -> p (a c)').rearrange('p (k c) -> p k c', k=2))
    mixviews.append(olatT)
    rden = sb('rden', [128, 8])
    if do_sample:
        LK = [Tl('LK%d' % i, Vc.h[:, 8 * i:8 * i + 8, :]) for i in range(2)]
        LK += [Tl('LK%d' % (2 + i), KT.h[:, 0, 2048 * i:2048 * (i + 1)].rearrange('p (r c) -> p r c', c=256)) for i in range(2)]
        LP = [Tl('LP%d' % i, Vc.h[:, 16 + i, :].rearrange('p (r c) -> p r c', c=32)) for i in range(2)]
        LP += [Tl('LP%d' % (2 + i), KT.h[:, 1, 256 * i:256 * (i + 1)].rearrange('p (r c) -> p r c', c=32)) for i in range(2)]
        KTs = Tl('KTs', Vc.h[:, 18:22, :].rearrange('p a c -> p (a c)').rearrange('p (k t) -> p k t', k=2))
        KTpes = Tl('KTpes', Vc.h[0:32, 22:24, :].rearrange('p a c -> p (a c)'))
        KTs2 = Tl('KTs2', Vc.h[:, 24:28, :].rearrange('p a c -> p (a c)').rearrange('p (k t) -> p k t', k=2))
        KTpes2 = Tl('KTpes2', Vc.h[0:32, 28:30, :].rearrange('p a c -> p (a c)'))
        salias = LK[0:2] + LP[0:2] + [KTs, KTpes, KTs2, KTpes2]
        kalias = LK[2:4] + LP[2:4]
        pt_sb = sb('pt_sb', [128, 16], I32); idx_sb = sb('idx_sb', [128, 16, 16], I32)
        Vn = sb('Vn', [4, 256], BF16); PTn = sb('PTn', [4, 32], BF16)
        olat_s = sb('olat_s', [32, 256], BF16)
        gvout = gvf

    P.dma('sp', identf[:], c_identf[:])
    cp('dve', identb[:], identf[:])
    P.dma('sp', maskTf[:], c_maskT[:])
    cp('dve', maskTb[:], maskTf[:])
    P.dma('sp', masknewf[:], c_masknew[:])
    cp('dve', masknewb[:], masknewf[:])
    P.dma('sp', maskblk[:], c_maskblk[:])
    mset('dve', onesb[:], 1.0)
    P.dma('sp', css[:], c_css[:])
    P.dma('sp', callsb[:], c_all[:])
    P.dma('sp', bada72[:], b_ada[:])
    P.dma('sp', gall_sb[:], gall[:])
    P.dma('sp', GKV[:], g_kv1[:].f(lambda a: a.partition_broadcast(128)))
    P.dma('sp', GF[:], g_final[:].f(lambda a: a.partition_broadcast(128)))
    P.dma('sp', LNG[:], lnv[0, :].f(lambda a: a.partition_broadcast(128)))
    P.dma('sp', LNB[:], lnv[1, :].f(lambda a: a.partition_broadcast(128)))
    P.dma('sp', wuqf[:], w_uq[:].re("(k p) c -> p k c", p=128))
    cp('dve', wuq[:], wuqf[:])
    P.dma('sp', wukf[:], w_uk[:].re("(k p) c -> p k c", p=128))
    cp('dve', wukb[:], wukf[:])
    P.dma('sp', wuvf[:], w_uv[:].re("(k p) c -> p k c", p=128))
    P.dma('sp', wsf[:], w_s[:].re("g t s -> t g s"))
    P.dma('sp', bsf[:], b_s[:])

    mm(PS[0][:, 0:72], bada72[0:72, :], identf[0:72, 0:72])
    cp('dve', badaT[:], PS[0][:, 0:72])
    mm(PS[1][:, 0:28], gall_sb[0:28, :], identf[0:28, 0:28])
    cp('dve', gT[:], PS[1][:, 0:28])
    mm(PS[0][:, 0:8], bsf[0:8, :], identf[0:8, 0:8])
    cp('dve', bsT[:], PS[0][:, 0:8])
    mset('dve', wukT[:], 0.0)
    for pr in range(4):
        for cc in range(2):
            mm(PS[1][:, cc * 128:(cc + 1) * 128], wukb[:, cc, pr * 128:(pr + 1) * 128], identb[:])
        for h2 in range(2):
            cp('dve', wukT[h2 * 64:(h2 + 1) * 64, 2 * pr + h2, :], PS[1][h2 * 64:(h2 + 1) * 64, 0:256])
    mset('dve', wuvpad[:], 0.0)
    for h in range(NH):
        cp('dve', wuvpad[:, h, :, (h % 2) * 64:(h % 2) * 64 + 64], wuvf[:, :, h * 64:(h + 1) * 64])
    for g in range(8):
        mm(PS[0][:, 0:128], wsf[:, g, :], identf[:])
        tt('dve', WS[:, g, :], PS[0][:, 0:128], maskTf[:], ALU.mult)
    if do_sample:
        mset('dve', wssf[:], 0.0)
        for j in range(16):
            for g in range(8):
                P.dma('sp', wssf[4 * j:4 * j + 4, g, 4 * j:4 * j + 4], w_s[g, 0:4, 0:4].re("t s -> s t"),
                      nowait_w=(j > 0 or g > 0), slow=True)
        tt('dve', WSs[:], wssf[:], maskblk[:].us(1).bc([64, 8, 64]), ALU.mult)
        P.dma('sp', pt_sb[:], ptab[:])
        for a in range(16):
            ts('dve', idx_sb[:, :, a], pt_sb[:], 16, a, ALU.mult, ALU.add)

    act(scb[:], callsb[:], AF.Silu)
    for kc in range(8):
        mm(PS[0][:, kc * 17:(kc + 1) * 17], scb[0:17, kc * 128:(kc + 1) * 128], identb[0:17, 0:17])
    cp('dve', scT[:], PS[0][:, 0:136].re("p (k s) -> p k s", s=17))
    wadaT = w_ada[:].re("(k p) c -> p k c", p=128)
    for ch in range(18):
        slot = ring_next()
        sv = slot[:, 0:4096].re("p (k c) -> p k c", c=512)
        P.dma('pool', sv, wadaT[:, :, ch * 512:(ch + 1) * 512])
        if ch == 0 and os.environ.get('KD2', '0') != '0':
            if os.environ.get('KD2') in ('1', '3'):
                dump('scT', scT[:])
            if os.environ.get('KD2') in ('2', '3'):
                dump('wada0', sv)
        for j in range(4):
            J = ch * 4 + j
            jj = J % 24
            for kc in range(8):
                mm(PS[2][:, jj * 17:(jj + 1) * 17], sv[:, kc, j * 128:(j + 1) * 128], scT[:, kc, :],
                   start=(kc == 0), stop=(kc == 7))
            if jj == 23:
                J0 = J - 23
                tt('dve', modT[:, J0:J0 + 24, :], PS[2][:, 0:408].re("p (j s) -> p j s", s=17),
                   badaT[:, J0:J0 + 24].us(2).bc([128, 24, 17]), ALU.add)

    dump('modT', modT[:])
    def cast_w(dst, src, rows):
        for r0 in range(0, rows, 128):
            P.dma('pool', dst[r0:r0 + 128, :], src[r0:r0 + 128, :], nowait_w=True)
    cast_w(s1g, w1g, D); cast_w(s1u, w1u, D); cast_w(s1d, w1d, DFF)
    cast_w(s_in, w_in, D); cast_w(s_pa, w_pa, 512); cast_w(s_pb, w_pb, 512); cast_w(s_o, w_o, D)
    cast_w(s2g, w2g, D); cast_w(s2u, w2u, D); cast_w(s2d, w2d, DFF)

    def wload(src_view, kcn, ncols):
        slot = ring_next()
        sv = slot[:, 0:kcn * ncols].re("p (k c) -> p k c", c=ncols)
        P.dma('sp', sv, src_view)
        return sv

    def wcols(scr, kcn, c0, ncols):
        return wload(scr[:].re("(k p) c -> p k c", p=128)[:, 0:kcn, c0:c0 + ncols], kcn, ncols)

    def setup_pass(kind):
        if kind == 's':
            nseq, c0, nt = 16, 1, 64
        else:
            nseq, c0, nt = 1, 0, 128
        for i in range(3):
            sc = modT[:, (i * 3 + 1) * 8:(i * 3 + 1) * 8 + 8, c0:c0 + nseq]
            ts('dve', GT[:, i, :, 0:nseq], sc, 1.0, None, ALU.add)
            tt('dve', GT[:, i, :, 0:nseq], GT[:, i, :, 0:nseq],
               gT[:, i * 8:(i + 1) * 8].us(2).bc([128, 8, nseq]), ALU.mult)
            for kc in range(8):
                J = (i * 3 + 2) * 8 + kc
                gsrc = modT[:, J, c0:c0 + nseq]
                if kind == 's':
                    cp('dve', tmpf[:, 0:64].re("p (s g) -> p s g", g=4), gsrc.us(2).bc([128, 16, 4]))
                else:
                    cp('dve', tmpf[:, 0:128], gsrc.bc([128, 128]))
                mm(PS[kc // 4][0:nt, (kc % 4) * 128:(kc % 4 + 1) * 128], tmpf[:, 0:nt], identf[:])
            fac = 1.0 if i == 1 else 0.5
            for hf in range(2):
                ts('dve', GATE[0:nt, i, hf * 512:(hf + 1) * 512], PS[hf][0:nt, :], fac, None, ALU.mult)

    def bview(v, kind, nk, nt):
        if kind == 's':
            return v.us(3).bc([128, nk, 16, 4])
        return v.bc([128, nk, nt])

    def dview(v, kind, nk, nt):
        if kind == 's':
            return v.re("p (k s g) -> p k s g", k=nk, g=4)
        return v.re("p (k t) -> p k t", k=nk)

    def norm_to_nT(kind, i, nsub, nt):
        c0, nseq = (1, 16) if kind == 's' else (0, 1)
        for s in range(nsub):
            act(junk[0:nt, :], xt[0:nt, s, :], AF.Square)
            rsum(st[0:nt, 0:1], junk[0:nt, :])
            rsqrt(st[0:nt, 1:2], st[0:nt, 0:1], 1.0 / D)
            ts('dve', xs[0:nt, :], xt[0:nt, s, :], st[0:nt, 1:2], None, ALU.mult)
            dump('st_' + kind, st[0:nt, 0:2])
            dump('xs_' + kind, xs[0:nt, :])
            dump('gT', gT[:])
            dump('GT_' + kind, GT[:])
            for hf in range(2):
                for k4 in range(4):
                    kc = hf * 4 + k4
                    mm(PS[hf][:, k4 * nt:(k4 + 1) * nt], xs[0:nt, kc * 128:(kc + 1) * 128], identb[0:nt, 0:nt])
                g = GT[:, i, hf * 4:hf * 4 + 4, 0:nseq]
                sh = modT[:, (i * 3) * 8 + hf * 4:(i * 3) * 8 + hf * 4 + 4, c0:c0 + nseq]
                tt('dve', dview(tmpf[:, 0:4 * nt], kind, 4, nt), dview(PS[hf][:, 0:4 * nt], kind, 4, nt),
                   bview(g, kind, 4, nt), ALU.mult)
                ov = nT[:, hf * 4:hf * 4 + 4, s * nt:(s + 1) * nt]
                if kind == 's':
                    ov = ov.re("p k (s g) -> p k s g", g=4)
                tt('pool', ov, dview(tmpf[:, 0:4 * nt], kind, 4, nt), bview(sh, kind, 4, nt), ALU.add)

    def ffn(kind, i, sg_, su_, sd_, nsub, nt):
        NT = nsub * nt
        gi = 0 if i == 0 else 2
        P.inherit([actT], mixviews)
        norm_to_nT(kind, gi, nsub, nt)
        dump('GATE_' + kind, GATE[0:nt, :, :])
        dump('nT%d_' % i + kind, nT[:, :, 0:NT])
        fc = 0
        flip = 0
        for c0 in range(0, DFF, 512):
            ncols = min(512, DFF - c0)
            wg = wcols(sg_, 8, c0, ncols)
            wu = wcols(su_, 8, c0, ncols)
            for j in range(ncols // 128):
                pg, pu = (PS[2], PS[3]) if flip == 0 else (PS[4], PS[5])
                flip ^= 1
                for kc in range(8):
                    mm(pg[:, 0:NT], wg[:, kc, j * 128:(j + 1) * 128], nT[:, kc, 0:NT], start=(kc == 0), stop=(kc == 7))
                for kc in range(8):
                    mm(pu[:, 0:NT], wu[:, kc, j * 128:(j + 1) * 128], nT[:, kc, 0:NT], start=(kc == 0), stop=(kc == 7))
                tg = tmpg if flip else tmph
                act(tg[:, 0:NT], pg[:, 0:NT], AF.Silu)
                tt('dve', actT[:, fc, 0:NT], tg[:, 0:NT], pu[:, 0:NT], ALU.mult)
                fc += 1
        dump('actT%d_' % i + kind, actT[:, :, 0:NT])
        halves = [[0, 1, 2, 3]] if nsub == 4 else [[0]]
        pb0 = 0 if nsub == 4 else 4
        for hv in halves:
            for f0 in range(0, NFC, 4):
                nf = min(4, NFC - f0)
                wd = wload(sd_[f0 * 128:(f0 + nf) * 128, :].re("(f p) d -> p f d", p=128), nf, D)
                for fl in range(nf):
                    f = f0 + fl
                    for si, s in enumerate(hv):
                        for dh in range(2):
                            mm(PS[pb0 + si * 2 + dh][0:nt, :], actT[:, f, s * nt:(s + 1) * nt], wd[:, fl, dh * 512:(dh + 1) * 512],
                               start=(f == 0), stop=(f == NFC - 1),
                               sig=(f == NFC - 1) or (fl == nf - 1 and si == len(hv) - 1 and dh == 1))
            for si, s in enumerate(hv):
                for dh in range(2):
                    tt('dve', tmpf[0:nt, :], PS[pb0 + si * 2 + dh][0:nt, :], GATE[0:nt, gi, dh * 512:(dh + 1) * 512], ALU.mult)
                    tt('pool', xt[0:nt, s, dh * 512:(dh + 1) * 512], xt[0:nt, s, dh * 512:(dh + 1) * 512], tmpf[0:nt, :], ALU.add)

    def dumpx(name, kind, nsub, nt):
        dump(name + '_' + kind, xt[0:nt, 0:nsub, :])

    def transpose_cols(dst, src, nt, nblk, psb):
        for b in range(nblk):
            mm(psb[:, b * nt:(b + 1) * nt], src[0:nt, b * 128:(b + 1) * 128], identb[0:nt, 0:nt])

    def mixer(kind, tile_i, nsub, nt):
        import os
        STOP = float(os.environ.get('KSTOP', '9'))
        NT = nsub * nt
        c0, nseq = (1, 16) if kind == 's' else (0, 1)
        norm_to_nT(kind, 1, nsub, nt)
        dump('nTm_' + kind, nT[:, :, 0:NT])
        P.inherit(mixviews, [actT])
        P.inherit([uf, gvf], [yoA, yoB])
        wA = wcols(s_in, 8, 0, 512)
        wA2 = wcols(s_in, 8, 512, 32)
        wB = wcols(s_in, 8, 544, 512)
        wC = wcols(s_in, 8, 1056, 512)
        for s in range(nsub):
            gq = tile_i * 4 + s
            ncol = slice(s * nt, (s + 1) * nt)
            for kc in range(8):
                mm(PS[0][0:nt, :], nT[:, kc, ncol], wA[:, kc, 0:512], start=(kc == 0), stop=(kc == 7))
            for kc in range(8):
                mm(PS[1][0:nt, 0:32], nT[:, kc, ncol], wA2[:, kc, 0:32], start=(kc == 0), stop=(kc == 7))
            act(junk[0:nt, 0:256], PS[0][0:nt, 0:256], AF.Square)
            rsum(st[0:nt, 2:3], junk[0:nt, 0:256])
            rsqrt(st[0:nt, 3:4], st[0:nt, 2:3], 1.0 / 256)
            ts('dve', cq_b[0:nt, :], PS[0][0:nt, 0:256], st[0:nt, 3:4], None, ALU.mult)
            act(junk[0:nt, 256:512], PS[0][0:nt, 256:512], AF.Square)
            rsum(st[0:nt, 4:5], junk[0:nt, 256:512])
            rsqrt(st[0:nt, 5:6], st[0:nt, 4:5], 1.0 / 256)
            stt('dve', kvf[0:nt, :], PS[0][0:nt, 256:512], st[0:nt, 5:6], GKV[0:nt, :], ALU.mult, ALU.mult)
            if STOP <= 1.01:
                return
            if kind == 's':
                cs = css[0:64, :]
            else:
                P.dma('sp', cst[:], c_csp[gq * 128:(gq + 1) * 128, :])
                cs = cst[:, :]
            x1 = PS[1][0:nt, 0:16]; x2 = PS[1][0:nt, 16:32]
            tt('dve', tmph[0:nt, 0:16], x1, cs[:, 0:16], ALU.mult)
            tt('dve', tmph[0:nt, 16:32], x2, cs[:, 16:32], ALU.mult)
            tt('dve', pef[0:nt, 0:16], tmph[0:nt, 0:16], tmph[0:nt, 16:32], ALU.subtract)
            tt('dve', tmph[0:nt, 32:48], x1, cs[:, 16:32], ALU.mult)
            tt('dve', tmph[0:nt, 48:64], x2, cs[:, 0:16], ALU.mult)
            tt('dve', pef[0:nt, 16:32], tmph[0:nt, 32:48], tmph[0:nt, 48:64], ALU.add)
            cp('dve', peb[0:nt, :], pef[0:nt, :])
            if kind == 's':
                P.dma('sp', kv_s[:], kvf[0:64, :], owner=kvf, nowait_w=True)
                P.dma('sp', pe_s[:], pef[0:64, :], owner=pef, nowait_w=True)
                cp('dve', kvb_s[:], kvf[0:64, :])
                kvsrc = kvb_s
            else:
                P.dma('sp', kv_p[gq * 128:(gq + 1) * 128, :], kvf[:], owner=kvf, nowait_w=True)
                P.dma('sp', pe_p[gq * 128:(gq + 1) * 128, :], pef[:], owner=pef, nowait_w=True)
                if STOP <= 1.015:
                    return
                cp('dve', Vc[:, gq, :], kvf[:])
                kvsrc = None
            if STOP <= 1.02:
                return
            for cc in range(2):
                mm(PS[2][:, cc * nt:(cc + 1) * nt], cq_b[0:nt, cc * 128:(cc + 1) * 128], identb[0:nt, 0:nt])
            for cc in range(2):
                ts('dve', cqnT[:, cc, 0:nt], PS[2][:, cc * nt:(cc + 1) * nt], gT[:, 24 + cc:25 + cc], None, ALU.mult)
            for cc in range(2):
                if kind == 's':
                    mm(PS[3][:, cc * nt:(cc + 1) * nt], kvb_s[0:nt, cc * 128:(cc + 1) * 128], identb[0:nt, 0:nt])
                else:
                    mm(PS[3][:, cc * nt:(cc + 1) * nt], Vc[:, gq, cc * 128:(cc + 1) * 128], identb[:])
            mm(PS[3][0:32, 2 * nt:3 * nt], peb[0:nt, :], identb[0:nt, 0:nt])
            if kind == 's':
                cp('dve', kvT_s[:, :, :], PS[3][:, 0:2 * nt].re("p (c t) -> p c t", c=2))
                cp('dve', peT_s[:, :], PS[3][0:32, 2 * nt:3 * nt])
            else:
                cp('dve', KT[:, :, gq * 128:(gq + 1) * 128], PS[3][:, 0:256].re("p (c t) -> p c t", c=2))
                cp('dve', KTpe[:, gq * 128:(gq + 1) * 128], PS[3][0:32, 256:384])
            if STOP <= 1.1:
                return
            for cc in range(2):
                mm(PS[0][0:nt, :], cqnT[:, cc, 0:nt], wuq[:, cc, 0:512], start=(cc == 0), stop=(cc == 1))
            for cc in range(2):
                mm(PS[1][0:nt, 0:256], cqnT[:, cc, 0:nt], wuq[:, cc, 512:768], start=(cc == 0), stop=(cc == 1))
            cp('act', qf[0:nt, 0:512], PS[0][0:nt, :])
            cp('dve', qf[0:nt, 512:768], PS[1][0:nt, 0:256])
            q3 = qf[0:nt, :].re("p (h e) -> p h e", e=96)
            cp('dve', qnb[0:nt, :].re("p (h e) -> p h e", e=64), q3[:, :, 0:64])
            q1 = q3[:, :, 64:80]; q2 = q3[:, :, 80:96]
            cosb = cs[:, 0:16].us(1).bc([nt, 8, 16]); sinb = cs[:, 16:32].us(1).bc([nt, 8, 16])
            ta = tmpg[0:nt, 0:128].re("p (h e) -> p h e", e=16); tb = tmpg[0:nt, 128:256].re("p (h e) -> p h e", e=16)
            tt('dve', ta, q1, cosb, ALU.mult)
            tt('dve', tb, q2, sinb, ALU.mult)
            tt('dve', qpf[0:nt, :, 0:16], ta, tb, ALU.subtract)
            tt('dve', ta, q1, sinb, ALU.mult)
            tt('dve', tb, q2, cosb, ALU.mult)
            tt('dve', qpf[0:nt, :, 16:32], ta, tb, ALU.add)
            cp('dve', qpb[0:nt, :, :], qpf[0:nt, :, :])
            if STOP <= 1.2:
                return
            transpose_cols(None, qnb, nt, 4, PS[2])
            cp('dve', qnT[:, :, 0:nt], PS[2][:, 0:4 * nt].re("p (b t) -> p b t", b=4))
            if STOP <= 1.25:
                return
            for h in range(NH):
                mm(PS[3][0:32, h * nt:(h + 1) * nt] if nt * 8 <= 512 else PS[3 + h // 4][0:32, (h % 4) * nt:(h % 4 + 1) * nt],
                   qpb[0:nt, h, :], identb[0:nt, 0:nt])
            if nt * 8 <= 512:
                cp('act', QTpe[:, 0:8 * nt], PS[3][0:32, 0:8 * nt])
            else:
                cp('act', QTpe[:, 0:512], PS[3][0:32, :])
                cp('act', QTpe[:, 512:1024], PS[4][0:32, :])
            if STOP <= 1.27:
                return
            for cc in range(2):
                for h in range(NH):
                    pr, h2 = h // 2, h % 2
                    if nt * 8 <= 512:
                        o = PS[5 + cc][:, h * nt:(h + 1) * nt]
                    else:
                        o = PS[5 + cc][:, (h % 4) * nt:(h % 4 + 1) * nt] if h < 4 else None
                    if o is None:
                        continue
                    mm(o, wukT[:, h, cc * 128:(cc + 1) * 128], qnT[:, pr, 0:nt])
                n1 = min(8 * nt, 512)
                cp('act' if cc == 0 else 'dve', QT[:, cc, 0:n1], PS[5 + cc][:, 0:n1])
            if nt * 8 > 512:
                for cc in range(2):
                    for h in range(4, NH):
                        pr, h2 = h // 2, h % 2
                        mm(PS[5 + cc][:, (h % 4) * nt:(h % 4 + 1) * nt], wukT[:, h, cc * 128:(cc + 1) * 128],
                           qnT[:, pr, 0:nt])
                    cp('act' if cc == 0 else 'dve', QT[:, cc, 512:1024], PS[5 + cc][:, :])
            if STOP <= 1.3:
                return
            if os.environ.get('KD2', '0') == '4':
                dump('QT_' + kind, QT[:, :, 0:8 * nt])
            if os.environ.get('KD2', '0') == '5':
                dump('QTpe_' + kind, QTpe[:, 0:8 * nt])
            for kc in range(8):
                mm(PS[0][0:nt, :], nT[:, kc, ncol], wB[:, kc, :], start=(kc == 0), stop=(kc == 7))
            for kc in range(8):
                mm(PS[1][0:nt, :], nT[:, kc, ncol], wC[:, kc, :], start=(kc == 0), stop=(kc == 7))
            act(uf[0:nt, :], PS[0][0:nt, :], AF.Gelu_apprx_tanh)
            act(gvf[0:nt, :], PS[1][0:nt, :], AF.Gelu_apprx_tanh)
            rsum(st[0:nt, 6:7], gvf[0:nt, :])
            act(junk[0:nt, 0:512], gvf[0:nt, :], AF.Square)
            rsum(st[0:nt, 7:8], junk[0:nt, 0:512])
            ts('dve', st[0:nt, 8:9], st[0:nt, 6:7], 1.0 / 512, None, ALU.mult)
            ts('dve', st[0:nt, 9:10], st[0:nt, 7:8], 1.0 / 512, EPS, ALU.mult, ALU.add)
            tt('dve', st[0:nt, 10:11], st[0:nt, 8:9], st[0:nt, 8:9], ALU.mult)
            tt('dve', st[0:nt, 9:10], st[0:nt, 9:10], st[0:nt, 10:11], ALU.subtract)
            act(st[0:nt, 9:10], st[0:nt, 9:10], AF.Sqrt)
            recip(st[0:nt, 9:10], st[0:nt, 9:10])
            ts('dve', vnf[0:nt, :], gvf[0:nt, :], st[0:nt, 8:9], st[0:nt, 9:10], ALU.subtract, ALU.mult)
            tt('dve', vnf[0:nt, :], vnf[0:nt, :], LNG[0:nt, :], ALU.mult)
            if kind == 's':
                tt('dve', gvout[0:64, :], vnf[0:64, :], LNB[0:64, :], ALU.add)
                P.dma('sp', gv_s[:], gvout[0:64, :], owner=gvout, nowait_w=True)
                cp('dve', vnb[0:nt, :], gvout[0:64, :])
            else:
                tt('pool', vnb[0:nt, :], vnf[0:nt, :], LNB[0:nt, :], ALU.add)
            for g in range(8):
                wsm = WSs[0:64, g, :] if kind == 's' else WS[:, g, :]
                mm(PS[2][0:nt, g * 64:(g + 1) * 64], wsm, vnb[0:nt, g * 64:(g + 1) * 64])
            bsv = bsrep[0:64, :] if kind == 's' else bsT[0:nt, :]
            tt('dve', tmph[0:nt, :].re("p (g e) -> p g e", e=64), PS[2][0:nt, :].re("p (g e) -> p g e", e=64),
               bsv.us(2).bc([nt, 8, 64]), ALU.add)
            tt('dve', obb[0:nt, :], tmph[0:nt, :], uf[0:nt, :], ALU.mult)
            transpose_cols(None, obb, nt, 4, PS[3])
            cp('act', obT[:, :, ncol], PS[3][:, 0:4 * nt].re("p (b t) -> p b t", b=4))
            if STOP <= 1.4:
                return
            if kind == 'p':
                attend_prompt(gq)
                nq = 128
            if STOP <= 1.5:
                return
            if kind == 'p':
                dump('olat_p', olat[:])
            if kind == 'p':
                for cc in range(2):
                    for hb in range(2):
                        for h4 in range(4):
                            h = hb * 4 + h4
                            mm(PS[hb][:, h4 * 128:(h4 + 1) * 128], olat[:, h, cc * 128:(cc + 1) * 128], identb[:])
                        cp('act' if hb == 0 else 'dve', olatT[:, cc, hb * 512:(hb + 1) * 512], PS[hb][:, :])
                for pr in range(4):
                    k = 0
                    for h in (2 * pr, 2 * pr + 1):
                        for cc in range(2):
                            mm(PS[2][:, pr * 128:(pr + 1) * 128], wuvpad[:, h, cc, :], olatT[:, cc, h * 128:(h + 1) * 128],
                               start=(k == 0), stop=(k == 3))
                            k += 1
                cp('dve', oaT[:, :, ncol], PS[2][:, :].re("p (b t) -> p b t", b=4))
        if kind == 's':
            attend_sample()
            for pr in range(4):
                k = 0
                for h in (2 * pr, 2 * pr + 1):
                    for cc in range(2):
                        rv = olatT[:, cc, 0:512].re("p (s h q) -> p s h q", h=8, q=4)[:, :, h, :]
                        mm(PS[2][:, pr * 64:(pr + 1) * 64], wuvpad[:, h, cc, :], rv, start=(k == 0), stop=(k == 3))
                        k += 1
            cp('dve', oaT[:, :, 0:64], PS[2][:, 0:256].re("p (b t) -> p b t", b=4))
        if STOP <= 1.6:
            return
        dump('oaT_' + kind, oaT[:, :, 0:NT])
        dump('obT_' + kind, obT[:, :, 0:NT])
        for dq in range(2):
            wpa = wcols(s_pa, 4, dq * 512, 512)
            wpb = wcols(s_pb, 4, dq * 512, 512)
            wga = wcols(s_in, 8, 1568 + dq * 512, 512)
            wgb = wcols(s_in, 8, 2592 + dq * 512, 512)
            for d4 in range(4):
                dc = dq * 4 + d4
                for kc in range(4):
                    mm(PS[0][:, 0:NT], wpa[:, kc, d4 * 128:(d4 + 1) * 128], oaT[:, kc, 0:NT], start=(kc == 0), stop=(kc == 3))
                for kc in range(8):
                    mm(PS[1][:, 0:NT], wga[:, kc, d4 * 128:(d4 + 1) * 128], nT[:, kc, 0:NT], start=(kc == 0), stop=(kc == 7))
                for kc in range(4):
                    mm(PS[2][:, 0:NT], wpb[:, kc, d4 * 128:(d4 + 1) * 128], obT[:, kc, 0:NT], start=(kc == 0), stop=(kc == 3))
                for kc in range(8):
                    mm(PS[3][:, 0:NT], wgb[:, kc, d4 * 128:(d4 + 1) * 128], nT[:, kc, 0:NT], start=(kc == 0), stop=(kc == 7))
                act(tmpg[:, 0:NT], PS[1][:, 0:NT], AF.Sigmoid)
                tt('dve', tmpg[:, 0:NT], tmpg[:, 0:NT], PS[0][:, 0:NT], ALU.mult)
                act(tmph[:, 0:NT], PS[3][:, 0:NT], AF.Sigmoid)
                tt('dve', tmph[:, 0:NT], tmph[:, 0:NT], PS[2][:, 0:NT], ALU.mult)
                tt('pool', mergedT[:, dc, 0:NT], tmpg[:, 0:NT], tmph[:, 0:NT], ALU.add)
        if STOP <= 1.7:
            return
        dump('mergedT_' + kind, mergedT[:, :, 0:NT])
        for dh in range(2):
            wo = wcols(s_o, 8, dh * 512, 512)
            for s in range(nsub):
                pb = PS[4 + (s % 2)]
                for kc in range(8):
                    mm(pb[0:nt, :], mergedT[:, kc, s * nt:(s + 1) * nt], wo[:, kc, :], start=(kc == 0), stop=(kc == 7))
                tt('dve', tmpf[0:nt, :], pb[0:nt, :], GATE[0:nt, 1, dh * 512:(dh + 1) * 512], ALU.mult)
                tt('pool', xt[0:nt, s, dh * 512:(dh + 1) * 512], xt[0:nt, s, dh * 512:(dh + 1) * 512], tmpf[0:nt, :], ALU.add)

    def attend_prompt(gq):
        for ch in range(2):
            cols = slice(ch * 512, (ch + 1) * 512)
            for kj in range(gq + 1):
                pS = PS[2 + (kj % 2)]
                kc_ = slice(kj * 128, (kj + 1) * 128)
                mm(pS[:, :], KT[:, 0, kc_], QT[:, 0, cols], start=True, stop=False)
                mm(pS[:, :], KT[:, 1, kc_], QT[:, 1, cols], start=False, stop=False)
                mm(pS[:, :], KTpe[0:32, kc_], QTpe[0:32, cols], start=False, stop=True)
                pt_ = PTt[kj % 2]
                act(pt_[:, :], pS[:, :], AF.Exp, scale=SM_SCALE)
                if kj == gq:
                    tt('pool', pt_[:, :].re("p (h q) -> p h q", q=128), pt_[:, :].re("p (h q) -> p h q", q=128),
                       maskTb[:, :].us(1).bc([128, 4, 128]), ALU.mult)
                for hh in range(4):
                    mm(PS[4 + hh // 2][:, (hh % 2) * 256:(hh % 2 + 1) * 256], pt_[:, hh * 128:(hh + 1) * 128], Vc[:, kj, :],
                       start=(kj == 0 and hh % 2 == 0), stop=(kj == gq), skip=True)
                    mm(PS[6][:, hh:hh + 1], pt_[:, hh * 128:(hh + 1) * 128], onesb[:, 0:1],
                       start=(kj == 0 and hh == 0), stop=(kj == gq), skip=True)
            P.op('dve', (lambda a, b: (lambda e: e.reciprocal(out=a, in_=b)))(rden[:, ch * 4:ch * 4 + 4].ap, PS[6][:, 0:4].ap),
                 [PS[6]], [rden])
            for hh in range(4):
                h = ch * 4 + hh
                ts('dve', olat[:, h, :], PS[4 + hh // 2][:, (hh % 2) * 256:(hh % 2 + 1) * 256], rden[:, h:h + 1], None, ALU.mult)

    def attend_sample():
        ckv = cache_kv[:].re("n (a r) c -> (n a) (r c)", r=8).ap
        cpe = cache_pe[:].re("n (a r) c -> (n a) (r c)", r=8).ap
        NR = 16 * 16
        psT = [(PS[0], PS[1], PS[7]), (PS[5], PS[6], PS[3])]
        KTset = [(KTs, KTpes), (KTs2, KTpes2)]

        def gather(R):
            sq, rb = divmod(R, 16)
            lk = LK[R % 4]; lp = LP[R % 4]
            ia = idx_sb[:, sq, rb:rb + 1].ap
            lka = lk[:].re("p r c -> p (r c)").ap; lpa = lp[:].re("p r c -> p (r c)").ap
            P.dma('pool', lk[:], cache_kv[:], extra=[idx_sb], fn=(lambda o, i_, x: (lambda e: e.indirect_dma_start(
                out=o, out_offset=None, in_=i_, in_offset=bass.IndirectOffsetOnAxis(ap=x, axis=0))))(lka, ckv, ia))
            P.dma('pool', lp[:], cache_pe[:], extra=[idx_sb], fn=(lambda o, i_, x: (lambda e: e.indirect_dma_start(
                out=o, out_offset=None, in_=i_, in_offset=bass.IndirectOffsetOnAxis(ap=x, axis=0))))(lpa, cpe, ia))

        def prep_q(sq):
            q = qs2[sq % 2]
            for cc in range(2):
                cp('dve', q[:, cc, :].re("p (h q) -> p h q", q=4),
                   QT[:, cc, 0:512].re("p (h s q) -> p s h q", h=8, q=4)[:, sq, :, :])
            cp('dve', q[0:32, 2, :].re("p (h q) -> p h q", q=4),
               QTpe[0:32, 0:512].re("p (h s q) -> p s h q", h=8, q=4)[:, sq, :, :])

        def T(R, r4, par):
            lk = LK[R % 4]; lp = LP[R % 4]
            p0, p1, pp = psT[par]
            for rr in range(4):
                r = r4 * 4 + rr
                mm(p0[:, rr * 128:(rr + 1) * 128], lk[:, r, 0:128], identb[:])
                mm(p1[:, rr * 128:(rr + 1) * 128], lk[:, r, 128:256], identb[:])
                mm(pp[0:32, rr * 128:(rr + 1) * 128], lp[:, r, :], identb[:])
            kt, ktp = KTset[par]
            cp('act', kt[:, 0, :], p0[:, :])
            cp('dve', kt[:, 1, :], p1[:, :])
            cp('act', ktp[:, :], pp[0:32, :])

        def S(R, r4, par, q):
            kt, ktp = KTset[par]
            base = (R % 2) * 256
            for rr in range(4):
                r = r4 * 4 + rr
                so = PS[2][:, base + r * 32:base + (r + 1) * 32]
                mm(so, kt[:, 0, rr * 128:(rr + 1) * 128], q[:, 0, :], start=True, stop=False)
                mm(so, kt[:, 1, rr * 128:(rr + 1) * 128], q[:, 1, :], start=False, stop=False)
                mm(so, ktp[0:32, rr * 128:(rr + 1) * 128], q[0:32, 2, :], start=False, stop=True)

        def X(R):
            base = (R % 2) * 256
            act(PTt[R % 2][:, 0:256], PS[2][:, base:base + 256], AF.Exp, scale=SM_SCALE)

        def PV(R, first):
            lk = LK[R % 4]; pt_ = PTt[R % 2]
            for r in range(8):
                mm(PS[4][0:32, 0:256], pt_[:, r * 32:(r + 1) * 32], lk[:, r, :], start=(first and r == 0), stop=False, skip=True)
                mm(PS[4][0:32, 256:257], pt_[:, r * 32:(r + 1) * 32], onesb[:, 0:1], start=False, stop=False, skip=True)

        def epilogue(sq):
            q = qs2[sq % 2]
            tcol = slice(sq * 4, sq * 4 + 4)
            mm(PS[0][0:4, 0:32], kvT_s[:, 0, tcol], q[:, 0, :], start=True, stop=False)
            mm(PS[0][0:4, 0:32], kvT_s[:, 1, tcol], q[:, 1, :], start=False, stop=False)
            mm(PS[0][0:4, 0:32], peT_s[0:32, tcol], q[0:32, 2, :], start=False, stop=True)
            act(PTn[:, :], PS[0][0:4, 0:32], AF.Exp, scale=SM_SCALE)
            tt('dve', PTn[:, :], PTn[:, :], masknewb[:, :], ALU.mult)
            mm(PS[0][0:4, 256:512], identb[0:64, tcol], kvb_s[0:64, :])
            cp('dve', Vn[:, :], PS[0][0:4, 256:512])
            mm(PS[4][0:32, 0:256], PTn[0:4, :], Vn[0:4, :], start=False, stop=True, skip=True)
            mm(PS[4][0:32, 256:257], PTn[0:4, :], onesb[0:4, 0:1], start=False, stop=True, skip=True)
            recip(rden[0:32, 0:1], PS[4][0:32, 256:257])
            ts('dve', olat_s[:, :], PS[4][0:32, 0:256], rden[0:32, 0:1], None, ALU.mult)
            for cc in range(2):
                mm(PS[1][:, cc * 32:(cc + 1) * 32], olat_s[0:32, cc * 128:(cc + 1) * 128], identb[0:32, 0:32])
            cp('dve', olatT[:, :, sq * 32:(sq + 1) * 32], PS[1][:, 0:64].re("p (c t) -> p c t", c=2))

        for R in range(3):
            gather(R)
        prep_q(0)
        T(0, 0, 0)
        for R in range(NR):
            sq, rb = divmod(R, 16)
            if rb == 0 and sq + 1 < 16:
                prep_q(sq + 1)
            T(R, 1, 1)
            S(R, 0, 0, qs2[sq % 2])
            if R + 1 < NR:
                T(R + 1, 0, 0)
            S(R, 1, 1, qs2[sq % 2])
            X(R)
            if rb >= 1:
                PV(R - 1, first=(rb == 1))
            if rb == 15:
                PV(R, first=False)
                epilogue(sq)
            if R + 3 < NR:
                gather(R + 3)

    if do_sample:
        qs2 = [sb('qs_sb%d' % i, [128, 3, 32], BF16) for i in range(2)]
        bsrep = sb('bsrep', [64, 8])
        for j in range(16):
            P.dma('sp', bsrep[4 * j:4 * j + 4, :], b_s[:, 0:4].re("g t -> t g"), nowait_w=True, slow=True)

    def final_out(kind, tile_i, nsub, nt):
        P.inherit([yoA, yoB], [uf, gvf])
        for s in range(nsub):
            act(junk[0:nt, :], xt[0:nt, s, :], AF.Square)
            rsum(st[0:nt, 12:13], junk[0:nt, :])
            rsqrt(st[0:nt, 13:14], st[0:nt, 12:13], 1.0 / D)
            for dh, yo in enumerate((yoA, yoB)):
                dsl = slice(dh * 512, (dh + 1) * 512)
                stt('dve', yo[0:nt, :], xt[0:nt, s, dsl], st[0:nt, 13:14], GF[0:nt, dsl], ALU.mult, ALU.mult)
                if kind == 's':
                    P.dma('sp', y_s[:, dsl], yo[0:64, :], owner=yo, nowait_w=True)
                else:
                    r0 = tile_i * 512 + s * 128
                    P.dma('sp', y_p[r0:r0 + 128, dsl], yo[:], owner=yo, nowait_w=True)

    def run_pass(kind, tile_i):
        if kind == 's':
            nsub, nt = 1, 64
            P.dma('sp', xt[0:64, 0, :], x_s[:])
        else:
            nsub, nt = 4, 128
            P.dma('sp', xt[:], x_p[tile_i * 512:(tile_i + 1) * 512, :].re("(s p) d -> p s d", p=128))
        import os
        STOP = float(os.environ.get('KSTOP', '9'))
        ffn(kind, 0, s1g, s1u, s1d, nsub, nt)
        dumpx('x1', kind, nsub, nt)
        if STOP <= 1:
            return
        mixer(kind, tile_i, nsub, nt)
        dumpx('x2', kind, nsub, nt)
        if STOP <= 2:
            return
        ffn(kind, 2, s2g, s2u, s2d, nsub, nt)
        dumpx('x3', kind, nsub, nt)
        final_out(kind, tile_i, nsub, nt)

    P.inherit([actT], [wuqf, wukf, wuvf, wsf, wukb] + ([wssf] if do_sample else []))
    P.inherit([xt], [callsb])
    P.inherit([xs], [scb])
    if do_sample:
        setup_pass('s')
        run_pass('s', 0)
        P.inherit([Vc], salias)
        P.inherit([KT], kalias)
    if n_tiles > 0:
        setup_pass('p')
        for ti in range(n_tiles):
            run_pass('p', ti)

    P.finish()
    print('ops', {e: len(v) for e, v in P.ops.items()}, 'cnt', P.cnt, 'maxd', max([t.dcnt for t in P.dtiles] + [0]), 'nd', len(P.dtiles))
    P.emit()
    es.close()
    return nc


def _consts():
    identf = np.eye(128, dtype=np.float32)
    maskT = (np.arange(128)[None, :] >= np.arange(128)[:, None]).astype(np.float32)
    masknew = np.zeros((4, 32), np.float32)
    for j in range(4):
        for h in range(8):
            for q in range(4):
                masknew[j, h * 4 + q] = 1.0 if j <= q else 0.0
    s_ = np.arange(64)
    maskblk = ((s_[:, None] // 4 == s_[None, :] // 4) & (s_[None, :] % 4 >= s_[:, None] % 4)).astype(np.float32)
    half = 16
    freqs = (np.float32(10000.0) ** (np.float32(-2.0) * np.arange(half, dtype=np.float32) / np.float32(32))).astype(np.float32)

    def tab(pos):
        ang = pos.astype(np.float32)[:, None] * freqs[None, :]
        return np.concatenate([np.cos(ang), np.sin(ang)], axis=1).astype(np.float32)
    csp = tab(np.arange(SEQ))
    css = tab(PAST + (np.arange(64) % 4))
    return dict(c_identf=identf, c_maskT=maskT, c_masknew=masknew, c_maskblk=maskblk, c_csp=csp, c_css=css)


def make_in_maps(inputs, cores, n_pool=None, cache_override=None):
    f = lambda a: np.ascontiguousarray(np.asarray(a))
    I = {k: np.asarray(v) for k, v in inputs.items()}
    consts = _consts()
    shared = dict(
        w_ada=f(I['w_ada'][0]), b_ada=f(I['b_ada'][0].reshape(72, 128)),
        gall=f(np.concatenate([I['g_ffn1'][0].reshape(8, 128), I['g_mix'][0].reshape(8, 128), I['g_ffn2'][0].reshape(8, 128),
                               I['g_q'][0].reshape(2, 128), I['g_kv'][0].reshape(2, 128)], axis=0)),
        w1g=f(I['w1_gate'][0]), w1u=f(I['w1_up'][0]), w1d=f(I['w1_down'][0]),
        w2g=f(I['w2_gate'][0]), w2u=f(I['w2_up'][0]), w2d=f(I['w2_down'][0]),
        w_in=f(I['w_in'][0]), w_uq=f(I['w_uq'][0]),
        w_uk=f(I['w_uk'][0].reshape(256, 512)), w_uv=f(I['w_uv'][0].reshape(256, 512)),
        lnv=f(np.stack([I['ln_v_g'][0], I['ln_v_b'][0]])), w_s=f(I['w_s'][0]), b_s=f(I['b_s'][0]),
        w_pa=f(I['w_pa'][0]), w_pb=f(I['w_pb'][0]), w_o=f(I['w_o'][0]),
        g_kv1=f(I['g_kv'][0]), g_final=f(I['g_final']), **consts)
    maps = []
    for c in cores:
        m = dict(shared)
        m['x_p'] = f(I['x_prompt'][c])
        m['x_s'] = f(I['x_sample'][16 * c:16 * c + 16].reshape(64, D))
        m['c_all'] = f(np.concatenate([I['c_prompt'][c:c + 1], I['c_sample'][16 * c:16 * c + 16]], axis=0))
        if cache_override is not None:
            m['cache_kv'], m['cache_pe'], m['ptab'] = cache_override[c]
        else:
            m['cache_kv'] = I['cache_kv'][0]
            m['cache_pe'] = I['cache_pe'][0]
            m['ptab'] = f(I['page_table'][16 * c:16 * c + 16].astype(np.int32).T)
        maps.append(m)
    return maps


_NC = {}


def kernel(**inputs):
    n_pool = int(np.asarray(inputs['cache_kv']).shape[1])
    if n_pool not in _NC:
        _NC[n_pool] = build(n_pool=n_pool)
    nc = _NC[n_pool]
    maps = make_in_maps(inputs, list(range(8)))
    res = run_bass_kernel_spmd(nc, maps, core_ids=list(range(8)))
    R = res.results
    y_prompt = np.stack([R[c]['y_p'] for c in range(8)]).astype(np.float32)
    y_sample = np.concatenate([R[c]['y_s'].reshape(16, 4, D) for c in range(8)]).astype(np.float32)
    kv_prompt = np.stack([R[c]['kv_p'] for c in range(8)])[None].astype(np.float32)
    pe_prompt = np.stack([R[c]['pe_p'] for c in range(8)])[None].astype(np.float32)
    kv_sample = np.concatenate([R[c]['kv_s'].reshape(16, 4, 256) for c in range(8)])[None].astype(np.float32)
    pe_sample = np.concatenate([R[c]['pe_s'].reshape(16, 4, 32) for c in range(8)])[None].astype(np.float32)
    gv_sample = np.concatenate([R[c]['gv_s'].reshape(16, 4, 512) for c in range(8)])[None].astype(np.float32)
    return (y_prompt, y_sample, kv_prompt, pe_prompt, kv_sample, pe_sample, gv_sample)
```

```python
import numpy as np
from contextlib import ExitStack
import concourse.bass as bass
import concourse.mybir as mybir
from concourse.bass_utils import run_bass_kernel_spmd

F32, BF16, I32 = mybir.dt.float32, mybir.dt.bfloat16, mybir.dt.int32
AF = mybir.ActivationFunctionType
ALU = mybir.AluOpType

D = 1024
DFF = 2816
NFC = 22
NIN = 3616
SEQ = 4096
NH = 8
EPS = 1e-6
SM_SCALE = 96 ** -0.5
PAST = 16384
ENGS = ['pe', 'act', 'dve', 'pool', 'sp']


class Tl:
    def __init__(s, name, h):
        s.name, s.h = name, h
        s.w = None
        s.r = {}
        s.pre = {}
        s.dsem = None
        s.dcnt = 0

    def __getitem__(s, k):
        return V(s, s.h[k])


class V:
    def __init__(s, t, ap):
        s.t, s.ap = t, ap

    def __getitem__(s, k):
        return V(s.t, s.ap[k])

    def f(s, fn):
        return V(s.t, fn(s.ap))

    def re(self, pat, **kw):
        return V(self.t, self.ap.rearrange(pat, **kw))

    def bc(s, shape):
        return V(s.t, s.ap.broadcast_to(list(shape)))

    def us(s, ax):
        return V(s.t, s.ap.unsqueeze(ax))


class Prog:
    def __init__(s, nc, es):
        s.nc, s.es = nc, es
        s.sem = {e: es.enter_context(nc.semaphore('s_' + e)) for e in ENGS}
        s.cnt = {e: 0 for e in ENGS}
        s.seen = {e: {} for e in ENGS}
        s.ops = {e: [] for e in ENGS}
        s.dtiles = []

    def _waits(s, eng, reads, writes, nowait_w=False):
        need = {}

        def add(ev):
            sm, val = ev
            if sm is s.sem[eng] and eng in ('pe', 'sp'):
                return
            k = id(sm)
            if k not in need or need[k][1] < val:
                need[k] = (sm, val)
        for t in reads:
            if t.w is not None:
                add(t.w)
            for ev in t.pre.values():
                add(ev)
        for t in writes:
            if t.w is not None and not nowait_w:
                add(t.w)
            for ev in t.r.values():
                add(ev)
            for ev in t.pre.values():
                add(ev)
        out = []
        for k, (sm, val) in need.items():
            if s.seen[eng].get(k, 0) < val:
                s.seen[eng][k] = val
                out.append((sm, val))
        return out

    @staticmethod
    def _addr(t, ev):
        k = id(ev[0])
        if k not in t.r or t.r[k][1] < ev[1]:
            t.r[k] = ev

    def op(s, eng, fn, reads, writes, sig=True):
        reads = list({id(t): t for t in reads}.values())
        writes = list({id(t): t for t in writes}.values())
        waits = s._waits(eng, reads, writes)
        if sig:
            s.cnt[eng] += 1
            ev = (s.sem[eng], s.cnt[eng])
        else:
            ev = (s.sem[eng], s.cnt[eng] + 1)
        s.ops[eng].append((waits, fn, s.sem[eng] if sig else None, 1))
        for t in writes:
            t.w = ev
            t.r = {}
            t.pre = {}
        for t in reads:
            s._addr(t, ev)

    def dma(s, q, out, in_, nowait_w=False, owner=None, fn=None, extra=(), slow=False):
        t = owner if owner is not None else out.t
        if t.dsem is None:
            t.dsem = s.es.enter_context(s.nc.semaphore('d%d' % len(s.dtiles)))
            s.dtiles.append(t)
        waits = s._waits(q, [in_.t] + list(extra), [out.t], nowait_w)
        t.dcnt += 16
        ev = (t.dsem, t.dcnt)
        if fn is None:
            oa, ia = out.ap, in_.ap
            if slow:
                fn = lambda e: e.dma_start(out=oa, in_=ia, allow_slow_non_contiguous=True)
            else:
                fn = lambda e: e.dma_start(out=oa, in_=ia)
        s.ops[q].append((waits, fn, t.dsem, 16))
        out.t.w = ev
        if not nowait_w:
            out.t.r = {}
            out.t.pre = {}
        s._addr(in_.t, ev)
        for x in extra:
            s._addr(x, ev)

    @staticmethod
    def inherit(dsts, srcs):
        evs = {}
        for t in srcs:
            for ev in ([t.w] if t.w is not None else []) + list(t.r.values()) + list(t.pre.values()):
                k = id(ev[0])
                if k not in evs or evs[k][1] < ev[1]:
                    evs[k] = ev
        for d in dsts:
            for k, ev in evs.items():
                if k not in d.pre or d.pre[k][1] < ev[1]:
                    d.pre[k] = ev

    def finish(s):
        waits = []
        for t in s.dtiles:
            waits.append((t.dsem, t.dcnt))
        for e in ENGS:
            if e != 'sp' and s.cnt[e] > 0:
                waits.append((s.sem[e], s.cnt[e]))
        s.ops['sp'].append((waits, None, None, 0))

    def emit(s):
        nc = s.nc

        def replay(name, e):
            for waits, fn, sem, inc in s.ops[name]:
                for sm, val in waits:
                    e.wait_ge(sm, val)
                if fn is None:
                    continue
                ins = fn(e)
                if sem is not None:
                    ins.then_inc(sem, inc)
        with nc.Block() as block:
            @block.tensor
            def _(e):
                replay('pe', e)

            @block.scalar
            def _(e):
                replay('act', e)

            @block.vector
            def _(e):
                replay('dve', e)

            @block.gpsimd
            def _(e):
                replay('pool', e)

            @block.sync
            def _(e):
                replay('sp', e)


def build(n_pool=20480, n_tiles=8, do_sample=True):
    nc = bass.Bass("TRN2", target_bir_lowering=False)
    es = ExitStack()
    P = Prog(nc, es)

    def din(name, shape, dt=F32):
        return Tl(name, nc.dram_tensor(name, list(shape), dt, kind="ExternalInput").ap())

    def dout(name, shape, dt=F32):
        return Tl(name, nc.dram_tensor(name, list(shape), dt, kind="ExternalOutput").ap())

    def dscr(name, shape, dt=BF16):
        return Tl(name, nc.dram_tensor(name, list(shape), dt).ap())

    def sb(name, shape, dt=F32):
        return Tl(name, es.enter_context(nc.sbuf_tensor(name, list(shape), dt)))

    import os
    DBG = os.environ.get('KDBG', '0') == '1'
    dseen = set()

    def dump(name, v):
        if not DBG or name in dseen:
            return
        dseen.add(name)
        o = dout('dbg_' + name, list(v.ap.shape), v.ap.dtype)
        P.dma('sp', o[:], v)

    def mm(out, lhsT, rhs, start=True, stop=True, sig=None, skip=False):
        oa, la, ra = out.ap, lhsT.ap, rhs.ap
        if skip:
            P.op('pe', lambda e: e.matmul(oa, lhsT=la, rhs=ra, start=start, stop=stop, skip_group_check=True),
                 [lhsT.t, rhs.t], [out.t], sig=(stop if sig is None else sig))
        else:
            P.op('pe', lambda e: e.matmul(oa, lhsT=la, rhs=ra, start=start, stop=stop),
                 [lhsT.t, rhs.t], [out.t], sig=(stop if sig is None else sig))

    def act(out, in_, func, scale=None, bias=None, accum=None, eng='act'):
        oa, ia = out.ap, in_.ap
        kw = {}
        rd = [in_.t]
        wr = [out.t]
        if scale is not None:
            if isinstance(scale, V):
                kw['scale'] = scale.ap
                rd.append(scale.t)
            else:
                kw['scale'] = scale
        if bias is not None:
            if isinstance(bias, V):
                kw['bias'] = bias.ap
                rd.append(bias.t)
            else:
                kw['bias'] = bias
        if accum is not None:
            kw['accum_out'] = accum.ap
            wr.append(accum.t)
        P.op('act', lambda e: e.activation(out=oa, in_=ia, func=func, **kw), rd, wr)

    def tt(eng, out, in0, in1, op):
        oa, a0, a1 = out.ap, in0.ap, in1.ap
        P.op(eng, lambda e: e.tensor_tensor(out=oa, in0=a0, in1=a1, op=op), [in0.t, in1.t], [out.t])

    def ts(eng, out, in0, s1, s2, op0, op1=None):
        oa, a0 = out.ap, in0.ap
        rd = [in0.t]
        if isinstance(s1, V):
            rd.append(s1.t)
            s1 = s1.ap
        if isinstance(s2, V):
            rd.append(s2.t)
            s2 = s2.ap
        if op1 is None:
            P.op(eng, lambda e: e.tensor_scalar(out=oa, in0=a0, scalar1=s1, scalar2=None, op0=op0), rd, [out.t])
        else:
            P.op(eng, lambda e: e.tensor_scalar(out=oa, in0=a0, scalar1=s1, scalar2=s2, op0=op0, op1=op1), rd, [out.t])

    def stt(eng, out, in0, sc, in1, op0, op1):
        oa, a0, a1 = out.ap, in0.ap, in1.ap
        rd = [in0.t, in1.t]
        if isinstance(sc, V):
            rd.append(sc.t)
            sc = sc.ap
        P.op(eng, lambda e: e.scalar_tensor_tensor(out=oa, in0=a0, scalar=sc, in1=a1, op0=op0, op1=op1), rd, [out.t])

    def cp(eng, out, in_):
        oa, ia = out.ap, in_.ap
        if eng == 'act':
            P.op('act', lambda e: e.activation(out=oa, in_=ia, func=AF.Identity), [in_.t], [out.t])
        else:
            P.op(eng, lambda e: e.tensor_copy(out=oa, in_=ia), [in_.t], [out.t])

    def recip(out, in_):
        oa, ia = out.ap, in_.ap
        P.op('dve', lambda e: e.reciprocal(out=oa, in_=ia), [in_.t], [out.t])

    def rsqrt(out, in_, scale):
        ts('dve', out, in_, scale, EPS, ALU.mult, ALU.add)
        act(out, out, AF.Sqrt)
        recip(out, out)

    def rsum(out, in_):
        oa, ia = out.ap, in_.ap
        P.op('dve', lambda e: e.reduce_sum(out=oa, in_=ia, axis=mybir.AxisListType.X), [in_.t], [out.t])

    def mset(eng, out, val):
        oa = out.ap
        P.op(eng, lambda e: e.memset(oa, val), [], [out.t])

    x_p = din('x_p', [SEQ, D])
    x_s = din('x_s', [64, D])
    cache_kv = din('cache_kv', [n_pool, 128, 256])
    cache_pe = din('cache_pe', [n_pool, 128, 32])
    ptab = din('ptab', [128, 16], I32)
    c_all = din('c_all', [17, D])
    w_ada = din('w_ada', [D, 9 * D])
    b_ada = din('b_ada', [72, 128])
    gall = din('gall', [28, 128])
    w1g = din('w1g', [D, DFF]); w1u = din('w1u', [D, DFF]); w1d = din('w1d', [DFF, D])
    w2g = din('w2g', [D, DFF]); w2u = din('w2u', [D, DFF]); w2d = din('w2d', [DFF, D])
    w_in = din('w_in', [D, NIN])
    w_uq = din('w_uq', [256, 768])
    w_uk = din('w_uk', [256, 512])
    w_uv = din('w_uv', [256, 512])
    lnv = din('lnv', [2, 512])
    w_s = din('w_s', [8, 128, 128])
    b_s = din('b_s', [8, 128])
    w_pa = din('w_pa', [512, D]); w_pb = din('w_pb', [512, D]); w_o = din('w_o', [D, D])
    g_kv1 = din('g_kv1', [256])
    g_final = din('g_final', [D])
    c_identf = din('c_identf', [128, 128])
    c_maskT = din('c_maskT', [128, 128])
    c_masknew = din('c_masknew', [4, 32])
    c_maskblk = din('c_maskblk', [64, 64])
    c_csp = din('c_csp', [SEQ, 32])
    c_css = din('c_css', [64, 32])

    y_p = dout('y_p', [SEQ, D]); y_s = dout('y_s', [64, D])
    kv_p = dout('kv_p', [SEQ, 256]); pe_p = dout('pe_p', [SEQ, 32])
    kv_s = dout('kv_s', [64, 256]); pe_s = dout('pe_s', [64, 32]); gv_s = dout('gv_s', [64, 512])

    s1g = dscr('s1g', [D, DFF]); s1u = dscr('s1u', [D, DFF]); s1d = dscr('s1d', [DFF, D])
    s2g = dscr('s2g', [D, DFF]); s2u = dscr('s2u', [D, DFF]); s2d = dscr('s2d', [DFF, D])
    s_in = dscr('s_in', [D, NIN])
    s_pa = dscr('s_pa', [512, D]); s_pb = dscr('s_pb', [512, D]); s_o = dscr('s_o', [D, D])

    PS = [Tl('ps%d' % i, es.enter_context(nc.psum_tensor('ps%d' % i, [128, 512], F32))) for i in range(8)]
    NRING = 4
    RSZ = 4096
    ring = [sb('ring%d' % i, [128, RSZ], BF16) for i in range(NRING)]
    ring_i = [0]

    def ring_next():
        t = ring[ring_i[0] % NRING]
        ring_i[0] += 1
        return t

    identf = sb('identf', [128, 128]); identb = sb('identb', [128, 128], BF16)
    maskTb = sb('maskTb', [128, 128], BF16); maskTf = sb('maskTf', [128, 128])
    masknewf = sb('masknewf', [4, 32]); masknewb = sb('masknewb', [4, 32], BF16)
    maskblk = sb('maskblk', [64, 64])
    onesb = sb('onesb', [128, 1], BF16)
    cst = sb('cst', [128, 32]); css = sb('css', [64, 32])
    xt = sb('xt', [128, 4, D])
    xs = sb('xs', [128, D], BF16)
    junk = xs
    nT = sb('nT', [128, 8, 512], BF16)
    actT = sb('actT', [128, NFC, 512], BF16)
    tmpf = sb('tmpf', [128, 512]); tmpg = sb('tmpg', [128, 512]); tmph = sb('tmph', [128, 512])
    st = sb('st', [128, 16])
    KT = sb('KT', [128, 2, SEQ], BF16); KTpe = sb('KTpe', [32, SEQ], BF16)
    Vc = sb('Vc', [128, 32, 257], BF16)
    Vfl = Vc.h[:, :, :].rearrange('p a c -> p (a c)')
    modT = sb('modT', [128, 72, 17])
    badaT = sb('badaT', [128, 72]); bada72 = sb('bada72', [72, 128])
    gall_sb = sb('gall_sb', [28, 128]); gT = sb('gT', [128, 28])
    callsb = Tl('callsb', xt.h[0:17, 0, :]); scb = Tl('scb', xs.h[0:17, :]); scT = sb('scT', [128, 8, 17], BF16)
    GT = sb('GT', [128, 3, 8, 16]); GATE = sb('GATE', [128, 3, D])
    GKV = sb('GKV', [128, 256]); GF = sb('GF', [128, D]); LNG = sb('LNG', [128, 512]); LNB = sb('LNB', [128, 512])
    wuq = sb('wuq', [128, 2, 768], BF16); wuqf = Tl('wuqf', actT.h[:, 0:6, :].bitcast(F32).rearrange('p a c -> p (a c)').rearrange('p (k c) -> p k c', k=2))
    wukf = Tl('wukf', actT.h[:, 6:10, :].bitcast(F32).rearrange('p a c -> p (a c)').rearrange('p (k c) -> p k c', k=2)); wukb = Tl('wukb', actT.h[:, 20:22, :])
    wukT = sb('wukT', [128, 8, 256], BF16)
    wuvf = Tl('wuvf', actT.h[:, 10:14, :].bitcast(F32).rearrange('p a c -> p (a c)').rearrange('p (k c) -> p k c', k=2)); wuvpad = sb('wuvpad', [128, 8, 2, 128], BF16)
    wsf = Tl('wsf', actT.h[:, 14:18, :].bitcast(F32).rearrange('p a c -> p (a c)').rearrange('p (g c) -> p g c', g=8)); WS = sb('WS', [128, 8, 128], BF16)
    WSs = sb('WSs', [64, 8, 64], BF16); wssf = Tl('wssf', actT.h[0:64, 18:20, :].bitcast(F32).rearrange('p a c -> p (a c)').rearrange('p (g c) -> p g c', g=8))
    bsf = sb('bsf', [8, 128]); bsT = sb('bsT', [128, 8])
    cq_b = sb('cq_b', [128, 256], BF16); cqnT = sb('cqnT', [128, 2, 128], BF16)
    kvf = sb('kvf', [128, 256]); pef = sb('pef', [128, 32]); peb = sb('peb', [128, 32], BF16)
    kvb_s = sb('kvb_s', [64, 256], BF16); kvT_s = sb('kvT_s', [128, 2, 64], BF16); peT_s = sb('peT_s', [32, 64], BF16)
    qf = sb('qf', [128, 768]); qnb = sb('qnb', [128, 512], BF16); qnT = sb('qnT', [128, 4, 128], BF16)
    qpf = sb('qpf', [128, 8, 32]); qpb = sb('qpb', [128, 8, 32], BF16)
    QT = sb('QT', [128, 2, 1024], BF16); QTpe = sb('QTpe', [32, 1024], BF16)
    uf = sb('uf', [128, 512]); gvf = sb('gvf', [128, 512]); vnf = gvf; yoA = Tl('yoA', uf.h[:, :]); yoB = Tl('yoB', gvf.h[:, :]); vnb = sb('vnb', [128, 512], BF16)
    obb = sb('obb', [128, 512], BF16)
    obT = Tl('obT', actT.h[:, 0:4, :]); oaT = Tl('oaT', actT.h[:, 4:8, :])
    mergedT = Tl('mergedT', actT.h[:, 8:16, :])
    mixviews = [obT, oaT, mergedT]
    PTt = [sb('PT%d' % i, [128, 512], BF16) for i in range(2)]
    olat = sb('olat', [128, 8, 256], BF16)
    olatT = Tl('olatT', actT.h[:, 16:20, :].rearrange('p a c -> p (a c)').rearrange('p (k c) -> p k c', k=2))
    mixviews.append(olatT)
    rden = sb('rden', [128, 8])
    if do_sample:
        LK = [Tl('LK%d' % i, Vfl[:, 2048 * i:2048 * (i + 1)].rearrange('p (r c) -> p r c', c=256)) for i in range(2)]
        LK += [Tl('LK%d' % (2 + i), KT.h[:, 0, 2048 * i:2048 * (i + 1)].rearrange('p (r c) -> p r c', c=256)) for i in range(2)]
        LP = [Tl('LP%d' % i, Vfl[:, 4096 + 256 * i:4096 + 256 * (i + 1)].rearrange('p (r c) -> p r c', c=32)) for i in range(2)]
        LP += [Tl('LP%d' % (2 + i), KT.h[:, 1, 256 * i:256 * (i + 1)].rearrange('p (r c) -> p r c', c=32)) for i in range(2)]
        KTs = Tl('KTs', Vfl[:, 4608:5632].rearrange('p (k t) -> p k t', k=2))
        KTpes = Tl('KTpes', Vfl[0:32, 5632:6144])
        KTs2 = Tl('KTs2', Vfl[:, 6144:7168].rearrange('p (k t) -> p k t', k=2))
        KTpes2 = Tl('KTpes2', Vfl[0:32, 7168:7680])
        salias = LK[0:2] + LP[0:2] + [KTs, KTpes, KTs2, KTpes2]
        kalias = LK[2:4] + LP[2:4]
        pt_sb = sb('pt_sb', [128, 16], I32); idx_sb = sb('idx_sb', [128, 16, 16], I32)
        Vn = sb('Vn', [4, 256], BF16); PTn = sb('PTn', [4, 32], BF16)
        olat_s = sb('olat_s', [32, 256], BF16)
        gvout = gvf

    P.dma('sp', identf[:], c_identf[:])
    cp('dve', identb[:], identf[:])
    P.dma('sp', maskTf[:], c_maskT[:])
    cp('dve', maskTb[:], maskTf[:])
    P.dma('sp', masknewf[:], c_masknew[:])
    cp('dve', masknewb[:], masknewf[:])
    P.dma('sp', maskblk[:], c_maskblk[:])
    mset('dve', onesb[:], 1.0)
    P.dma('sp', css[:], c_css[:])
    P.dma('sp', callsb[:], c_all[:])
    P.dma('sp', bada72[:], b_ada[:])
    P.dma('sp', gall_sb[:], gall[:])
    P.dma('sp', GKV[:], g_kv1[:].f(lambda a: a.partition_broadcast(128)))
    P.dma('sp', GF[:], g_final[:].f(lambda a: a.partition_broadcast(128)))
    P.dma('sp', LNG[:], lnv[0, :].f(lambda a: a.partition_broadcast(128)))
    P.dma('sp', LNB[:], lnv[1, :].f(lambda a: a.partition_broadcast(128)))
    P.dma('sp', wuqf[:], w_uq[:].re("(k p) c -> p k c", p=128))
    cp('dve', wuq[:], wuqf[:])
    P.dma('sp', wukf[:], w_uk[:].re("(k p) c -> p k c", p=128))
    cp('dve', wukb[:], wukf[:])
    P.dma('sp', wuvf[:], w_uv[:].re("(k p) c -> p k c", p=128))
    P.dma('sp', wsf[:], w_s[:].re("g t s -> t g s"))
    P.dma('sp', bsf[:], b_s[:])

    mm(PS[0][:, 0:72], bada72[0:72, :], identf[0:72, 0:72])
    cp('dve', badaT[:], PS[0][:, 0:72])
    mm(PS[1][:, 0:28], gall_sb[0:28, :], identf[0:28, 0:28])
    cp('dve', gT[:], PS[1][:, 0:28])
    mm(PS[0][:, 0:8], bsf[0:8, :], identf[0:8, 0:8])
    cp('dve', bsT[:], PS[0][:, 0:8])
    mset('dve', wukT[:], 0.0)
    for pr in range(4):
        for cc in range(2):
            mm(PS[1][:, cc * 128:(cc + 1) * 128], wukb[:, cc, pr * 128:(pr + 1) * 128], identb[:])
        for h2 in range(2):
            cp('dve', wukT[h2 * 64:(h2 + 1) * 64, 2 * pr + h2, :], PS[1][h2 * 64:(h2 + 1) * 64, 0:256])
    mset('dve', wuvpad[:], 0.0)
    for h in range(NH):
        cp('dve', wuvpad[:, h, :, (h % 2) * 64:(h % 2) * 64 + 64], wuvf[:, :, h * 64:(h + 1) * 64])
    for g in range(8):
        mm(PS[0][:, 0:128], wsf[:, g, :], identf[:])
        tt('dve', WS[:, g, :], PS[0][:, 0:128], maskTf[:], ALU.mult)
    if do_sample:
        mset('dve', wssf[:], 0.0)
        for j in range(16):
            for g in range(8):
                P.dma('sp', wssf[4 * j:4 * j + 4, g, 4 * j:4 * j + 4], w_s[g, 0:4, 0:4].re("t s -> s t"),
                      nowait_w=(j > 0 or g > 0), slow=True)
        tt('dve', WSs[:], wssf[:], maskblk[:].us(1).bc([64, 8, 64]), ALU.mult)
        P.dma('sp', pt_sb[:], ptab[:])
        for a in range(16):
            ts('dve', idx_sb[:, :, a], pt_sb[:], 16, a, ALU.mult, ALU.add)

    act(scb[:], callsb[:], AF.Silu)
    for kc in range(8):
        mm(PS[0][:, kc * 17:(kc + 1) * 17], scb[0:17, kc * 128:(kc + 1) * 128], identb[0:17, 0:17])
    cp('dve', scT[:], PS[0][:, 0:136].re("p (k s) -> p k s", s=17))
    wadaT = w_ada[:].re("(k p) c -> p k c", p=128)
    for ch in range(18):
        slot = ring_next()
        sv = slot[:, 0:4096].re("p (k c) -> p k c", c=512)
        P.dma('pool', sv, wadaT[:, :, ch * 512:(ch + 1) * 512])
        if ch == 0 and os.environ.get('KD2', '0') != '0':
            if os.environ.get('KD2') in ('1', '3'):
                dump('scT', scT[:])
            if os.environ.get('KD2') in ('2', '3'):
                dump('wada0', sv)
        for j in range(4):
            J = ch * 4 + j
            jj = J % 24
            for kc in range(8):
                mm(PS[2][:, jj * 17:(jj + 1) * 17], sv[:, kc, j * 128:(j + 1) * 128], scT[:, kc, :],
                   start=(kc == 0), stop=(kc == 7))
            if jj == 23:
                J0 = J - 23
                tt('dve', modT[:, J0:J0 + 24, :], PS[2][:, 0:408].re("p (j s) -> p j s", s=17),
                   badaT[:, J0:J0 + 24].us(2).bc([128, 24, 17]), ALU.add)

    dump('modT', modT[:])
    def cast_w(dst, src, rows):
        for r0 in range(0, rows, 128):
            P.dma('pool', dst[r0:r0 + 128, :], src[r0:r0 + 128, :], nowait_w=True)
    cast_w(s1g, w1g, D); cast_w(s1u, w1u, D); cast_w(s1d, w1d, DFF)
    cast_w(s_in, w_in, D); cast_w(s_pa, w_pa, 512); cast_w(s_pb, w_pb, 512); cast_w(s_o, w_o, D)
    cast_w(s2g, w2g, D); cast_w(s2u, w2u, D); cast_w(s2d, w2d, DFF)

    def wload(src_view, kcn, ncols):
        slot = ring_next()
        sv = slot[:, 0:kcn * ncols].re("p (k c) -> p k c", c=ncols)
        P.dma('sp', sv, src_view)
        return sv

    def wcols(scr, kcn, c0, ncols):
        return wload(scr[:].re("(k p) c -> p k c", p=128)[:, 0:kcn, c0:c0 + ncols], kcn, ncols)

    def setup_pass(kind):
        if kind == 's':
            nseq, c0, nt = 16, 1, 64
        else:
            nseq, c0, nt = 1, 0, 128
        for i in range(3):
            sc = modT[:, (i * 3 + 1) * 8:(i * 3 + 1) * 8 + 8, c0:c0 + nseq]
            ts('dve', GT[:, i, :, 0:nseq], sc, 1.0, None, ALU.add)
            tt('dve', GT[:, i, :, 0:nseq], GT[:, i, :, 0:nseq],
               gT[:, i * 8:(i + 1) * 8].us(2).bc([128, 8, nseq]), ALU.mult)
            for kc in range(8):
                J = (i * 3 + 2) * 8 + kc
                gsrc = modT[:, J, c0:c0 + nseq]
                if kind == 's':
                    cp('dve', tmpf[:, 0:64].re("p (s g) -> p s g", g=4), gsrc.us(2).bc([128, 16, 4]))
                else:
                    cp('dve', tmpf[:, 0:128], gsrc.bc([128, 128]))
                mm(PS[kc // 4][0:nt, (kc % 4) * 128:(kc % 4 + 1) * 128], tmpf[:, 0:nt], identf[:])
            fac = 1.0 if i == 1 else 0.5
            for hf in range(2):
                ts('dve', GATE[0:nt, i, hf * 512:(hf + 1) * 512], PS[hf][0:nt, :], fac, None, ALU.mult)

    def bview(v, kind, nk, nt):
        if kind == 's':
            return v.us(3).bc([128, nk, 16, 4])
        return v.bc([128, nk, nt])

    def dview(v, kind, nk, nt):
        if kind == 's':
            return v.re("p (k s g) -> p k s g", k=nk, g=4)
        return v.re("p (k t) -> p k t", k=nk)

    def norm_to_nT(kind, i, nsub, nt):
        c0, nseq = (1, 16) if kind == 's' else (0, 1)
        for s in range(nsub):
            act(junk[0:nt, :], xt[0:nt, s, :], AF.Square)
            rsum(st[0:nt, 0:1], junk[0:nt, :])
            rsqrt(st[0:nt, 1:2], st[0:nt, 0:1], 1.0 / D)
            ts('dve', xs[0:nt, :], xt[0:nt, s, :], st[0:nt, 1:2], None, ALU.mult)
            dump('st_' + kind, st[0:nt, 0:2])
            dump('xs_' + kind, xs[0:nt, :])
            dump('gT', gT[:])
            dump('GT_' + kind, GT[:])
            for hf in range(2):
                for k4 in range(4):
                    kc = hf * 4 + k4
                    mm(PS[hf][:, k4 * nt:(k4 + 1) * nt], xs[0:nt, kc * 128:(kc + 1) * 128], identb[0:nt, 0:nt])
                g = GT[:, i, hf * 4:hf * 4 + 4, 0:nseq]
                sh = modT[:, (i * 3) * 8 + hf * 4:(i * 3) * 8 + hf * 4 + 4, c0:c0 + nseq]
                tt('dve', dview(tmpf[:, 0:4 * nt], kind, 4, nt), dview(PS[hf][:, 0:4 * nt], kind, 4, nt),
                   bview(g, kind, 4, nt), ALU.mult)
                ov = nT[:, hf * 4:hf * 4 + 4, s * nt:(s + 1) * nt]
                if kind == 's':
                    ov = ov.re("p k (s g) -> p k s g", g=4)
                tt('pool', ov, dview(tmpf[:, 0:4 * nt], kind, 4, nt), bview(sh, kind, 4, nt), ALU.add)

    def ffn(kind, i, sg_, su_, sd_, nsub, nt):
        NT = nsub * nt
        gi = 0 if i == 0 else 2
        P.inherit([actT], mixviews)
        norm_to_nT(kind, gi, nsub, nt)
        dump('GATE_' + kind, GATE[0:nt, :, :])
        dump('nT%d_' % i + kind, nT[:, :, 0:NT])
        fc = 0
        flip = 0
        for c0 in range(0, DFF, 512):
            ncols = min(512, DFF - c0)
            wg = wcols(sg_, 8, c0, ncols)
            wu = wcols(su_, 8, c0, ncols)
            for j in range(ncols // 128):
                pg, pu = (PS[2], PS[3]) if flip == 0 else (PS[4], PS[5])
                flip ^= 1
                for kc in range(8):
                    mm(pg[:, 0:NT], wg[:, kc, j * 128:(j + 1) * 128], nT[:, kc, 0:NT], start=(kc == 0), stop=(kc == 7))
                for kc in range(8):
                    mm(pu[:, 0:NT], wu[:, kc, j * 128:(j + 1) * 128], nT[:, kc, 0:NT], start=(kc == 0), stop=(kc == 7))
                tg = tmpg if flip else tmph
                act(tg[:, 0:NT], pg[:, 0:NT], AF.Silu)
                tt('dve', actT[:, fc, 0:NT], tg[:, 0:NT], pu[:, 0:NT], ALU.mult)
                fc += 1
        dump('actT%d_' % i + kind, actT[:, :, 0:NT])
        halves = [[0, 1, 2, 3]] if nsub == 4 else [[0]]
        pb0 = 0 if nsub == 4 else 4
        for hv in halves:
            for f0 in range(0, NFC, 4):
                nf = min(4, NFC - f0)
                wd = wload(sd_[f0 * 128:(f0 + nf) * 128, :].re("(f p) d -> p f d", p=128), nf, D)
                for fl in range(nf):
                    f = f0 + fl
                    for si, s in enumerate(hv):
                        for dh in range(2):
                            mm(PS[pb0 + si * 2 + dh][0:nt, :], actT[:, f, s * nt:(s + 1) * nt], wd[:, fl, dh * 512:(dh + 1) * 512],
                               start=(f == 0), stop=(f == NFC - 1),
                               sig=(f == NFC - 1) or (fl == nf - 1 and si == len(hv) - 1 and dh == 1))
            for si, s in enumerate(hv):
                for dh in range(2):
                    tt('dve', tmpf[0:nt, :], PS[pb0 + si * 2 + dh][0:nt, :], GATE[0:nt, gi, dh * 512:(dh + 1) * 512], ALU.mult)
                    tt('pool', xt[0:nt, s, dh * 512:(dh + 1) * 512], xt[0:nt, s, dh * 512:(dh + 1) * 512], tmpf[0:nt, :], ALU.add)

    def dumpx(name, kind, nsub, nt):
        dump(name + '_' + kind, xt[0:nt, 0:nsub, :])

    def transpose_cols(dst, src, nt, nblk, psb):
        for b in range(nblk):
            mm(psb[:, b * nt:(b + 1) * nt], src[0:nt, b * 128:(b + 1) * 128], identb[0:nt, 0:nt])

    def mixer(kind, tile_i, nsub, nt):
        import os
        STOP = float(os.environ.get('KSTOP', '9'))
        NT = nsub * nt
        c0, nseq = (1, 16) if kind == 's' else (0, 1)
        norm_to_nT(kind, 1, nsub, nt)
        dump('nTm_' + kind, nT[:, :, 0:NT])
        P.inherit(mixviews, [actT])
        P.inherit([uf, gvf], [yoA, yoB])
        wA = wcols(s_in, 8, 0, 512)
        wA2 = wcols(s_in, 8, 512, 32)
        wB = wcols(s_in, 8, 544, 512)
        wC = wcols(s_in, 8, 1056, 512)
        for s in range(nsub):
            gq = tile_i * 4 + s
            ncol = slice(s * nt, (s + 1) * nt)
            for kc in range(8):
                mm(PS[0][0:nt, :], nT[:, kc, ncol], wA[:, kc, 0:512], start=(kc == 0), stop=(kc == 7))
            for kc in range(8):
                mm(PS[1][0:nt, 0:32], nT[:, kc, ncol], wA2[:, kc, 0:32], start=(kc == 0), stop=(kc == 7))
            act(junk[0:nt, 0:256], PS[0][0:nt, 0:256], AF.Square)
            rsum(st[0:nt, 2:3], junk[0:nt, 0:256])
            rsqrt(st[0:nt, 3:4], st[0:nt, 2:3], 1.0 / 256)
            ts('dve', cq_b[0:nt, :], PS[0][0:nt, 0:256], st[0:nt, 3:4], None, ALU.mult)
            act(junk[0:nt, 256:512], PS[0][0:nt, 256:512], AF.Square)
            rsum(st[0:nt, 4:5], junk[0:nt, 256:512])
            rsqrt(st[0:nt, 5:6], st[0:nt, 4:5], 1.0 / 256)
            stt('dve', kvf[0:nt, :], PS[0][0:nt, 256:512], st[0:nt, 5:6], GKV[0:nt, :], ALU.mult, ALU.mult)
            if STOP <= 1.01:
                return
            if kind == 's':
                cs = css[0:64, :]
            else:
                P.dma('sp', cst[:], c_csp[gq * 128:(gq + 1) * 128, :])
                cs = cst[:, :]
            x1 = PS[1][0:nt, 0:16]; x2 = PS[1][0:nt, 16:32]
            tt('dve', tmph[0:nt, 0:16], x1, cs[:, 0:16], ALU.mult)
            tt('dve', tmph[0:nt, 16:32], x2, cs[:, 16:32], ALU.mult)
            tt('dve', pef[0:nt, 0:16], tmph[0:nt, 0:16], tmph[0:nt, 16:32], ALU.subtract)
            tt('dve', tmph[0:nt, 32:48], x1, cs[:, 16:32], ALU.mult)
            tt('dve', tmph[0:nt, 48:64], x2, cs[:, 0:16], ALU.mult)
            tt('dve', pef[0:nt, 16:32], tmph[0:nt, 32:48], tmph[0:nt, 48:64], ALU.add)
            cp('dve', peb[0:nt, :], pef[0:nt, :])
            if kind == 's':
                P.dma('sp', kv_s[:], kvf[0:64, :], owner=kvf, nowait_w=True)
                P.dma('sp', pe_s[:], pef[0:64, :], owner=pef, nowait_w=True)
                cp('dve', kvb_s[:], kvf[0:64, :])
                kvsrc = kvb_s
            else:
                P.dma('sp', kv_p[gq * 128:(gq + 1) * 128, :], kvf[:], owner=kvf, nowait_w=True)
                P.dma('sp', pe_p[gq * 128:(gq + 1) * 128, :], pef[:], owner=pef, nowait_w=True)
                if STOP <= 1.015:
                    return
                cp('dve', Vc[:, gq, 0:256], kvf[:])
                kvsrc = None
            if STOP <= 1.02:
                return
            for cc in range(2):
                mm(PS[2][:, cc * nt:(cc + 1) * nt], cq_b[0:nt, cc * 128:(cc + 1) * 128], identb[0:nt, 0:nt])
            for cc in range(2):
                ts('dve', cqnT[:, cc, 0:nt], PS[2][:, cc * nt:(cc + 1) * nt], gT[:, 24 + cc:25 + cc], None, ALU.mult)
            for cc in range(2):
                if kind == 's':
                    mm(PS[3][:, cc * nt:(cc + 1) * nt], kvb_s[0:nt, cc * 128:(cc + 1) * 128], identb[0:nt, 0:nt])
                else:
                    mm(PS[3][:, cc * nt:(cc + 1) * nt], Vc[:, gq, cc * 128:(cc + 1) * 128], identb[:])
            mm(PS[3][0:32, 2 * nt:3 * nt], peb[0:nt, :], identb[0:nt, 0:nt])
            if kind == 's':
                cp('dve', kvT_s[:, :, :], PS[3][:, 0:2 * nt].re("p (c t) -> p c t", c=2))
                cp('dve', peT_s[:, :], PS[3][0:32, 2 * nt:3 * nt])
            else:
                cp('dve', KT[:, :, gq * 128:(gq + 1) * 128], PS[3][:, 0:256].re("p (c t) -> p c t", c=2))
                cp('dve', KTpe[:, gq * 128:(gq + 1) * 128], PS[3][0:32, 256:384])
            if STOP <= 1.1:
                return
            for cc in range(2):
                mm(PS[0][0:nt, :], cqnT[:, cc, 0:nt], wuq[:, cc, 0:512], start=(cc == 0), stop=(cc == 1))
            for cc in range(2):
                mm(PS[1][0:nt, 0:256], cqnT[:, cc, 0:nt], wuq[:, cc, 512:768], start=(cc == 0), stop=(cc == 1))
            cp('act', qf[0:nt, 0:512], PS[0][0:nt, :])
            cp('dve', qf[0:nt, 512:768], PS[1][0:nt, 0:256])
            q3 = qf[0:nt, :].re("p (h e) -> p h e", e=96)
            cp('dve', qnb[0:nt, :].re("p (h e) -> p h e", e=64), q3[:, :, 0:64])
            q1 = q3[:, :, 64:80]; q2 = q3[:, :, 80:96]
            cosb = cs[:, 0:16].us(1).bc([nt, 8, 16]); sinb = cs[:, 16:32].us(1).bc([nt, 8, 16])
            ta = tmpg[0:nt, 0:128].re("p (h e) -> p h e", e=16); tb = tmpg[0:nt, 128:256].re("p (h e) -> p h e", e=16)
            tt('dve', ta, q1, cosb, ALU.mult)
            tt('dve', tb, q2, sinb, ALU.mult)
            tt('dve', qpf[0:nt, :, 0:16], ta, tb, ALU.subtract)
            tt('dve', ta, q1, sinb, ALU.mult)
            tt('dve', tb, q2, cosb, ALU.mult)
            tt('dve', qpf[0:nt, :, 16:32], ta, tb, ALU.add)
            cp('dve', qpb[0:nt, :, :], qpf[0:nt, :, :])
            if STOP <= 1.2:
                return
            transpose_cols(None, qnb, nt, 4, PS[2])
            cp('dve', qnT[:, :, 0:nt], PS[2][:, 0:4 * nt].re("p (b t) -> p b t", b=4))
            if STOP <= 1.25:
                return
            for h in range(NH):
                mm(PS[3][0:32, h * nt:(h + 1) * nt] if nt * 8 <= 512 else PS[3 + h // 4][0:32, (h % 4) * nt:(h % 4 + 1) * nt],
                   qpb[0:nt, h, :], identb[0:nt, 0:nt])
            if nt * 8 <= 512:
                cp('act', QTpe[:, 0:8 * nt], PS[3][0:32, 0:8 * nt])
            else:
                cp('act', QTpe[:, 0:512], PS[3][0:32, :])
                cp('act', QTpe[:, 512:1024], PS[4][0:32, :])
            if STOP <= 1.27:
                return
            for cc in range(2):
                for h in range(NH):
                    pr, h2 = h // 2, h % 2
                    if nt * 8 <= 512:
                        o = PS[5 + cc][:, h * nt:(h + 1) * nt]
                    else:
                        o = PS[5 + cc][:, (h % 4) * nt:(h % 4 + 1) * nt] if h < 4 else None
                    if o is None:
                        continue
                    mm(o, wukT[:, h, cc * 128:(cc + 1) * 128], qnT[:, pr, 0:nt])
                n1 = min(8 * nt, 512)
                cp('act' if cc == 0 else 'dve', QT[:, cc, 0:n1], PS[5 + cc][:, 0:n1])
            if nt * 8 > 512:
                for cc in range(2):
                    for h in range(4, NH):
                        pr, h2 = h // 2, h % 2
                        mm(PS[5 + cc][:, (h % 4) * nt:(h % 4 + 1) * nt], wukT[:, h, cc * 128:(cc + 1) * 128],
                           qnT[:, pr, 0:nt])
                    cp('act' if cc == 0 else 'dve', QT[:, cc, 512:1024], PS[5 + cc][:, :])
            if STOP <= 1.3:
                return
            if os.environ.get('KD2', '0') == '4':
                dump('QT_' + kind, QT[:, :, 0:8 * nt])
            if os.environ.get('KD2', '0') == '5':
                dump('QTpe_' + kind, QTpe[:, 0:8 * nt])
            for kc in range(8):
                mm(PS[0][0:nt, :], nT[:, kc, ncol], wB[:, kc, :], start=(kc == 0), stop=(kc == 7))
            for kc in range(8):
                mm(PS[1][0:nt, :], nT[:, kc, ncol], wC[:, kc, :], start=(kc == 0), stop=(kc == 7))
            act(uf[0:nt, :], PS[0][0:nt, :], AF.Gelu_apprx_tanh)
            act(gvf[0:nt, :], PS[1][0:nt, :], AF.Gelu_apprx_tanh)
            rsum(st[0:nt, 6:7], gvf[0:nt, :])
            act(junk[0:nt, 0:512], gvf[0:nt, :], AF.Square)
            rsum(st[0:nt, 7:8], junk[0:nt, 0:512])
            ts('dve', st[0:nt, 8:9], st[0:nt, 6:7], 1.0 / 512, None, ALU.mult)
            ts('dve', st[0:nt, 9:10], st[0:nt, 7:8], 1.0 / 512, EPS, ALU.mult, ALU.add)
            tt('dve', st[0:nt, 10:11], st[0:nt, 8:9], st[0:nt, 8:9], ALU.mult)
            tt('dve', st[0:nt, 9:10], st[0:nt, 9:10], st[0:nt, 10:11], ALU.subtract)
            act(st[0:nt, 9:10], st[0:nt, 9:10], AF.Sqrt)
            recip(st[0:nt, 9:10], st[0:nt, 9:10])
            ts('dve', vnf[0:nt, :], gvf[0:nt, :], st[0:nt, 8:9], st[0:nt, 9:10], ALU.subtract, ALU.mult)
            tt('dve', vnf[0:nt, :], vnf[0:nt, :], LNG[0:nt, :], ALU.mult)
            if kind == 's':
                tt('dve', gvout[0:64, :], vnf[0:64, :], LNB[0:64, :], ALU.add)
                P.dma('sp', gv_s[:], gvout[0:64, :], owner=gvout, nowait_w=True)
                cp('dve', vnb[0:nt, :], gvout[0:64, :])
            else:
                tt('pool', vnb[0:nt, :], vnf[0:nt, :], LNB[0:nt, :], ALU.add)
            for g in range(8):
                wsm = WSs[0:64, g, :] if kind == 's' else WS[:, g, :]
                mm(PS[2][0:nt, g * 64:(g + 1) * 64], wsm, vnb[0:nt, g * 64:(g + 1) * 64])
            bsv = bsrep[0:64, :] if kind == 's' else bsT[0:nt, :]
            tt('dve', tmph[0:nt, :].re("p (g e) -> p g e", e=64), PS[2][0:nt, :].re("p (g e) -> p g e", e=64),
               bsv.us(2).bc([nt, 8, 64]), ALU.add)
            tt('dve', obb[0:nt, :], tmph[0:nt, :], uf[0:nt, :], ALU.mult)
            transpose_cols(None, obb, nt, 4, PS[3])
            cp('act', obT[:, :, ncol], PS[3][:, 0:4 * nt].re("p (b t) -> p b t", b=4))
            if STOP <= 1.4:
                return
            if kind == 'p':
                attend_prompt(gq)
                nq = 128
            if STOP <= 1.5:
                return
            if kind == 'p':
                dump('olat_p', olat[:])
            if kind == 'p':
                for cc in range(2):
                    for hb in range(2):
                        for h4 in range(4):
                            h = hb * 4 + h4
                            mm(PS[hb][:, h4 * 128:(h4 + 1) * 128], olat[:, h, cc * 128:(cc + 1) * 128], identb[:])
                        cp('act' if hb == 0 else 'dve', olatT[:, cc, hb * 512:(hb + 1) * 512], PS[hb][:, :])
                for pr in range(4):
                    k = 0
                    for h in (2 * pr, 2 * pr + 1):
                        for cc in range(2):
                            mm(PS[2][:, pr * 128:(pr + 1) * 128], wuvpad[:, h, cc, :], olatT[:, cc, h * 128:(h + 1) * 128],
                               start=(k == 0), stop=(k == 3))
                            k += 1
                cp('dve', oaT[:, :, ncol], PS[2][:, :].re("p (b t) -> p b t", b=4))
        if kind == 's':
            attend_sample()
            for pr in range(4):
                k = 0
                for h in (2 * pr, 2 * pr + 1):
                    for cc in range(2):
                        rv = olatT[:, cc, 0:512].re("p (s h q) -> p s h q", h=8, q=4)[:, :, h, :]
                        mm(PS[2][:, pr * 64:(pr + 1) * 64], wuvpad[:, h, cc, :], rv, start=(k == 0), stop=(k == 3))
                        k += 1
            cp('dve', oaT[:, :, 0:64], PS[2][:, 0:256].re("p (b t) -> p b t", b=4))
        if STOP <= 1.6:
            return
        dump('oaT_' + kind, oaT[:, :, 0:NT])
        dump('obT_' + kind, obT[:, :, 0:NT])
        for dq in range(2):
            wpa = wcols(s_pa, 4, dq * 512, 512)
            wpb = wcols(s_pb, 4, dq * 512, 512)
            wga = wcols(s_in, 8, 1568 + dq * 512, 512)
            wgb = wcols(s_in, 8, 2592 + dq * 512, 512)
            for d4 in range(4):
                dc = dq * 4 + d4
                for kc in range(4):
                    mm(PS[0][:, 0:NT], wpa[:, kc, d4 * 128:(d4 + 1) * 128], oaT[:, kc, 0:NT], start=(kc == 0), stop=(kc == 3))
                for kc in range(8):
                    mm(PS[1][:, 0:NT], wga[:, kc, d4 * 128:(d4 + 1) * 128], nT[:, kc, 0:NT], start=(kc == 0), stop=(kc == 7))
                for kc in range(4):
                    mm(PS[2][:, 0:NT], wpb[:, kc, d4 * 128:(d4 + 1) * 128], obT[:, kc, 0:NT], start=(kc == 0), stop=(kc == 3))
                for kc in range(8):
                    mm(PS[3][:, 0:NT], wgb[:, kc, d4 * 128:(d4 + 1) * 128], nT[:, kc, 0:NT], start=(kc == 0), stop=(kc == 7))
                act(tmpg[:, 0:NT], PS[1][:, 0:NT], AF.Sigmoid)
                tt('dve', tmpg[:, 0:NT], tmpg[:, 0:NT], PS[0][:, 0:NT], ALU.mult)
                act(tmph[:, 0:NT], PS[3][:, 0:NT], AF.Sigmoid)
                tt('dve', tmph[:, 0:NT], tmph[:, 0:NT], PS[2][:, 0:NT], ALU.mult)
                tt('pool', mergedT[:, dc, 0:NT], tmpg[:, 0:NT], tmph[:, 0:NT], ALU.add)
        if STOP <= 1.7:
            return
        dump('mergedT_' + kind, mergedT[:, :, 0:NT])
        for dh in range(2):
            wo = wcols(s_o, 8, dh * 512, 512)
            for s in range(nsub):
                pb = PS[4 + (s % 2)]
                for kc in range(8):
                    mm(pb[0:nt, :], mergedT[:, kc, s * nt:(s + 1) * nt], wo[:, kc, :], start=(kc == 0), stop=(kc == 7))
                tt('dve', tmpf[0:nt, :], pb[0:nt, :], GATE[0:nt, 1, dh * 512:(dh + 1) * 512], ALU.mult)
                tt('pool', xt[0:nt, s, dh * 512:(dh + 1) * 512], xt[0:nt, s, dh * 512:(dh + 1) * 512], tmpf[0:nt, :], ALU.add)

    def attend_prompt(gq):
        for ch in range(2):
            cols = slice(ch * 512, (ch + 1) * 512)
            for kj in range(gq + 1):
                pS = PS[2 + (kj % 2)]
                kc_ = slice(kj * 128, (kj + 1) * 128)
                mm(pS[:, :], KT[:, 0, kc_], QT[:, 0, cols], start=True, stop=False)
                mm(pS[:, :], KT[:, 1, kc_], QT[:, 1, cols], start=False, stop=False)
                mm(pS[:, :], KTpe[0:32, kc_], QTpe[0:32, cols], start=False, stop=True)
                pt_ = PTt[kj % 2]
                act(pt_[:, :], pS[:, :], AF.Exp, scale=SM_SCALE)
                if kj == gq:
                    tt('pool', pt_[:, :].re("p (h q) -> p h q", q=128), pt_[:, :].re("p (h q) -> p h q", q=128),
                       maskTb[:, :].us(1).bc([128, 4, 128]), ALU.mult)
                for hh in range(4):
                    mm(PS[4 + hh][:, 0:257], pt_[:, hh * 128:(hh + 1) * 128], Vc[:, kj, 0:257],
                       start=(kj == 0), stop=(kj == gq))
            for hh in range(4):
                h = ch * 4 + hh
                recip(rden[:, h:h + 1], PS[4 + hh][:, 256:257])
                ts('dve', olat[:, h, :], PS[4 + hh][:, 0:256], rden[:, h:h + 1], None, ALU.mult)

    def attend_sample():
        ckv = cache_kv[:].re("n (a r) c -> (n a) (r c)", r=8).ap
        cpe = cache_pe[:].re("n (a r) c -> (n a) (r c)", r=8).ap
        NR = 16 * 16
        psT = [(PS[0], PS[1], PS[7]), (PS[5], PS[6], PS[3])]
        KTset = [(KTs, KTpes), (KTs2, KTpes2)]

        def gather(R):
            sq, rb = divmod(R, 16)
            lk = LK[R % 4]; lp = LP[R % 4]
            ia = idx_sb[:, sq, rb:rb + 1].ap
            lka = lk[:].re("p r c -> p (r c)").ap; lpa = lp[:].re("p r c -> p (r c)").ap
            P.dma('pool', lk[:], cache_kv[:], extra=[idx_sb], fn=(lambda o, i_, x: (lambda e: e.indirect_dma_start(
                out=o, out_offset=None, in_=i_, in_offset=bass.IndirectOffsetOnAxis(ap=x, axis=0))))(lka, ckv, ia))
            P.dma('pool', lp[:], cache_pe[:], extra=[idx_sb], fn=(lambda o, i_, x: (lambda e: e.indirect_dma_start(
                out=o, out_offset=None, in_=i_, in_offset=bass.IndirectOffsetOnAxis(ap=x, axis=0))))(lpa, cpe, ia))

        def prep_q(sq):
            q = qs2[sq % 2]
            for cc in range(2):
                cp('dve', q[:, cc, :].re("p (h q) -> p h q", q=4),
                   QT[:, cc, 0:512].re("p (h s q) -> p s h q", h=8, q=4)[:, sq, :, :])
            cp('dve', q[0:32, 2, :].re("p (h q) -> p h q", q=4),
               QTpe[0:32, 0:512].re("p (h s q) -> p s h q", h=8, q=4)[:, sq, :, :])

        def T(R, r4, par):
            lk = LK[R % 4]; lp = LP[R % 4]
            p0, p1, pp = psT[par]
            for rr in range(4):
                r = r4 * 4 + rr
                mm(p0[:, rr * 128:(rr + 1) * 128], lk[:, r, 0:128], identb[:])
                mm(p1[:, rr * 128:(rr + 1) * 128], lk[:, r, 128:256], identb[:])
                mm(pp[0:32, rr * 128:(rr + 1) * 128], lp[:, r, :], identb[:])
            kt, ktp = KTset[par]
            cp('act', kt[:, 0, :], p0[:, :])
            cp('dve', kt[:, 1, :], p1[:, :])
            cp('act', ktp[:, :], pp[0:32, :])

        def S(R, r4, par, q):
            kt, ktp = KTset[par]
            base = (R % 2) * 256
            for rr in range(4):
                r = r4 * 4 + rr
                so = PS[2][:, base + r * 32:base + (r + 1) * 32]
                mm(so, kt[:, 0, rr * 128:(rr + 1) * 128], q[:, 0, :], start=True, stop=False)
                mm(so, kt[:, 1, rr * 128:(rr + 1) * 128], q[:, 1, :], start=False, stop=False)
                mm(so, ktp[0:32, rr * 128:(rr + 1) * 128], q[0:32, 2, :], start=False, stop=True)

        def X(R):
            base = (R % 2) * 256
            act(PTt[R % 2][:, 0:256], PS[2][:, base:base + 256], AF.Exp, scale=SM_SCALE)

        def PV(R, first):
            lk = LK[R % 4]; pt_ = PTt[R % 2]
            for r in range(8):
                mm(PS[4][0:32, 0:256], pt_[:, r * 32:(r + 1) * 32], lk[:, r, :], start=(first and r == 0), stop=False, skip=True)
                mm(PS[4][0:32, 256:257], pt_[:, r * 32:(r + 1) * 32], onesb[:, 0:1], start=False, stop=False, skip=True)

        def epilogue(sq):
            q = qs2[sq % 2]
            tcol = slice(sq * 4, sq * 4 + 4)
            mm(PS[0][0:4, 0:32], kvT_s[:, 0, tcol], q[:, 0, :], start=True, stop=False)
            mm(PS[0][0:4, 0:32], kvT_s[:, 1, tcol], q[:, 1, :], start=False, stop=False)
            mm(PS[0][0:4, 0:32], peT_s[0:32, tcol], q[0:32, 2, :], start=False, stop=True)
            act(PTn[:, :], PS[0][0:4, 0:32], AF.Exp, scale=SM_SCALE)
            tt('dve', PTn[:, :], PTn[:, :], masknewb[:, :], ALU.mult)
            mm(PS[0][0:4, 256:512], identb[0:64, tcol], kvb_s[0:64, :])
            cp('dve', Vn[:, :], PS[0][0:4, 256:512])
            mm(PS[4][0:32, 0:256], PTn[0:4, :], Vn[0:4, :], start=False, stop=True, skip=True)
            mm(PS[4][0:32, 256:257], PTn[0:4, :], onesb[0:4, 0:1], start=False, stop=True, skip=True)
            recip(rden[0:32, 0:1], PS[4][0:32, 256:257])
            ts('dve', olat_s[:, :], PS[4][0:32, 0:256], rden[0:32, 0:1], None, ALU.mult)
            for cc in range(2):
                mm(PS[1][:, cc * 32:(cc + 1) * 32], olat_s[0:32, cc * 128:(cc + 1) * 128], identb[0:32, 0:32])
            cp('dve', olatT[:, :, sq * 32:(sq + 1) * 32], PS[1][:, 0:64].re("p (c t) -> p c t", c=2))

        for R in range(3):
            gather(R)
        prep_q(0)
        T(0, 0, 0)
        for R in range(NR):
            sq, rb = divmod(R, 16)
            if rb == 0 and sq + 1 < 16:
                prep_q(sq + 1)
            T(R, 1, 1)
            S(R, 0, 0, qs2[sq % 2])
            if R + 1 < NR:
                T(R + 1, 0, 0)
            S(R, 1, 1, qs2[sq % 2])
            X(R)
            if rb >= 1:
                PV(R - 1, first=(rb == 1))
            if rb == 15:
                PV(R, first=False)
                epilogue(sq)
            if R + 3 < NR:
                gather(R + 3)

    if do_sample:
        qs2 = [sb('qs_sb%d' % i, [128, 3, 32], BF16) for i in range(2)]
        bsrep = sb('bsrep', [64, 8])
        for j in range(16):
            P.dma('sp', bsrep[4 * j:4 * j + 4, :], b_s[:, 0:4].re("g t -> t g"), nowait_w=True, slow=True)

    def final_out(kind, tile_i, nsub, nt):
        P.inherit([yoA, yoB], [uf, gvf])
        for s in range(nsub):
            act(junk[0:nt, :], xt[0:nt, s, :], AF.Square)
            rsum(st[0:nt, 12:13], junk[0:nt, :])
            rsqrt(st[0:nt, 13:14], st[0:nt, 12:13], 1.0 / D)
            for dh, yo in enumerate((yoA, yoB)):
                dsl = slice(dh * 512, (dh + 1) * 512)
                stt('dve', yo[0:nt, :], xt[0:nt, s, dsl], st[0:nt, 13:14], GF[0:nt, dsl], ALU.mult, ALU.mult)
                if kind == 's':
                    P.dma('sp', y_s[:, dsl], yo[0:64, :], owner=yo, nowait_w=True)
                else:
                    r0 = tile_i * 512 + s * 128
                    P.dma('sp', y_p[r0:r0 + 128, dsl], yo[:], owner=yo, nowait_w=True)

    def run_pass(kind, tile_i):
        if kind == 's':
            nsub, nt = 1, 64
            P.dma('sp', xt[0:64, 0, :], x_s[:])
        else:
            nsub, nt = 4, 128
            P.dma('sp', xt[:], x_p[tile_i * 512:(tile_i + 1) * 512, :].re("(s p) d -> p s d", p=128))
        import os
        STOP = float(os.environ.get('KSTOP', '9'))
        ffn(kind, 0, s1g, s1u, s1d, nsub, nt)
        dumpx('x1', kind, nsub, nt)
        if STOP <= 1:
            return
        mixer(kind, tile_i, nsub, nt)
        dumpx('x2', kind, nsub, nt)
        if STOP <= 2:
            return
        ffn(kind, 2, s2g, s2u, s2d, nsub, nt)
        dumpx('x3', kind, nsub, nt)
        final_out(kind, tile_i, nsub, nt)

    P.inherit([actT], [wuqf, wukf, wuvf, wsf, wukb] + ([wssf] if do_sample else []))
    P.inherit([xt], [callsb])
    P.inherit([xs], [scb])
    if do_sample:
        setup_pass('s')
        run_pass('s', 0)
        P.inherit([Vc], salias)
        P.inherit([KT], kalias)
    if n_tiles > 0:
        mset('dve', Vc[:, :, 256:257], 1.0)
        setup_pass('p')
        for ti in range(n_tiles):
            run_pass('p', ti)

    P.finish()
    print('ops', {e: len(v) for e, v in P.ops.items()}, 'cnt', P.cnt, 'maxd', max([t.dcnt for t in P.dtiles] + [0]), 'nd', len(P.dtiles))
    P.emit()
    es.close()
    return nc


def _consts():
    identf = np.eye(128, dtype=np.float32)
    maskT = (np.arange(128)[None, :] >= np.arange(128)[:, None]).astype(np.float32)
    masknew = np.zeros((4, 32), np.float32)
    for j in range(4):
        for h in range(8):
            for q in range(4):
                masknew[j, h * 4 + q] = 1.0 if j <= q else 0.0
    s_ = np.arange(64)
    maskblk = ((s_[:, None] // 4 == s_[None, :] // 4) & (s_[None, :] % 4 >= s_[:, None] % 4)).astype(np.float32)
    half = 16
    freqs = (np.float32(10000.0) ** (np.float32(-2.0) * np.arange(half, dtype=np.float32) / np.float32(32))).astype(np.float32)

    def tab(pos):
        ang = pos.astype(np.float32)[:, None] * freqs[None, :]
        return np.concatenate([np.cos(ang), np.sin(ang)], axis=1).astype(np.float32)
    csp = tab(np.arange(SEQ))
    css = tab(PAST + (np.arange(64) % 4))
    return dict(c_identf=identf, c_maskT=maskT, c_masknew=masknew, c_maskblk=maskblk, c_csp=csp, c_css=css)


def make_in_maps(inputs, cores, n_pool=None, cache_override=None):
    f = lambda a: np.ascontiguousarray(np.asarray(a))
    I = {k: np.asarray(v) for k, v in inputs.items()}
    consts = _consts()
    shared = dict(
        w_ada=f(I['w_ada'][0]), b_ada=f(I['b_ada'][0].reshape(72, 128)),
        gall=f(np.concatenate([I['g_ffn1'][0].reshape(8, 128), I['g_mix'][0].reshape(8, 128), I['g_ffn2'][0].reshape(8, 128),
                               I['g_q'][0].reshape(2, 128), I['g_kv'][0].reshape(2, 128)], axis=0)),
        w1g=f(I['w1_gate'][0]), w1u=f(I['w1_up'][0]), w1d=f(I['w1_down'][0]),
        w2g=f(I['w2_gate'][0]), w2u=f(I['w2_up'][0]), w2d=f(I['w2_down'][0]),
        w_in=f(I['w_in'][0]), w_uq=f(I['w_uq'][0]),
        w_uk=f(I['w_uk'][0].reshape(256, 512)), w_uv=f(I['w_uv'][0].reshape(256, 512)),
        lnv=f(np.stack([I['ln_v_g'][0], I['ln_v_b'][0]])), w_s=f(I['w_s'][0]), b_s=f(I['b_s'][0]),
        w_pa=f(I['w_pa'][0]), w_pb=f(I['w_pb'][0]), w_o=f(I['w_o'][0]),
        g_kv1=f(I['g_kv'][0]), g_final=f(I['g_final']), **consts)
    maps = []
    for c in cores:
        m = dict(shared)
        m['x_p'] = f(I['x_prompt'][c])
        m['x_s'] = f(I['x_sample'][16 * c:16 * c + 16].reshape(64, D))
        m['c_all'] = f(np.concatenate([I['c_prompt'][c:c + 1], I['c_sample'][16 * c:16 * c + 16]], axis=0))
        if cache_override is not None:
            m['cache_kv'], m['cache_pe'], m['ptab'] = cache_override[c]
        else:
            m['cache_kv'] = I['cache_kv'][0]
            m['cache_pe'] = I['cache_pe'][0]
            m['ptab'] = f(I['page_table'][16 * c:16 * c + 16].astype(np.int32).T)
        maps.append(m)
    return maps


_NC = {}


def kernel(**inputs):
    n_pool = int(np.asarray(inputs['cache_kv']).shape[1])
    if n_pool not in _NC:
        _NC[n_pool] = build(n_pool=n_pool)
    nc = _NC[n_pool]
    maps = make_in_maps(inputs, list(range(8)))
    res = run_bass_kernel_spmd(nc, maps, core_ids=list(range(8)))
    R = res.results
    y_prompt = np.stack([R[c]['y_p'] for c in range(8)]).astype(np.float32)
    y_sample = np.concatenate([R[c]['y_s'].reshape(16, 4, D) for c in range(8)]).astype(np.float32)
    kv_prompt = np.stack([R[c]['kv_p'] for c in range(8)])[None].astype(np.float32)
    pe_prompt = np.stack([R[c]['pe_p'] for c in range(8)])[None].astype(np.float32)
    kv_sample = np.concatenate([R[c]['kv_s'].reshape(16, 4, 256) for c in range(8)])[None].astype(np.float32)
    pe_sample = np.concatenate([R[c]['pe_s'].reshape(16, 4, 32) for c in range(8)])[None].astype(np.float32)
    gv_sample = np.concatenate([R[c]['gv_s'].reshape(16, 4, 512) for c in range(8)])[None].astype(np.float32)
    return (y_prompt, y_sample, kv_prompt, pe_prompt, kv_sample, pe_sample, gv_sample)
```
